# Optimizing a Trainium2 kernel written in Bass

```python
import math
import jax
import jax.numpy as jnp
from jax import lax
import numpy as np

D_MODEL = 1024
BATCH = 1
SEQ = 16384
DEPTH = 4
DEC_BATCH = 8
DEC_SEQ = 4096
PAST_LEN = 128

A_HEADS = 4
A_HEAD_DIM = 128
A_WIDTH = A_HEADS * A_HEAD_DIM
A_CHUNK = 64
B_Q_HEADS = 8
B_KV_HEADS = 2
B_HEAD_DIM = 64
B_WIDTH = B_Q_HEADS * B_HEAD_DIM
B_KV_WIDTH = B_KV_HEADS * B_HEAD_DIM
WINDOW = 128
B_BLOCK = 128
ROPE_THETA = 10000.0
C_HEADS = 4
C_HEAD_DIM = 128
C_WIDTH = C_HEADS * C_HEAD_DIM
C_CHUNK = 64
CONV_K = 5
MEM_TOKENS = 256
MEM_HEADS = 4
MEM_HEAD_DIM = D_MODEL // MEM_HEADS
FFN_HIDDEN = ((8 * D_MODEL + 3 * 256 - 1) // (3 * 256)) * 256
DN_ALPHA = (2 * DEPTH) ** 0.25
DN_BETA = (8 * DEPTH) ** -0.25
SPLIT_SIZES = (A_WIDTH, A_WIDTH, A_WIDTH, A_WIDTH, A_WIDTH,
               B_WIDTH, B_KV_WIDTH, B_KV_WIDTH,
               3 * C_WIDTH, C_WIDTH, 2 * C_HEADS, 2 * C_HEADS,
               3 * D_MODEL)
IN_COLS = sum(SPLIT_SIZES)

kernel_name = 'hybrid_bidir_hgrn2_swa_gdn_encoder'


def split_points():
    pts = []
    acc = 0
    for s in SPLIT_SIZES[:-1]:
        acc += s
        pts.append(acc)
    return pts


def layer_norm(x, g, b, eps=1e-5):
    xf = x.astype(jnp.float32)
    mu = jnp.mean(xf, -1, keepdims=True)
    var = jnp.mean(jnp.square(xf - mu), -1, keepdims=True)
    return ((xf - mu) * lax.rsqrt(var + eps) * g.astype(jnp.float32) + b.astype(jnp.float32)).astype(x.dtype)


def rms_norm(x, g, eps=1e-6):
    xf = x.astype(jnp.float32)
    return xf * lax.rsqrt(jnp.mean(xf * xf, -1, keepdims=True) + eps) * g.astype(jnp.float32)


def l2_normalize(x, eps=1e-6):
    return x * lax.rsqrt(jnp.sum(x * x, -1, keepdims=True) + eps)


def rope_tables(seq_len):
    inv = ROPE_THETA ** (-jnp.arange(0, B_HEAD_DIM, 2, dtype=jnp.float32) / B_HEAD_DIM)
    ang = jnp.arange(seq_len, dtype=jnp.float32)[:, None] * inv[None, :]
    return jnp.cos(ang), jnp.sin(ang)


def apply_rope(x, cos, sin):
    x1, x2 = jnp.split(x.astype(jnp.float32), 2, axis=-1)
    c = cos[None, :, None, :]
    s = sin[None, :, None, :]
    return jnp.concatenate([x1 * c - x2 * s, x2 * c + x1 * s], axis=-1)


def hgrn_lower_bounds(logits):
    cum = jnp.cumsum(jax.nn.softmax(logits.astype(jnp.float32), axis=1), axis=1)
    return cum - cum[:, :1]


def gla_chunk(q, k, v, log_f):
    B, H, S, dk = q.shape
    dv = v.shape[-1]
    C = A_CHUNK
    n = S // C
    qc = q.reshape(B, H, n, C, dk)
    kc = k.reshape(B, H, n, C, dk)
    vc = v.reshape(B, H, n, C, dv)
    gc = jnp.cumsum(log_f.reshape(B, H, n, C, dk), axis=3)
    q_dec = qc * jnp.exp(gc)
    k_dec = kc * jnp.exp(gc[:, :, :, -1:] - gc)
    g_last = jnp.exp(gc[:, :, :, -1])
    incl = jnp.tril(jnp.ones((C, C), dtype=bool))[:, :, None]

    def step(state, inp):
        q_i, k_i, v_i, g_i, qd_i, kd_i, gl_i = inp
        rel = jnp.where(incl, g_i[:, :, :, None, :] - g_i[:, :, None, :, :], -jnp.inf)
        scores = jnp.sum(q_i[:, :, :, None, :] * k_i[:, :, None, :, :] * jnp.exp(rel), axis=-1)
        out = jnp.einsum('bhij,bhjv->bhiv', scores, v_i) + jnp.einsum('bhid,bhdv->bhiv', qd_i, state)
        state = state * gl_i[..., None] + jnp.einsum('bhjd,bhjv->bhdv', kd_i, v_i)
        return state, out

    xs = tuple(jnp.moveaxis(t, 2, 0) for t in (qc, kc, vc, gc, q_dec, k_dec, g_last))
    _, out = lax.scan(step, jnp.zeros((B, H, dk, dv), jnp.float32), xs)
    return jnp.moveaxis(out, 0, 2).reshape(B, H, S, dv)


def hgrn2_branch(q_in, f_fwd_in, f_bwd_in, i_in, g_in, lb_fwd, lb_bwd, norm_g):
    B, S, _ = q_in.shape

    def heads(t):
        return jnp.swapaxes(t.astype(jnp.float32).reshape(B, S, A_HEADS, A_HEAD_DIM), 1, 2)

    q = jax.nn.silu(heads(q_in)) * A_HEAD_DIM ** -0.5
    v = heads(i_in)

    def gate_terms(f_in, lb):
        z = heads(f_in)
        lb = lb.reshape(A_HEADS, 1, A_HEAD_DIM)
        log_f = jnp.logaddexp(jnp.log(lb), jnp.log1p(-lb) + jax.nn.log_sigmoid(z))
        key = (1.0 - lb) * jax.nn.sigmoid(-z)
        return key, log_f

    k_f, lf_f = gate_terms(f_fwd_in, lb_fwd)
    k_b, lf_b = gate_terms(f_bwd_in, lb_bwd)
    flip = lambda t: jnp.flip(t, axis=2)
    o = gla_chunk(q, k_f, v, lf_f) + flip(gla_chunk(flip(q), flip(k_b), flip(v), flip(lf_b)))
    o = jnp.swapaxes(o, 1, 2)
    gate = g_in.astype(jnp.float32).reshape(B, S, A_HEADS, A_HEAD_DIM)
    o = rms_norm(o, norm_g) * jax.nn.silu(gate)
    return o.reshape(B, S, A_WIDTH).astype(q_in.dtype)


def window_attention(q_in, k_in, v_in, sink, cos, sin):
    B, S, _ = q_in.shape
    nb = S // B_BLOCK
    G = B_Q_HEADS // B_KV_HEADS
    q = apply_rope(q_in.reshape(B, S, B_Q_HEADS, B_HEAD_DIM), cos, sin)
    k = apply_rope(k_in.reshape(B, S, B_KV_HEADS, B_HEAD_DIM), cos, sin)
    v = v_in.astype(jnp.float32).reshape(B, S, B_KV_HEADS, B_HEAD_DIM)
    qb = q.reshape(B, nb, B_BLOCK, B_KV_HEADS, G, B_HEAD_DIM)
    pad = ((0, 0), (B_BLOCK, B_BLOCK), (0, 0), (0, 0))
    kp = jnp.pad(k, pad).reshape(B, nb + 2, B_BLOCK, B_KV_HEADS, B_HEAD_DIM)
    vp = jnp.pad(v, pad).reshape(B, nb + 2, B_BLOCK, B_KV_HEADS, B_HEAD_DIM)
    kw = jnp.concatenate([kp[:, :-2], kp[:, 1:-1], kp[:, 2:]], axis=2)
    vw = jnp.concatenate([vp[:, :-2], vp[:, 1:-1], vp[:, 2:]], axis=2)
    s = jnp.einsum('bnqhgd,bnkhd->bnhgqk', qb, kw) * B_HEAD_DIM ** -0.5
    blk = jnp.arange(nb)[:, None, None]
    qpos = blk * B_BLOCK + jnp.arange(B_BLOCK)[None, :, None]
    kpos = (blk - 1) * B_BLOCK + jnp.arange(3 * B_BLOCK)[None, None, :]
    mask = (jnp.abs(qpos - kpos) <= WINDOW) & (kpos >= 0) & (kpos < S)
    s = jnp.where(mask[None, :, None, None], s, -jnp.inf)
    sink_l = sink.astype(jnp.float32).reshape(B_KV_HEADS, G)[None, None, :, :, None, None]
    m = jnp.maximum(jnp.max(s, -1, keepdims=True), sink_l)
    p = jnp.exp(s - m)
    denom = jnp.sum(p, -1, keepdims=True) + jnp.exp(sink_l - m)
    o = jnp.einsum('bnhgqk,bnkhd->bnqhgd', p / denom, vw)
    return o.reshape(B, S, B_WIDTH).astype(q_in.dtype)


def depthwise_conv(x, w):
    return lax.conv_general_dilated(x, w[:, None, :], window_strides=(1,),
                                    padding=[(CONV_K // 2, CONV_K // 2)],
                                    dimension_numbers=('NWC', 'WIO', 'NWC'),
                                    feature_group_count=x.shape[-1])


def gated_delta_chunk(q, k, v, g, beta):
    B, H, S, dk = q.shape
    dv = v.shape[-1]
    C = C_CHUNK
    n = S // C
    qc = q.reshape(B, H, n, C, dk)
    kc = k.reshape(B, H, n, C, dk)
    vc = v.reshape(B, H, n, C, dv)
    bc = beta.reshape(B, H, n, C)
    gc = jnp.cumsum(g.reshape(B, H, n, C), axis=-1)
    incl = jnp.tril(jnp.ones((C, C), dtype=bool))
    strict = jnp.tril(jnp.ones((C, C), dtype=bool), -1)
    decay = jnp.exp(jnp.where(incl, gc[..., :, None] - gc[..., None, :], -jnp.inf))
    kk = jnp.einsum('bhnid,bhnjd->bhnij', kc, kc)
    a_strict = jnp.where(strict, kk * decay * bc[..., :, None], 0.0)
    rhs = jnp.concatenate([vc * bc[..., None], kc * (bc * jnp.exp(gc))[..., None]], axis=-1)
    sol = lax.linalg.triangular_solve(a_strict + jnp.eye(C, dtype=a_strict.dtype), rhs,
                                      left_side=True, lower=True)
    u = sol[..., :dv]
    w = sol[..., dv:]
    qk = jnp.einsum('bhnid,bhnjd->bhnij', qc, kc) * decay
    q_dec = qc * jnp.exp(gc)[..., None]
    k_dec = kc * jnp.exp(gc[..., -1:] - gc)[..., None]
    g_last = jnp.exp(gc[..., -1])

    def step(state, inp):
        u_i, w_i, qk_i, qd_i, kd_i, gl_i = inp
        v_new = u_i - jnp.einsum('bhcd,bhdv->bhcv', w_i, state)
        out = jnp.einsum('bhcd,bhdv->bhcv', qd_i, state) + jnp.einsum('bhij,bhjv->bhiv', qk_i, v_new)
        state = state * gl_i[..., None, None] + jnp.einsum('bhcd,bhcv->bhdv', kd_i, v_new)
        return state, out

    xs = tuple(jnp.moveaxis(t, 2, 0) for t in (u, w, qk, q_dec, k_dec, g_last))
    _, out = lax.scan(step, jnp.zeros((B, H, dk, dv), jnp.float32), xs)
    return jnp.moveaxis(out, 0, 2).reshape(B, H, S, dv)


def gated_deltanet_branch(qkv_in, gate_in, beta_in, a_in, conv_w, a_log, dt_bias, norm_g):
    B, S, _ = qkv_in.shape
    qkv = jax.nn.silu(depthwise_conv(qkv_in, conv_w).astype(jnp.float32))
    q, k, v = jnp.split(qkv, 3, axis=-1)

    def heads(t):
        return jnp.swapaxes(t.reshape(B, S, C_HEADS, C_HEAD_DIM), 1, 2)

    q = l2_normalize(heads(q)) * C_HEAD_DIM ** -0.5
    k = l2_normalize(heads(k))
    v = heads(v)
    beta = jax.nn.sigmoid(beta_in.astype(jnp.float32).reshape(B, S, 2, C_HEADS))
    g = -jnp.exp(a_log.astype(jnp.float32)) * jax.nn.softplus(
        a_in.astype(jnp.float32).reshape(B, S, 2, C_HEADS) + dt_bias.astype(jnp.float32))
    beta = jnp.transpose(beta, (2, 0, 3, 1))
    g = jnp.transpose(g, (2, 0, 3, 1))
    flip = lambda t: jnp.flip(t, axis=2)
    o_f = gated_delta_chunk(q, k, v, g[0], beta[0])
    o_b = flip(gated_delta_chunk(flip(q), flip(k), flip(v), flip(g[1]), flip(beta[1])))
    o = jnp.swapaxes(o_f + o_b, 1, 2)
    gate = gate_in.astype(jnp.float32).reshape(B, S, C_HEADS, C_HEAD_DIM)
    o = rms_norm(o, norm_g) * jax.nn.silu(gate)
    return o.reshape(B, S, C_WIDTH).astype(qkv_in.dtype)


def memory_cross_attention(x, mem, w_q, w_kv, w_o):
    B, S, _ = x.shape
    M = mem.shape[1]
    q = (x @ w_q).astype(jnp.float32).reshape(B, S, MEM_HEADS, MEM_HEAD_DIM)
    k, v = jnp.split((mem @ w_kv).astype(jnp.float32), 2, axis=-1)
    k = k.reshape(B, M, MEM_HEADS, MEM_HEAD_DIM)
    v = v.reshape(B, M, MEM_HEADS, MEM_HEAD_DIM)
    p = jax.nn.softmax(jnp.einsum('bshd,bmhd->bhsm', q, k) * MEM_HEAD_DIM ** -0.5, axis=-1)
    o = jnp.einsum('bhsm,bmhd->bshd', p, v).reshape(B, S, D_MODEL).astype(x.dtype)
    return o @ w_o


def swiglu(x, w_in, w_out):
    gate, up = jnp.split(x @ w_in, 2, axis=-1)
    return (jax.nn.silu(gate) * up) @ w_out


def trunk(x, mem, w_in, hgrn_lb_logits, hgrn_norm_g, attn_sink, gdn_conv_w, gdn_a_log, gdn_dt_bias,
          gdn_norm_g, w_branch_a, w_branch_b, w_branch_c, w_mix_out, w_mem_q, w_mem_kv, w_mem_o,
          w_ffn_in, w_ffn_out, ln_g, ln_b):
    cos, sin = rope_tables(x.shape[1])
    lb = hgrn_lower_bounds(hgrn_lb_logits)
    pts = split_points()
    for l in range(DEPTH):
        h = x @ w_in[l]
        (a_q, a_f_fwd, a_f_bwd, a_i, a_g, b_q, b_k, b_v,
         c_qkv, c_gate, c_beta, c_a, merge) = jnp.split(h, pts, axis=-1)
        o_a = hgrn2_branch(a_q, a_f_fwd, a_f_bwd, a_i, a_g, lb[0, l], lb[1, l], hgrn_norm_g[l])
        o_b = window_attention(b_q, b_k, b_v, attn_sink[l], cos, sin)
        o_c = gated_deltanet_branch(c_qkv, c_gate, c_beta, c_a, gdn_conv_w[l], gdn_a_log[l],
                                    gdn_dt_bias[l], gdn_norm_g[l])
        g_a, g_b, g_c = jnp.split(jax.nn.sigmoid(merge), 3, axis=-1)
        mix = g_a * (o_a @ w_branch_a[l]) + g_b * (o_b @ w_branch_b[l]) + g_c * (o_c @ w_branch_c[l])
        x = layer_norm(DN_ALPHA * x + mix @ w_mix_out[l], ln_g[l, 0], ln_b[l, 0])
        x = layer_norm(DN_ALPHA * x + memory_cross_attention(x, mem, w_mem_q[l], w_mem_kv[l], w_mem_o[l]),
                       ln_g[l, 1], ln_b[l, 1])
        x = layer_norm(DN_ALPHA * x + swiglu(x, w_ffn_in[l], w_ffn_out[l]), ln_g[l, 2], ln_b[l, 2])
    return x


def setup_inputs(seed: int = 0) -> dict:
    key = jax.random.key(seed)
    ks = jax.random.split(key, 24)
    nrm = lambda k, shape, scale: jax.random.normal(k, shape, jnp.float32) * scale
    dt = jnp.exp(jax.random.uniform(ks[8], (DEPTH, 2, C_HEADS), jnp.float32,
                                    minval=math.log(1e-3), maxval=math.log(1e-1)))
    return {
        'x_prompt': nrm(ks[0], (BATCH, SEQ, D_MODEL), 1.0),
        'x_sample': nrm(ks[1], (DEC_BATCH, DEC_SEQ, D_MODEL), 1.0),
        'mem_prompt': nrm(ks[2], (BATCH, MEM_TOKENS, D_MODEL), 1.0),
        'mem_sample': nrm(ks[3], (DEC_BATCH, MEM_TOKENS, D_MODEL), 1.0),
        'w_in': nrm(ks[4], (DEPTH, D_MODEL, IN_COLS), D_MODEL ** -0.5),
        'hgrn_lb_logits': nrm(ks[5], (2, DEPTH, A_WIDTH), 0.5),
        'hgrn_norm_g': 1.0 + nrm(ks[6], (DEPTH, A_HEAD_DIM), 0.02),
        'attn_sink': nrm(ks[7], (DEPTH, B_Q_HEADS), 0.5),
        'gdn_conv_w': nrm(ks[9], (DEPTH, CONV_K, 3 * C_WIDTH), CONV_K ** -0.5),
        'gdn_a_log': jnp.log(jax.random.uniform(ks[10], (DEPTH, 2, C_HEADS), jnp.float32, minval=1.0, maxval=16.0)),
        'gdn_dt_bias': dt + jnp.log(-jnp.expm1(-dt)),
        'gdn_norm_g': 1.0 + nrm(ks[11], (DEPTH, C_HEAD_DIM), 0.02),
        'w_branch_a': nrm(ks[12], (DEPTH, A_WIDTH, D_MODEL), A_WIDTH ** -0.5),
        'w_branch_b': nrm(ks[13], (DEPTH, B_WIDTH, D_MODEL), B_WIDTH ** -0.5),
        'w_branch_c': nrm(ks[14], (DEPTH, C_WIDTH, D_MODEL), C_WIDTH ** -0.5),
        'w_mix_out': nrm(ks[15], (DEPTH, D_MODEL, D_MODEL), DN_BETA * D_MODEL ** -0.5),
        'w_mem_q': nrm(ks[16], (DEPTH, D_MODEL, D_MODEL), D_MODEL ** -0.5),
        'w_mem_kv': nrm(ks[17], (DEPTH, D_MODEL, 2 * D_MODEL), D_MODEL ** -0.5),
        'w_mem_o': nrm(ks[18], (DEPTH, D_MODEL, D_MODEL), DN_BETA * D_MODEL ** -0.5),
        'w_ffn_in': nrm(ks[19], (DEPTH, D_MODEL, 2 * FFN_HIDDEN), D_MODEL ** -0.5),
        'w_ffn_out': nrm(ks[20], (DEPTH, FFN_HIDDEN, D_MODEL), DN_BETA * FFN_HIDDEN ** -0.5),
        'ln_g': 1.0 + nrm(ks[21], (DEPTH, 3, D_MODEL), 0.02),
        'ln_b': nrm(ks[22], (DEPTH, 3, D_MODEL), 0.02),
    }


def reference(x_prompt, x_sample, mem_prompt, mem_sample, w_in, hgrn_lb_logits, hgrn_norm_g, attn_sink,
              gdn_conv_w, gdn_a_log, gdn_dt_bias, gdn_norm_g, w_branch_a, w_branch_b, w_branch_c, w_mix_out,
              w_mem_q, w_mem_kv, w_mem_o, w_ffn_in, w_ffn_out, ln_g, ln_b):
    y_prompt = trunk(x_prompt, mem_prompt, w_in, hgrn_lb_logits, hgrn_norm_g, attn_sink, gdn_conv_w,
                     gdn_a_log, gdn_dt_bias, gdn_norm_g, w_branch_a, w_branch_b, w_branch_c, w_mix_out,
                     w_mem_q, w_mem_kv, w_mem_o, w_ffn_in, w_ffn_out, ln_g, ln_b)
    y_sample = trunk(x_sample, mem_sample, w_in, hgrn_lb_logits, hgrn_norm_g, attn_sink, gdn_conv_w,
                     gdn_a_log, gdn_dt_bias, gdn_norm_g, w_branch_a, w_branch_b, w_branch_c, w_mix_out,
                     w_mem_q, w_mem_kv, w_mem_o, w_ffn_in, w_ffn_out, ln_g, ln_b)
    return (y_prompt, y_sample)
```

```python
from contextlib import ExitStack

import numpy as np
import concourse.bass as bass
import concourse.mybir as mybir
from concourse.bass_utils import run_bass_kernel_spmd

F32 = mybir.dt.float32
BF16 = mybir.dt.bfloat16
AF = mybir.ActivationFunctionType
ALU = mybir.AluOpType
AX = mybir.AxisListType

ENGS = ("pe", "act", "dve", "pool", "sp")

D = 1024
KC = 8
IN_COLS = 8464
FFN_H = 2816
FKC = 22
MEM = 256
DEPTH = 4
ALPHA = float((2 * DEPTH) ** 0.25)
OFF = dict(AQ=0, AFF=512, AFB=1024, AI=1536, AG=2048, BQ=2560, BK=3072, BV=3200, CQ=3328,
           CG=4864, CB=5376, CA=5384, MG=5392)
C_ID, C_UI, C_LI, C_SL, C_SU, C_ONE, C_SEL0, C_SEL1 = range(8)
NCONST = 8


class DSem:
    __slots__ = ("h", "count", "sw")

    def __init__(self, h):
        self.h = h
        self.count = 0
        self.sw = False


class Buf:
    __slots__ = ("t", "name", "writers", "readers", "dsem", "excl")

    def __init__(self, t, name):
        self.t = t
        self.name = name
        self.excl = False
        self.writers = {}
        self.readers = {}
        self.dsem = None

    def __getitem__(self, k):
        return self.t[k]


class Op:
    __slots__ = ("eng", "fn", "deps", "needs_sig", "val", "idx", "dma", "dsem", "dval", "kind",
                 "dma_waits", "swdma")

    def __init__(self, eng, fn):
        self.eng = eng
        self.fn = fn
        self.deps = []
        self.needs_sig = False
        self.val = 0
        self.idx = 0
        self.dma = False
        self.dsem = None
        self.dval = 0
        self.kind = "op"
        self.dma_waits = None
        self.swdma = False


class Prog:
    def __init__(self, nc, esem, bar, dsems):
        self.nc = nc
        self.esem = esem
        self.bar = bar
        ds_all = [DSem(h) for h in dsems]
        nsw = 2
        for d_ in ds_all[:nsw]:
            d_.sw = True
        self.free_dsems = {"sw": ds_all[:nsw], "hw": ds_all[nsw:]}
        self.all_dsems = ds_all
        self.streams = {e: [] for e in ENGS}
        self.n = {e: 0 for e in ENGS}
        self.sig = {e: 0 for e in ENGS}
        self.seen = {e: {} for e in ENGS}
        self.nbar = 0
        self.bufs = []
        self.nops = 0
        self.pend_w = []
        self.pend_r = []
        self.pend_i = []
        self.swdummy = None

    def buf(self, t, name):
        b = Buf(t, name)
        self.bufs.append(b)
        return b

    def release(self, bufs):
        ids = set(id(b) for b in bufs)
        for b in bufs:
            if b.dsem is not None:
                for kind, d in b.dsem.items():
                    self.free_dsems[kind].append(d)
                b.dsem = None
        self.bufs = [b for b in self.bufs if id(b) not in ids]

    def _flush_pending(self):
        if not self.pend_w and not self.pend_r:
            return
        w, r = self.pend_w, self.pend_r + self.pend_i
        self.pend_w, self.pend_r, self.pend_i = [], [], []
        self.op("pool", lambda e: e.drain(), reads=r, writes=w)

    def _add_dep(self, op, prod):
        if prod is op or prod.swdma:
            return
        if prod.dma:
            key = ("d", id(prod.dsem))
            v = prod.dval
        else:
            if prod.eng == op.eng and not op.dma and op.eng in ("pe", "sp"):
                return
            key = ("e", prod.eng)
            v = prod.idx
        s = self.seen[op.eng]
        if s.get(key, -1) >= v:
            return
        s[key] = v
        prod.needs_sig = True
        op.deps.append(prod)

    def op(self, eng, fn, reads=(), writes=(), pwrites=(), dma_buf=None, swdma=False):
        if not swdma and (self.pend_w or self.pend_r):
            pend = set(id(b) for b in self.pend_w) | set(id(b) for b in self.pend_r) | set(id(b) for b in self.pend_i)
            if any(id(b) in pend for b in tuple(reads) + tuple(writes) + tuple(pwrites)):
                self._flush_pending()
        o = Op(eng, fn)
        o.swdma = swdma
        o.idx = self.n[eng]
        self.n[eng] += 1
        self.nops += 1
        if dma_buf is not None:
            o.dma = True
            kind = "sw" if eng == "pool" else "hw"
            if dma_buf.dsem is None:
                dma_buf.dsem = {}
            if kind not in dma_buf.dsem:
                dma_buf.dsem[kind] = self.free_dsems[kind].pop()
            o.dsem = dma_buf.dsem[kind]
            o.dsem.count += 16
            o.dval = o.dsem.count
        okey = ("d", id(o.dsem)) if o.dma else ("e", eng)
        for b in reads:
            for w in b.writers.values():
                self._add_dep(o, w)
            if b.excl:
                for r in b.readers.values():
                    if r.eng != eng:
                        self._add_dep(o, r)
        for b in writes:
            for w in b.writers.values():
                self._add_dep(o, w)
            for r in b.readers.values():
                self._add_dep(o, r)
        for b in pwrites:
            for r in b.readers.values():
                self._add_dep(o, r)
        for b in reads:
            b.readers[okey] = o
        for b in writes:
            b.writers = {okey: o}
            b.readers = {}
        for b in pwrites:
            b.writers[okey] = o
        self.streams[eng].append(o)
        return o

    def end_section(self):
        self._flush_pending()
        nc = self.nc
        esem = self.esem
        for e in ENGS:
            c = 0
            for o in self.streams[e]:
                if o.kind == "op" and not o.dma and o.needs_sig:
                    c += 1
                    o.val = c
        streams = self.streams
        self.streams = {e: [] for e in ENGS}
        handles = {"sp": nc.sync, "pe": nc.tensor, "act": nc.scalar, "dve": nc.vector, "pool": nc.gpsimd}
        for e in ENGS:
            eng = handles[e]
            for o in streams[e]:
                for p in o.deps:
                    if p.dma:
                        eng.wait_ge(p.dsem.h, p.dval)
                    else:
                        eng.wait_ge(esem[p.eng], p.val)
                ins = o.fn(eng)
                if o.swdma:
                    ins.then_inc(self.swdummy, 16)
                elif o.dma:
                    ins.then_inc(o.dsem.h, 16)
                elif o.needs_sig:
                    ins.then_inc(esem[e], 1)
        used = [ds for ds in self.all_dsems if ds.count > 0]
        for ds in used:
            nc.sync.wait_ge(ds.h, ds.count)
        nc.all_engine_barrier()
        for e in ENGS:
            nc.sync.sem_clear(esem[e])
        for ds in used:
            if not ds.sw:
                nc.sync.sem_clear(ds.h)
                ds.count = 0
        nc.all_engine_barrier()
        self.seen = {e: {} for e in ENGS}
        for bf in self.bufs:
            bf.writers = {}
            bf.readers = {}


class Ring:
    def __init__(self, items):
        self.items = items
        self.i = 0

    def next(self):
        b = self.items[self.i % len(self.items)]
        self.i += 1
        return b


class Seq:
    def __init__(self, name, S, x_in, mem, y_out):
        self.name = name
        self.S = S
        self.x_in = x_in
        self.mem = mem
        self.y_out = y_out


def build(SP, SS, depth=DEPTH, debug=False, stop_after=None, seqs_enabled=("p", "s"), layer_list=None):
    nc = bass.Bass("TRN2", target_bir_lowering=False)
    SMAX = max(SP, SS)
    dbg_outputs = []

    def din(name, shape, dt=F32):
        return nc.dram_tensor(name, list(shape), dt, kind="ExternalInput").ap()

    def dscr(name, shape, dt):
        if debug:
            dbg_outputs.append(name)
            return nc.dram_tensor(name, list(shape), dt, kind="ExternalOutput").ap()
        return nc.dram_tensor(name, list(shape), dt).ap()

    x_prompt = din("x_prompt", [SP, D])
    x_sample = din("x_sample", [SS, D])
    mem_prompt = din("mem_prompt", [MEM, D])
    mem_sample = din("mem_sample", [MEM, D])
    w_in = din("w_in", [DEPTH, D, IN_COLS])
    lb_logits = din("hgrn_lb_logits", [1, 2 * DEPTH * 512])
    hgrn_norm_g = din("hgrn_norm_g", [DEPTH, 128])
    attn_sink = din("attn_sink", [DEPTH, 8])
    gdn_conv_w = din("gdn_conv_w", [DEPTH, 5 * 1536])
    gdn_a_log = din("gdn_a_log", [DEPTH, 8])
    gdn_dt_bias = din("gdn_dt_bias", [DEPTH, 8])
    gdn_norm_g = din("gdn_norm_g", [DEPTH, 128])
    w_branch_a = din("w_branch_a", [DEPTH, 512, D])
    w_branch_b = din("w_branch_b", [DEPTH, 512, D])
    w_branch_c = din("w_branch_c", [DEPTH, 512, D])
    w_mix_out = din("w_mix_out", [DEPTH, D, D])
    w_mem_q = din("w_mem_q", [DEPTH, D, D])
    w_mem_kv = din("w_mem_kv", [DEPTH, D, 2 * D])
    w_mem_o = din("w_mem_o", [DEPTH, D, D])
    w_ffn_in = din("w_ffn_in", [DEPTH, D, 2 * FFN_H])
    w_ffn_out = din("w_ffn_out", [DEPTH, FFN_H, D])
    ln_g = din("ln_g", [DEPTH, 3 * D])
    ln_b = din("ln_b", [DEPTH, 3 * D])
    consts_d = din("consts", [128, NCONST * 128])
    rope_d = din("rope", [SMAX, 128])
    amask_d = din("amask", [128, 4 * 384])
    cshift_d = din("cshift", [128, 9 * 128])
    zrow_d = din("zrow", [2, 1536])
    iota_d = din("iota", [128, 1])

    y_prompt = nc.dram_tensor("y_prompt", [SP, D], F32, kind="ExternalOutput").ap()
    y_sample = nc.dram_tensor("y_sample", [SS, D], F32, kind="ExternalOutput").ap()

    wb = {}
    wsrc = dict(w_in=w_in, wba=w_branch_a, wbb=w_branch_b, wbc=w_branch_c, wmix=w_mix_out,
                wmq=w_mem_q, wkv=w_mem_kv, wmo=w_mem_o, wfi=w_ffn_in, wfo=w_ffn_out)
    for k, v in wsrc.items():
        wb[k] = dscr(k + "_bf", list(v.shape), BF16) if (debug and k == "w_in") else nc.dram_tensor(k + "_bf", list(v.shape), BF16).ap()
    S_ = SMAX
    hq = dscr("s_hq", [S_, 512], BF16)
    hlf = [dscr(f"s_hlf{i}", [S_, 512], F32) for i in range(2)]
    hk = [dscr(f"s_hk{i}", [S_, 512], BF16) for i in range(2)]
    hv = dscr("s_hv", [S_, 512], BF16)
    ga_d = dscr("s_ga", [S_, 512], BF16)
    bq_d = dscr("s_bq", [S_, 512], BF16)
    bkv_d = dscr("s_bkv", [S_, 256], BF16)
    cq_d = [dscr(f"s_cq{i}", [S_, 512], F32) for i in range(3)]
    gcg_d = dscr("s_gcg", [S_, 512], BF16)
    bg_d = dscr("s_bg", [S_, 16], F32)
    mg_d = [dscr(f"s_mg{i}", [S_, 512], BF16) for i in range(6)]
    gq_d = dscr("s_gq", [S_, 512], F32)
    gk_d = dscr("s_gk", [S_, 512], F32)
    gv_d = dscr("s_gv", [S_, 512], BF16)
    ob_d = dscr("s_ob", [S_, 512], BF16)
    oa_d = [dscr(f"s_oa{i}", [S_, 512], F32) for i in range(2)]
    oc_d = [dscr(f"s_oc{i}", [S_, 512], F32) for i in range(2)]
    xs_d = [dscr("s_xa", [S_, D], F32), dscr("s_xb", [S_, D], F32)]

    uid = [0]

    with ExitStack() as gs:
        esem = {e: gs.enter_context(nc.semaphore("s_" + e)) for e in ENGS}
        bar = gs.enter_context(nc.semaphore("bar"))
        dsems = [gs.enter_context(nc.semaphore(f"d{i}")) for i in range(56)]
        P = Prog(nc, esem, bar, dsems)
        swpool = [gs.enter_context(nc.semaphore(f"swd{i}")) for i in range(36)]
        P.swdummy = swpool[0]

        class Tiles:
            def __init__(self):
                self.es = ExitStack()
                self.bufs = []

            def sb(self, name, shape, dt):
                uid[0] += 1
                t = self.es.enter_context(nc.sbuf_tensor(f"{name}_{uid[0]}", list(shape), dt))
                b = P.buf(t, name)
                self.bufs.append(b)
                return b

            def ring(self, name, shape, dt, n):
                return Ring([self.sb(f"{name}{i}", shape, dt) for i in range(n)])

            def close(self):
                P.release(self.bufs)
                self.es.close()

        GT = Tiles()
        PSALL = gs.enter_context(nc.psum_tensor("psall", [128, 8 * 512], F32))
        PS = [P.buf(PSALL[:, i * 512:(i + 1) * 512], f"ps{i}") for i in range(8)]
        for b_ in PS:
            b_.excl = True
        PSALLB = PSALL.bitcast(BF16)
        CSH = GT.sb("CSH", [128, 9, 128], BF16)
        CSHF = GT.sb("CSHF", [128, 9, 128], F32)
        CF = GT.sb("CF", [128, NCONST, 128], F32)
        IDB = GT.sb("IDB", [128, 128], BF16)
        EPS = GT.sb("EPS", [128, 2], F32)
        IOTA = GT.sb("IOTA", [128, 1], F32)
        ddummy = P.buf(None, "wcast")

        def _idma(sb_buf, sb_ap, dyn, is_load):
            nsub = dyn.n // 128 if dyn.multi else 1
            for s_ in range(nsub):
                if dyn.multi:
                    sap = sb_ap[:, s_, :] if nsub > 1 or len(sb_ap.shape) == 3 else sb_ap
                else:
                    sap = sb_ap
                icol = dyn.row.idx.col(dyn.row.off + s_ * 128)
                npart = sap.shape[0]
                ioff = bass.IndirectOffsetOnAxis(ap=icol[0:npart, :], axis=0)
                dr = dyn.dram()
                if is_load:
                    P.op("pool", lambda e, sap=sap, dr=dr, ioff=ioff: e.indirect_dma_start(out=sap, out_offset=None, in_=dr, in_offset=ioff),
                         reads=[dyn.row.idx.Ibuf], pwrites=[sb_buf], swdma=True)
                else:
                    P.op("pool", lambda e, sap=sap, dr=dr, ioff=ioff: e.indirect_dma_start(out=dr, out_offset=ioff, in_=sap, in_offset=None),
                         reads=[dyn.row.idx.Ibuf, sb_buf], swdma=True)
            if not any(b is dyn.row.idx.Ibuf for b in P.pend_i):
                P.pend_i.append(dyn.row.idx.Ibuf)
            if is_load:
                if not any(b is sb_buf for b in P.pend_w):
                    P.pend_w.append(sb_buf)
            else:
                if not any(b is sb_buf for b in P.pend_r):
                    P.pend_r.append(sb_buf)

        def ld(dst, dst_ap, src_ap, partial=False, eng="sp"):
            if isinstance(src_ap, DynAP):
                _idma(dst, dst_ap, src_ap, True)
                return
            if partial:
                P.op(eng, lambda e: e.dma_start(out=dst_ap, in_=src_ap), pwrites=[dst], dma_buf=dst)
            else:
                P.op(eng, lambda e: e.dma_start(out=dst_ap, in_=src_ap), writes=[dst], dma_buf=dst)

        def st(dst_ap, src, src_ap, eng="sp"):
            if isinstance(dst_ap, DynAP):
                _idma(src, src_ap, dst_ap, False)
                return
            P.op(eng, lambda e: e.dma_start(out=dst_ap, in_=src_ap), reads=[src], dma_buf=src)

        def end_phase(tiles):
            P.end_section()
            if tiles is not None:
                tiles.close()

        class DynRow:
            def __init__(self, idx, off=0):
                self.idx = idx
                self.off = off

            def __add__(self, k):
                return DynRow(self.idx, self.off + k)

            def __sub__(self, k):
                return DynRow(self.idx, self.off - k)

        class DynAP:
            def __init__(self, ap, row, n, c_lo=None, c_hi=None, multi=False):
                self.ap, self.row, self.n, self.c_lo, self.c_hi, self.multi = ap, row, n, c_lo, c_hi, multi

            def __getitem__(self, key):
                cs = key[1]
                return DynAP(self.ap, self.row, self.n, cs.start, cs.stop, self.multi)

            def rearrange(self, *_a, **_k):
                return DynAP(self.ap, self.row, self.n, self.c_lo, self.c_hi, True)

            def dram(self):
                return self.ap if self.c_lo is None else self.ap[:, self.c_lo:self.c_hi]

        class RowIdx:
            def __init__(self, T, name, offs, base, step):
                self.offs = list(offs)
                self.step = step
                n = len(self.offs)
                self.F = T.sb(name + "F", [128, n], F32)
                self.I = T.sb(name + "I", [128, n], mybir.dt.int32)
                self.Ibuf = self.I
                for j, o_ in enumerate(self.offs):
                    P.op("dve", lambda e, j=j, o_=o_: e.tensor_scalar(out=self.F[:, j:j + 1], in0=IOTA[:, 0:1], scalar1=float(base + o_), scalar2=None, op0=ALU.add),
                         reads=[IOTA], writes=[self.F] if j == 0 else (), pwrites=() if j == 0 else [self.F])
                P.op("dve", lambda e: e.tensor_copy(out=self.I[:], in_=self.F[:]), reads=[self.F], writes=[self.I])

            def col(self, off):
                j = self.offs.index(off)
                return self.I[:, j:j + 1]

            def advance(self):
                P.op("dve", lambda e: e.tensor_scalar(out=self.F[:], in0=self.F[:], scalar1=float(self.step), scalar2=None, op0=ALU.add),
                     reads=[self.F], writes=[self.F])
                P.op("dve", lambda e: e.tensor_copy(out=self.I[:], in_=self.F[:]), reads=[self.F, self.I], writes=[self.I])

        def rows(ap, r0, n):
            if isinstance(r0, DynRow):
                return DynAP(ap, r0, n)
            return ap[r0:r0 + n]

        loop_regs = nc.alloc_registers("mk_loop_i", engines=mybir.ALL_ENGINES)
        loop_cnt = [0]

        def loop(n0, n1, body, idxs=()):
            if n1 - n0 <= 0:
                return
            P.end_section()
            loop_cnt[0] += 1
            P.swdummy = swpool[loop_cnt[0] % len(swpool)]
            ls, le = f"mk_loop{loop_cnt[0]}_loop", f"mk_loop{loop_cnt[0]}_end"
            engines = mybir.ALL_ENGINES
            nc.regs_mov(loop_regs, n0)
            nc.br(ls, engines=engines)
            with nc.body(ls, valid_engines=engines):
                body()
                for ix in idxs:
                    ix.advance()
                P.end_section()
                nc.regs_alu(loop_regs, loop_regs, 1, op=ALU.add)
                nc.br_lt(loop_regs, n1, on_true=ls, on_false=le, engines=engines)
            nc.switch_bb(le)

        ld(CF, CF[:].rearrange("p a b -> p (a b)"), consts_d[:, :])
        P.op("dve", lambda e: e.tensor_copy(out=IDB[:], in_=CF[:, C_ID, :]), reads=[CF], writes=[IDB])
        ld(IOTA, IOTA[:], iota_d[:, :])
        ld(CSHF, CSHF[:].rearrange("p a b -> p (a b)"), cshift_d[:, :])
        P.op("dve", lambda e: e.tensor_copy(out=CSH[:], in_=CSHF[:]), reads=[CSHF], writes=[CSH])
        P.op("pool", lambda e: e.memset(EPS[:, 0:1], 1e-6), writes=[EPS])
        P.op("pool", lambda e: e.memset(EPS[:, 1:2], 1e-5), pwrites=[EPS])
        for k, v in wsrc.items():
            L, R, Cc = v.shape
            for l in range(depth):
                for r0 in range(0, R, 128):
                    P.op("pool", lambda e, k=k, v=v, l=l, r0=r0: e.dma_start(
                        out=wb[k][l, r0:r0 + 128, :], in_=v[l, r0:r0 + 128, :]), dma_buf=ddummy)
        end_phase(None)

        ident = CF[:, C_ID, :]

        def phase_p1(seq, l, xcur):
            S = seq.S
            BLK = min(512, S)
            NSUB = BLK // 128
            NB = S // BLK
            T = Tiles()
            xin = T.ring("xin", [128, D], F32, 2 * NSUB)
            xT = T.ring("xT", [128, KC, BLK], BF16, 2)
            wpan = T.ring("wpan", [128, KC, 512], BF16, 3)
            stg16 = T.ring("stg16", [128, NSUB, 512], BF16, 4)
            stg32 = T.ring("stg32", [128, NSUB, 512], F32, 3)
            tmp = T.ring("tmp", [128, 512], F32, 8)
            ropet = T.ring("ropet", [128, NSUB, 128], F32, 2)
            LG = T.sb("LG", [128, 2, DEPTH, 512], F32)
            MX = T.sb("MX", [128, 2, 512], F32)
            SSUM = T.sb("SSUM", [128, 2, 512], F32)
            LB = T.sb("LB", [128, 2, 512], F32)
            OML = T.sb("OML", [128, 2, 512], F32)
            DTB = T.sb("DTB", [128, 8], F32)
            NEGA = T.sb("NEGA", [128, 8], F32)
            pst = Ring(PS[0:2])
            psm = Ring(PS[2:8])

            ld(LG, LG[:].rearrange("p a b c -> p (a b c)"), lb_logits.partition_broadcast(128))
            P.op("dve", lambda e: e.tensor_tensor(out=MX[:], in0=LG[:, :, 0, :], in1=LG[:, :, 1, :], op=ALU.max),
                 reads=[LG], writes=[MX])
            for m in (2, 3):
                P.op("dve", lambda e, m=m: e.tensor_tensor(out=MX[:], in0=MX[:], in1=LG[:, :, m, :], op=ALU.max),
                     reads=[LG, MX], writes=[MX])
            for m in range(DEPTH):
                P.op("dve", lambda e, m=m: e.tensor_tensor(out=LG[:, :, m, :], in0=LG[:, :, m, :], in1=MX[:],
                                                         op=ALU.subtract), reads=[LG, MX], writes=[LG])
            P.op("act", lambda e: e.activation(out=LG[:], in_=LG[:], func=AF.Exp), reads=[LG], writes=[LG])
            P.op("dve", lambda e: e.tensor_tensor(out=SSUM[:], in0=LG[:, :, 0, :], in1=LG[:, :, 1, :], op=ALU.add),
                 reads=[LG], writes=[SSUM])
            for m in (2, 3):
                P.op("dve", lambda e, m=m: e.tensor_tensor(out=SSUM[:], in0=SSUM[:], in1=LG[:, :, m, :], op=ALU.add),
                     reads=[LG, SSUM], writes=[SSUM])
            P.op("dve", lambda e: e.reciprocal(out=SSUM[:], in_=SSUM[:]), reads=[SSUM], writes=[SSUM])
            if l == 0:
                P.op("pool", lambda e: e.memset(LB[:], 0.0), writes=[LB])
            else:
                P.op("dve", lambda e: e.tensor_copy(out=LB[:], in_=LG[:, :, 1, :]), reads=[LG], writes=[LB])
                for m in range(2, l + 1):
                    P.op("dve", lambda e, m=m: e.tensor_tensor(out=LB[:], in0=LB[:], in1=LG[:, :, m, :], op=ALU.add),
                         reads=[LG, LB], writes=[LB])
                P.op("dve", lambda e: e.tensor_tensor(out=LB[:], in0=LB[:], in1=SSUM[:], op=ALU.mult),
                     reads=[LB, SSUM], writes=[LB])
            P.op("dve", lambda e: e.tensor_scalar(out=OML[:], in0=LB[:], scalar1=-1.0, scalar2=1.0,
                                                  op0=ALU.mult, op1=ALU.add), reads=[LB], writes=[OML])
            ld(DTB, DTB[:], gdn_dt_bias[l:l + 1, :].partition_broadcast(128))
            ld(NEGA, NEGA[:], gdn_a_log[l:l + 1, :].partition_broadcast(128))
            P.op("act", lambda e: e.activation(out=NEGA[:], in_=NEGA[:], func=AF.Exp), reads=[NEGA], writes=[NEGA])
            P.op("dve", lambda e: e.tensor_scalar(out=NEGA[:], in0=NEGA[:], scalar1=-1.0, scalar2=None, op0=ALU.mult),
                 reads=[NEGA], writes=[NEGA])

            panels = [("aq", OFF["AQ"], 512), ("aff", OFF["AFF"], 512), ("afb", OFF["AFB"], 512),
                      ("ai", OFF["AI"], 512), ("ag", OFF["AG"], 512), ("bq", OFF["BQ"], 512),
                      ("bkv", OFF["BK"], 256), ("cq0", OFF["CQ"], 512), ("cq1", OFF["CQ"] + 512, 512),
                      ("cq2", OFF["CQ"] + 1024, 512), ("cg", OFF["CG"], 512), ("bg", OFF["CB"], 16)]
            for i in range(6):
                panels.append((f"mg{i}", OFF["MG"] + 512 * i, 512))

            ix1 = RowIdx(T, "ix1", [s_ * 128 for s_ in range(NSUB)], 0, BLK)

            def p1_body(b=None):
                t0 = DynRow(ix1) if b is None else b * BLK
                xTt = xT.next()
                rt = ropet.next()
                ld(rt, rt[:], rows(rope_d, t0, BLK).rearrange("(s p) c -> p s c", p=128))
                for s in range(NSUB):
                    xt = xin.next()
                    ld(xt, xt[:], rows(xcur, t0 + s * 128, 128))
                    for half in range(2):
                        ps = pst.next()
                        for j in range(4):
                            kc = half * 4 + j
                            P.op("pe", lambda e, ps=ps, j=j, kc=kc, xt=xt: e.transpose(
                                out=ps[:, j * 128:(j + 1) * 128], in_=xt[:, kc * 128:(kc + 1) * 128], identity=ident),
                                reads=[xt, CF], writes=[ps] if j == 0 else (), pwrites=() if j == 0 else [ps])
                        eng = "act" if half == 0 else "dve"
                        if eng == "act":
                            P.op("act", lambda e, ps=ps, half=half, s=s, xTt=xTt: e.activation(
                                out=xTt[:, half * 4:half * 4 + 4, s * 128:(s + 1) * 128],
                                in_=ps[:].rearrange("p (a b) -> p a b", a=4), func=AF.Copy),
                                reads=[ps], pwrites=[xTt])
                        else:
                            P.op("dve", lambda e, ps=ps, half=half, s=s, xTt=xTt: e.tensor_copy(
                                out=xTt[:, half * 4:half * 4 + 4, s * 128:(s + 1) * 128],
                                in_=ps[:].rearrange("p (a b) -> p a b", a=4)),
                                reads=[ps], pwrites=[xTt])
                for (kind, c0, ncol) in panels:
                    wp = wpan.next()
                    ld(wp, wp[:, :, 0:ncol], wb["w_in"][l, :, c0:c0 + ncol].rearrange("(kc p) c -> p kc c", p=128))
                    if kind in ("aq", "ai", "ag", "bq", "cg") or kind.startswith("mg"):
                        so = stg16.next()
                    elif kind in ("aff", "afb"):
                        so = stg32.next()
                        so2 = stg16.next()
                    elif kind == "bkv":
                        so = stg16.next()
                    elif kind.startswith("cq") or kind == "bg":
                        so = stg32.next()
                    for s in range(NSUB):
                        ps = psm.next()
                        for kc in range(KC):
                            P.op("pe", lambda e, ps=ps, kc=kc, s=s, wp=wp, ncol=ncol, xTt=xTt: e.matmul(
                                ps[:, 0:ncol], lhsT=xTt[:, kc, s * 128:(s + 1) * 128], rhs=wp[:, kc, 0:ncol],
                                start=(kc == 0), stop=(kc == KC - 1)),
                                reads=[xTt, wp], writes=[ps] if kc == 0 else (), pwrites=() if kc == 0 else [ps])
                        if kind in ("aq", "ag", "cg"):
                            P.op("act", lambda e, ps=ps, so=so, s=s: e.activation(out=so[:, s, :], in_=ps[:], func=AF.Silu),
                                 reads=[ps], pwrites=[so])
                        elif kind.startswith("mg"):
                            P.op("act", lambda e, ps=ps, so=so, s=s: e.activation(out=so[:, s, :], in_=ps[:], func=AF.Sigmoid),
                                 reads=[ps], pwrites=[so])
                        elif kind == "ai":
                            P.op("dve", lambda e, ps=ps, so=so, s=s: e.tensor_copy(out=so[:, s, :], in_=ps[:]),
                                 reads=[ps], pwrites=[so])
                        elif kind.startswith("cq"):
                            P.op("act", lambda e, ps=ps, so=so, s=s: e.activation(out=so[:, s, :], in_=ps[:], func=AF.Copy),
                                 reads=[ps], pwrites=[so])
                        elif kind in ("aff", "afb"):
                            di = 0 if kind == "aff" else 1
                            sg = tmp.next()
                            ff = tmp.next()
                            P.op("act", lambda e, ps=ps, sg=sg: e.activation(out=sg[:], in_=ps[:], func=AF.Sigmoid),
                                 reads=[ps], writes=[sg])
                            P.op("dve", lambda e, sg=sg, di=di: e.tensor_tensor(out=sg[:], in0=sg[:], in1=OML[:, di, :], op=ALU.mult),
                                 reads=[sg, OML], writes=[sg])
                            P.op("pool", lambda e, sg=sg, ff=ff, di=di: e.tensor_tensor(out=ff[:], in0=sg[:], in1=LB[:, di, :], op=ALU.add),
                                 reads=[sg, LB], writes=[ff])
                            P.op("act", lambda e, ff=ff, so=so, s=s: e.activation(out=so[:, s, :], in_=ff[:], func=AF.Ln),
                                 reads=[ff], pwrites=[so])
                            P.op("pool", lambda e, sg=sg, so2=so2, s=s, di=di: e.tensor_tensor(
                                out=so2[:, s, :], in0=OML[:, di, :], in1=sg[:], op=ALU.subtract),
                                reads=[sg, OML], pwrites=[so2])
                        elif kind in ("bq", "bkv"):
                            nh = 8 if kind == "bq" else 2
                            w = nh * 64
                            ta = tmp.next()
                            tb = tmp.next()
                            cc = rt[:, s, 0:64].unsqueeze(1).to_broadcast([128, nh, 64])
                            nsin = rt[:, s, 64:96].unsqueeze(1).to_broadcast([128, nh, 32])
                            psin = rt[:, s, 96:128].unsqueeze(1).to_broadcast([128, nh, 32])
                            P.op("dve", lambda e, ps=ps, ta=ta, cc=cc, nh=nh, w=w: e.tensor_tensor(
                                out=ta[:, 0:w].rearrange("p (h d) -> p h d", h=nh),
                                in0=ps[:, 0:w].rearrange("p (h d) -> p h d", h=nh), in1=cc, op=ALU.mult),
                                reads=[ps, rt], writes=[ta])
                            P.op("dve", lambda e, ps=ps, tb=tb, nsin=nsin, nh=nh, w=w: e.tensor_tensor(
                                out=tb[:, 0:w].rearrange("p (h d) -> p h d", h=nh)[:, :, 0:32],
                                in0=ps[:, 0:w].rearrange("p (h d) -> p h d", h=nh)[:, :, 32:64], in1=nsin, op=ALU.mult),
                                reads=[ps, rt], writes=[tb])
                            P.op("dve", lambda e, ps=ps, tb=tb, psin=psin, nh=nh, w=w: e.tensor_tensor(
                                out=tb[:, 0:w].rearrange("p (h d) -> p h d", h=nh)[:, :, 32:64],
                                in0=ps[:, 0:w].rearrange("p (h d) -> p h d", h=nh)[:, :, 0:32], in1=psin, op=ALU.mult),
                                reads=[ps, rt], pwrites=[tb])
                            P.op("pool", lambda e, ta=ta, tb=tb, so=so, s=s, w=w: e.tensor_tensor(
                                out=so[:, s, 0:w], in0=ta[:, 0:w], in1=tb[:, 0:w], op=ALU.add),
                                reads=[ta, tb], pwrites=[so])
                            if kind == "bkv":
                                P.op("act", lambda e, ps=ps, so=so, s=s: e.activation(
                                    out=so[:, s, 128:256], in_=ps[:, 128:256], func=AF.Copy), reads=[ps], pwrites=[so])
                        elif kind == "bg":
                            ta = tmp.next()
                            P.op("act", lambda e, ps=ps, so=so, s=s: e.activation(out=so[:, s, 0:8], in_=ps[:, 0:8], func=AF.Sigmoid),
                                 reads=[ps], pwrites=[so])
                            P.op("dve", lambda e, ps=ps, ta=ta: e.tensor_tensor(out=ta[:, 0:8], in0=ps[:, 8:16], in1=DTB[:], op=ALU.add),
                                 reads=[ps, DTB], writes=[ta])
                            P.op("act", lambda e, ta=ta: e.activation(out=ta[:, 0:8], in_=ta[:, 0:8], func=AF.Exp),
                                 reads=[ta], writes=[ta])
                            P.op("act", lambda e, ta=ta: e.activation(out=ta[:, 0:8], in_=ta[:, 0:8], func=AF.Ln, bias=CF[:, C_ONE, 0:1]),
                                 reads=[ta, CF], writes=[ta])
                            P.op("dve", lambda e, ta=ta, so=so, s=s: e.tensor_tensor(out=so[:, s, 8:16], in0=ta[:, 0:8], in1=NEGA[:], op=ALU.mult),
                                 reads=[ta, NEGA], pwrites=[so])

                    def dst(ap2d, c_lo, c_hi):
                        return rows(ap2d, t0, BLK).rearrange("(s p) c -> p s c", p=128)

                    if kind == "aq":
                        st(dst(hq, 0, 512), so, so[:])
                    elif kind in ("aff", "afb"):
                        di = 0 if kind == "aff" else 1
                        st(dst(hlf[di], 0, 512), so, so[:])
                        st(dst(hk[di], 0, 512), so2, so2[:])
                    elif kind == "ai":
                        st(dst(hv, 0, 512), so, so[:])
                    elif kind == "ag":
                        st(dst(ga_d, 0, 512), so, so[:])
                    elif kind == "bq":
                        st(dst(bq_d, 0, 512), so, so[:])
                    elif kind == "bkv":
                        st(dst(bkv_d, 0, 256), so, so[:, :, 0:256])
                    elif kind.startswith("cq"):
                        i = int(kind[2])
                        st(dst(cq_d[i], 0, 512), so, so[:])
                    elif kind == "cg":
                        st(dst(gcg_d, 0, 512), so, so[:])
                    elif kind == "bg":
                        st(dst(bg_d, 0, 16), so, so[:, :, 0:16])
                    elif kind.startswith("mg"):
                        i = int(kind[2])
                        st(dst(mg_d[i], 0, 512), so, so[:])

            if NB > 1:
                loop(0, NB, p1_body, [ix1])
            else:
                p1_body(0)
            end_phase(T)


        def psb(i, lo, hi):
            return PSALLB[:, i * 1024 + lo:i * 1024 + hi]

        def phase_p2a(seq, l):
            S = seq.S
            NT = S // 128
            T = Tiles()
            CW = T.sb("CW", [128, 5, 1536], F32)
            AM = T.sb("AM", [128, 4, 384], F32)
            SINK = T.sb("SINK", [128, 8], F32)
            NSINK = T.sb("NSINK", [128, 8], F32)
            cqm = T.ring("cqm", [128, 1536], F32, 2)
            halo = T.ring("halo", [2, 1536], F32, 2)
            xw = T.ring("xw", [128, 5, 1536], BF16, 2)
            hw = T.ring("hw", [2, 2, 1536], BF16, 2)
            act32 = T.ring("a32", [128, 512], F32, 4)
            sq32 = T.ring("sq32", [128, 512], F32, 2)
            st32 = T.ring("st32", [128, 512], F32, 4)
            st16 = T.ring("st16", [128, 512], BF16, 4)
            small = T.ring("small", [128, 16], F32, 12)
            qt_in = T.ring("qtin", [128, 512], BF16, 2)
            kvt = T.ring("kvt", [128, 256], BF16, 6)
            KD = T.ring("KD", [128, 2, 128], BF16, 3)
            KT = T.ring("KT", [128, 2, 384], BF16, 2)
            QT = T.ring("QT", [128, 4, 128], BF16, 2)
            SC = T.ring("SC", [128, 4, 384], F32, 2)
            PB = T.ring("PB", [128, 4, 384], BF16, 2)
            PT = T.ring("PT", [128, 12, 128], BF16, 2)
            ld(CW, CW[:].rearrange("p a b -> p (a b)"), gdn_conv_w[l:l + 1, :].partition_broadcast(128))
            ld(AM, AM[:].rearrange("p a b -> p (a b)"), amask_d[:, :])
            ld(SINK, SINK[:], attn_sink[l:l + 1, :].partition_broadcast(128))
            P.op("dve", lambda e: e.tensor_scalar(out=NSINK[:], in0=SINK[:], scalar1=-1.0, scalar2=None, op0=ALU.mult),
                 reads=[SINK], writes=[NSINK])
            convps = Ring(PS[7:8])
            trps = Ring([4, 5])
            pvps = Ring(PS[6:7])
            ix2 = RowIdx(T, "ix2", [0, -128, 128, -2], 128, 128)

            def p2a_body(t, is_first, is_last):
                t0 = DynRow(ix2) if t is None else t * 128
                cm_ = cqm.next()
                hlp = halo.next()
                hln = halo.next()
                for j3 in range(3):
                    cs3 = slice(j3 * 512, (j3 + 1) * 512)
                    ld(cm_, cm_[:, cs3], rows(cq_d[j3], t0, 128), partial=(j3 > 0))
                    ld(hlp, hlp[0:2, cs3], rows(cq_d[j3], t0 - 2, 2) if not is_first else zrow_d[:, cs3], partial=(j3 > 0))
                    ld(hln, hln[0:2, cs3], rows(cq_d[j3], t0 + 128, 2) if not is_last else zrow_d[:, cs3], partial=(j3 > 0))
                xw_ = xw.next()
                hwp = hw.next()
                hwn = hw.next()
                for k in range(5):
                    eng = "pool" if k % 2 == 0 else "dve"
                    P.op(eng, lambda e, k=k, xw_=xw_, cm_=cm_: e.tensor_tensor(out=xw_[:, k, :], in0=cm_[:], in1=CW[:, k, :], op=ALU.mult),
                         reads=[cm_, CW], pwrites=[xw_])
                for k in (0, 1, 3, 4):
                    hsrc, hdst = (hlp, hwp) if k < 2 else (hln, hwn)
                    P.op("pool", lambda e, k=k, hsrc=hsrc, hdst=hdst: e.tensor_tensor(out=hdst[:, k % 3, :], in0=hsrc[:], in1=CW[0:2, k, :], op=ALU.mult),
                         reads=[hsrc, CW], pwrites=[hdst])
                outs3 = []
                for j in range(3):
                    ps = convps.next()
                    first = True
                    for k in range(5):
                        P.op("pe", lambda e, ps=ps, k=k, j=j, xw_=xw_, first=first: e.matmul(
                            ps[:], lhsT=CSH[:, k, :], rhs=xw_[:, k, j * 512:(j + 1) * 512], start=first, stop=False),
                            reads=[CSH, xw_], writes=[ps] if first else (), pwrites=() if first else [ps])
                        first = False
                    for ii, k in enumerate((0, 1, 3, 4)):
                        hsrc = hwp if k < 2 else hwn
                        P.op("pe", lambda e, ps=ps, k=k, ii=ii, j=j, hsrc=hsrc: e.matmul(
                            ps[:], lhsT=CSH[0:2, 5 + ii, :], rhs=hsrc[:, k % 3, j * 512:(j + 1) * 512], start=False, stop=(ii == 3)),
                            reads=[CSH, hsrc], pwrites=[ps])
                    if j < 2:
                        a_ = act32.next()
                        P.op("act", lambda e, ps=ps, a_=a_: e.activation(out=a_[:], in_=ps[:], func=AF.Silu), reads=[ps], writes=[a_])
                        outs3.append(a_)
                    else:
                        so = st16.next()
                        P.op("act", lambda e, ps=ps, so=so: e.activation(out=so[:], in_=ps[:], func=AF.Silu), reads=[ps], writes=[so])
                        st(rows(gv_d, t0, 128), so, so[:])
                for j, a_ in enumerate(outs3):
                    sq = sq32.next()
                    sm = small.next()
                    so = st32.next()
                    P.op("pool", lambda e, sq=sq, a_=a_: e.tensor_tensor(out=sq[:], in0=a_[:], in1=a_[:], op=ALU.mult), reads=[a_], writes=[sq])
                    P.op("dve", lambda e, sq=sq, sm=sm: e.tensor_reduce(out=sm[:, 0:4], in_=sq[:].rearrange("p (h d) -> p h d", h=4), axis=AX.X, op=ALU.add),
                         reads=[sq], writes=[sm])
                    P.op("act", lambda e, sm=sm: e.activation(out=sm[:, 4:8], in_=sm[:, 0:4], func=AF.Sqrt, bias=EPS[:, 0:1]),
                         reads=[sm, EPS], writes=[sm])
                    P.op("dve", lambda e, sm=sm: e.reciprocal(out=sm[:, 8:12], in_=sm[:, 4:8]), reads=[sm], writes=[sm])
                    if j == 0:
                        P.op("dve", lambda e, sm=sm: e.tensor_scalar(out=sm[:, 8:12], in0=sm[:, 8:12], scalar1=float(128 ** -0.5), scalar2=None, op0=ALU.mult),
                             reads=[sm], writes=[sm])
                    P.op("dve", lambda e, a_=a_, sm=sm, so=so: e.tensor_tensor(
                        out=so[:].rearrange("p (h d) -> p h d", h=4), in0=a_[:].rearrange("p (h d) -> p h d", h=4),
                        in1=sm[:, 8:12].unsqueeze(2).to_broadcast([128, 4, 128]), op=ALU.mult), reads=[a_, sm], writes=[so])
                    st(rows(gq_d if j == 0 else gk_d, t0, 128), so, so[:])

                variant = 0
                if is_first and is_last:
                    variant = 3
                elif is_first:
                    variant = 1
                elif is_last:
                    variant = 2
                qi = qt_in.next()
                ld(qi, qi[:], rows(bq_d, t0, 128))
                kvs = []
                for j in range(3):
                    kv_ = kvt.next()
                    if (j == 0 and is_first) or (j == 2 and is_last):
                        P.op("pool", lambda e, kv_=kv_: e.memset(kv_[:], 0.0), writes=[kv_])
                    else:
                        ld(kv_, kv_[:], rows(bkv_d, t0 + (j - 1) * 128, 128))
                    kvs.append(kv_)
                KT_ = KT.next()
                bi = trps.next()
                for j in range(3):
                    kd = KD.next()
                    P.op("pool", lambda e, kd=kd, kv_=kvs[j]: e.tensor_copy(
                        out=kd[:].rearrange("p g (r d) -> p g r d", r=2),
                        in_=kv_[:, 0:128].rearrange("p (g d) -> p g d", g=2).unsqueeze(2).to_broadcast([128, 2, 2, 64])),
                        reads=[kvs[j]], writes=[kd])
                    for g in range(2):
                        idx = j * 2 + g
                        P.op("pe", lambda e, kd=kd, g=g, idx=idx, bi=bi: e.transpose(
                            out=psb(bi, idx * 128, (idx + 1) * 128), in_=kd[:, g, :], identity=IDB[:]),
                            reads=[kd, IDB], writes=[PS[bi]] if idx == 0 else (), pwrites=() if idx == 0 else [PS[bi]])
                P.op("act", lambda e, bi=bi, KT_=KT_: e.activation(
                    out=KT_[:].rearrange("p g (j k) -> p j g k", j=3),
                    in_=psb(bi, 0, 768).rearrange("p (j g k) -> p j g k", j=3, g=2), func=AF.Copy),
                    reads=[PS[bi]], writes=[KT_])
                QT_ = QT.next()
                bi = trps.next()
                for pr in range(4):
                    P.op("pe", lambda e, pr=pr, bi=bi, qi=qi: e.transpose(
                        out=psb(bi, pr * 128, (pr + 1) * 128), in_=qi[:, pr * 128:(pr + 1) * 128], identity=IDB[:]),
                        reads=[qi, IDB], writes=[PS[bi]] if pr == 0 else (), pwrites=() if pr == 0 else [PS[bi]])
                P.op("dve", lambda e, bi=bi, QT_=QT_: e.tensor_copy(out=QT_[:].rearrange("p a b -> p (a b)"), in_=psb(bi, 0, 512)),
                     reads=[PS[bi]], writes=[QT_])
                ob_t = st16.next()
                for g in range(2):
                    for hh4 in range(4):
                        h = g * 4 + hh4
                        pr, half = h // 2, h % 2
                        P.op("pe", lambda e, hh4=hh4, pr=pr, half=half, g=g, QT_=QT_, KT_=KT_: e.matmul(
                            PSALL[:, hh4 * 512:hh4 * 512 + 384], lhsT=QT_[half * 64:(half + 1) * 64, pr, :],
                            rhs=KT_[half * 64:(half + 1) * 64, g, :], start=True, stop=True),
                            reads=[QT_, KT_], writes=[PS[hh4]])
                    sc = SC.next()
                    P.op("dve", lambda e, sc=sc, variant=variant: e.tensor_tensor(
                        out=sc[:], in0=PSALL[:, 0:2048].rearrange("p (h k) -> p h k", h=4)[:, :, 0:384],
                        in1=AM[:, variant, :].unsqueeze(1).to_broadcast([128, 4, 384]), op=ALU.add),
                        reads=PS[0:4] + [AM], writes=[sc])
                    sm = small.next()
                    P.op("dve", lambda e, sc=sc, sm=sm: e.tensor_reduce(out=sm[:, 0:4], in_=sc[:], axis=AX.X, op=ALU.max),
                         reads=[sc], writes=[sm])
                    P.op("dve", lambda e, sm=sm: e.tensor_scalar(out=sm[:, 0:4], in0=sm[:, 0:4], scalar1=-0.125, scalar2=None, op0=ALU.mult),
                         reads=[sm], writes=[sm])
                    P.op("dve", lambda e, sm=sm, g=g: e.tensor_tensor(out=sm[:, 0:4], in0=sm[:, 0:4], in1=NSINK[:, g * 4:(g + 1) * 4], op=ALU.min),
                         reads=[sm, NSINK], writes=[sm])
                    pb = PB.next()
                    for hh4 in range(4):
                        P.op("act", lambda e, hh4=hh4, sc=sc, pb=pb, sm=sm: e.activation(
                            out=pb[:, hh4, :], in_=sc[:, hh4, :], func=AF.Exp, scale=0.125, bias=sm[:, hh4:hh4 + 1],
                            accum_out=sm[:, 4 + hh4:5 + hh4]), reads=[sc, sm], pwrites=[pb, sm])
                    P.op("dve", lambda e, sm=sm, g=g: e.tensor_tensor(out=sm[:, 8:12], in0=sm[:, 0:4], in1=SINK[:, g * 4:(g + 1) * 4], op=ALU.add),
                         reads=[sm, SINK], writes=[sm])
                    P.op("act", lambda e, sm=sm: e.activation(out=sm[:, 8:12], in_=sm[:, 8:12], func=AF.Exp), reads=[sm], writes=[sm])
                    P.op("dve", lambda e, sm=sm: e.tensor_tensor(out=sm[:, 8:12], in0=sm[:, 8:12], in1=sm[:, 4:8], op=ALU.add),
                         reads=[sm], writes=[sm])
                    P.op("dve", lambda e, sm=sm: e.reciprocal(out=sm[:, 12:16], in_=sm[:, 8:12]), reads=[sm], writes=[sm])
                    pt = PT.next()
                    for half2 in range(2):
                        bi = trps.next()
                        for q6 in range(6):
                            blk = half2 * 6 + q6
                            hh4, j = blk // 3, blk % 3
                            P.op("pe", lambda e, bi=bi, q6=q6, hh4=hh4, j=j, pb=pb: e.transpose(
                                out=psb(bi, q6 * 128, (q6 + 1) * 128), in_=pb[:, hh4, j * 128:(j + 1) * 128], identity=IDB[:]),
                                reads=[pb, IDB], writes=[PS[bi]] if q6 == 0 else (), pwrites=() if q6 == 0 else [PS[bi]])
                        eng = "act" if half2 == 0 else "dve"
                        if eng == "act":
                            P.op("act", lambda e, bi=bi, pt=pt, half2=half2: e.activation(
                                out=pt[:, half2 * 6:half2 * 6 + 6, :].rearrange("p a b -> p (a b)"), in_=psb(bi, 0, 768), func=AF.Copy),
                                reads=[PS[bi]], pwrites=[pt])
                        else:
                            P.op("dve", lambda e, bi=bi, pt=pt, half2=half2: e.tensor_copy(
                                out=pt[:, half2 * 6:half2 * 6 + 6, :].rearrange("p a b -> p (a b)"), in_=psb(bi, 0, 768)),
                                reads=[PS[bi]], pwrites=[pt])
                    pv = pvps.next()
                    for hh4 in range(4):
                        for j in range(3):
                            P.op("pe", lambda e, hh4=hh4, j=j, pv=pv, pt=pt, kv_=kvs[j], g=g: e.matmul(
                                pv[:, hh4 * 64:(hh4 + 1) * 64], lhsT=pt[:, hh4 * 3 + j, :],
                                rhs=kv_[:, 128 + g * 64:128 + (g + 1) * 64], start=(j == 0), stop=(j == 2)),
                                reads=[pt, kvs[j]], writes=[pv] if (hh4 == 0 and j == 0) else (),
                                pwrites=() if (hh4 == 0 and j == 0) else [pv])
                    P.op("dve", lambda e, pv=pv, sm=sm, g=g, ob_t=ob_t: e.tensor_tensor(
                        out=ob_t[:, g * 256:(g + 1) * 256].rearrange("p (h d) -> p h d", h=4),
                        in0=pv[:, 0:256].rearrange("p (h d) -> p h d", h=4),
                        in1=sm[:, 12:16].unsqueeze(2).to_broadcast([128, 4, 64]), op=ALU.mult),
                        reads=[pv, sm], pwrites=[ob_t])
                st(rows(ob_d, t0, 128), ob_t, ob_t[:])

            p2a_body(0, True, NT == 1)
            if NT - 2 >= 2:
                loop(1, NT - 1, lambda: p2a_body(None, False, False), [ix2])
            else:
                for t_ in range(1, NT - 1):
                    p2a_body(t_, False, False)
            if NT > 1:
                p2a_body(NT - 1, False, True)
            end_phase(T)


        def phase_p2(seq, l):
            S = seq.S
            NT = S // 128
            T = Tiles()
            M4 = {}
            for ci in (C_UI, C_LI, C_SL, C_SU, C_ID):
                m = T.sb(f"M4_{ci}", [128, 4, 128], F32)
                P.op("pool", lambda e, m=m, ci=ci: e.tensor_copy(out=m[:], in_=CF[:, ci, :].unsqueeze(1).to_broadcast([128, 4, 128])),
                     reads=[CF], writes=[m])
                M4[ci] = m
            I4 = M4[C_ID]
            NM4 = {}
            for ci in (C_UI, C_LI, C_SL, C_SU):
                m = T.sb(f"NM4_{ci}", [128, 4, 128], F32)
                P.op("dve", lambda e, m=m, ci=ci: e.tensor_scalar(out=m[:], in0=M4[ci][:], scalar1=30000.0, scalar2=-30000.0,
                                                              op0=ALU.mult, op1=ALU.add), reads=[M4[ci]], writes=[m])
                NM4[ci] = m
            ONESF = CF[:, C_ONE, :]
            SA32 = [T.sb(f"SA32_{d}", [128, 4, 128], F32) for d in range(2)]
            SAbf = [T.sb(f"SAbf_{d}", [128, 4, 128], BF16) for d in range(2)]
            SC32 = [T.sb(f"SC32_{d}", [128, 4, 128], F32) for d in range(2)]
            SCbf = [T.sb(f"SCbf_{d}", [128, 4, 128], BF16) for d in range(2)]
            for b_ in SA32 + SAbf + SC32 + SCbf:
                P.op("pool", lambda e, b_=b_: e.memset(b_[:], 0.0), writes=[b_])
            R = lambda name, shape, dt, n=2: T.ring(name, shape, dt, n)
            hq_t = R("hq_t", [128, 512], BF16); hlf_t = R("hlf_t", [128, 512], F32)
            hk_t = R("hk_t", [128, 512], BF16); hv_t = R("hv_t", [128, 512], BF16)
            eG = R("eG", [128, 512], F32); enG = R("enG", [128, 512], F32); eR = R("eR", [128, 512], F32)
            qt_t = R("qt_t", [128, 512], BF16); kt_t = R("kt_t", [128, 512], BF16); kd_t = R("kd_t", [128, 512], BF16)
            QKT = R("QKT", [128, 8, 128], BF16); AT = R("AT", [128, 4, 128], BF16)
            egl = R("egl", [128, 8], F32); oA = R("oA", [128, 512], F32)
            gq_t = R("gq_t", [128, 4, 128], F32); gk_t = R("gk_t", [128, 4, 128], F32); gv_t = R("gv_t", [128, 4, 128], BF16)
            bg_t = R("bg_t", [128, 16], F32); ecol = R("ecol", [128, 16], F32); CL = R("CL", [128, 16], F32)
            Y0v = R("Y0v", [128, 4, 128], BF16); Y0k = R("Y0k", [128, 4, 128], BF16)
            kdec = R("kdec", [128, 4, 128], BF16); qdec = R("qdec", [128, 4, 128], BF16)
            knb = R("knb", [128, 4, 128], BF16); qnb = R("qnb", [128, 4, 128], BF16)
            KQT = R("KQT", [128, 4, 2, 128], BF16); QDT = R("QDT", [128, 4, 128], BF16)
            GREP = R("GREP", [128, 4, 128], F32); NGC = R("NGC", [128, 4, 128], F32)
            EDT = R("EDT", [128, 4, 128], F32); ED = R("ED", [128, 4, 128], F32)
            Dm = R("Dm", [128, 4, 128], F32); DTm = R("DTm", [128, 4, 128], F32)
            NN = R("NN", [128, 4, 128], F32, 4); MM = R("MM", [128, 4, 128], F32, 4)
            P32 = R("P32", [128, 4, 128], F32); QKm = R("QKm", [128, 4, 128], BF16)
            TT = R("TT", [128, 4, 128], BF16); U32 = R("U32", [128, 512], F32); WT = R("WT", [128, 4, 128], BF16)
            VN = R("VN", [128, 512], BF16); oC = R("oC", [128, 512], F32)
            B = PS

            def mm(out_ap, lhsT, rhs, rd, bank, first, start=True, stop=True):
                P.op("pe", lambda e: e.matmul(out_ap, lhsT=lhsT, rhs=rhs, start=start, stop=stop),
                     reads=rd, writes=[bank] if first else (), pwrites=() if first else [bank])

            def hgrn_step(t, di):
                t0 = t if isinstance(t, DynRow) else t * 128
                ci_cum, ci_rem = (C_UI, C_SL) if di == 0 else (C_LI, C_SU)
                q = hq_t.next(); lf = hlf_t.next(); k = hk_t.next(); v = hv_t.next()
                ld(q, q[:], rows(hq, t0, 128)); ld(lf, lf[:], rows(hlf[di], t0, 128))
                ld(k, k[:], rows(hk[di], t0, 128)); ld(v, v[:], rows(hv, t0, 128))
                yield
                mm(B[0][:], CF[:, ci_cum, :], lf[:], [CF, lf], B[0], True)
                mm(B[1][:], CF[:, ci_rem, :], lf[:], [CF, lf], B[1], True)
                for c in range(2):
                    for h in range(4):
                        mm(B[2][:, c * 4 + h:c * 4 + h + 1], lf[c * 64:(c + 1) * 64, h * 128:(h + 1) * 128],
                           CF[c * 64:(c + 1) * 64, C_ONE, 0:1], [CF, lf], B[2], c == 0 and h == 0)
                yield
                eg = eG.next(); eng_ = enG.next(); er = eR.next(); el = egl.next()
                P.op("act", lambda e: e.activation(out=eg[:], in_=B[0][:], func=AF.Exp), reads=[B[0]], writes=[eg])
                P.op("act", lambda e: e.activation(out=eng_[:], in_=B[0][:], func=AF.Exp, scale=-1.0), reads=[B[0]], writes=[eng_])
                P.op("act", lambda e: e.activation(out=er[:], in_=B[1][:], func=AF.Exp), reads=[B[1]], writes=[er])
                P.op("act", lambda e: e.activation(out=el[:], in_=B[2][:, 0:8], func=AF.Exp), reads=[B[2]], writes=[el])
                yield
                qt = qt_t.next(); kt = kt_t.next(); kd = kd_t.next()
                P.op("dve", lambda e: e.scalar_tensor_tensor(out=qt[:], in0=q[:], scalar=float(128 ** -0.5), in1=eg[:], op0=ALU.mult, op1=ALU.mult),
                     reads=[q, eg], writes=[qt])
                P.op("pool", lambda e: e.tensor_tensor(out=kt[:], in0=k[:], in1=eng_[:], op=ALU.mult), reads=[k, eng_], writes=[kt])
                P.op("pool", lambda e: e.tensor_tensor(out=kd[:], in0=k[:], in1=er[:], op=ALU.mult), reads=[k, er], writes=[kd])
                yield
                qk = QKT.next()
                for h in range(4):
                    P.op("pe", lambda e, h=h: e.transpose(out=psb(3, h * 128, (h + 1) * 128), in_=qt[:, h * 128:(h + 1) * 128], identity=IDB[:]),
                         reads=[qt, IDB], writes=[B[3]] if h == 0 else (), pwrites=() if h == 0 else [B[3]])
                for h in range(4):
                    P.op("pe", lambda e, h=h: e.transpose(out=psb(3, (4 + h) * 128, (5 + h) * 128), in_=kt[:, h * 128:(h + 1) * 128], identity=IDB[:]),
                         reads=[kt, IDB], pwrites=[B[3]])
                P.op("dve", lambda e: e.tensor_copy(out=qk[:].rearrange("p a b -> p (a b)"), in_=psb(3, 0, 1024)), reads=[B[3]], writes=[qk])
                for h in range(4):
                    mm(B[0][:, h * 128:(h + 1) * 128], qk[:, 4 + h, :], qk[:, h, :], [qk], B[0], h == 0)
                yield
                at = AT.next()
                hm = M4[C_UI if di == 0 else C_LI]
                P.op("dve", lambda e: e.tensor_tensor(out=at[:].rearrange("p a b -> p (a b)"), in0=B[0][:], in1=hm[:].rearrange("p a b -> p (a b)"), op=ALU.mult),
                     reads=[B[0], hm], writes=[at])
                S32 = SA32[di]; Sbf = SAbf[di]
                order = (0, 1) if di == 0 else (1, 0)
                for ic, c in enumerate(order):
                    yield
                    r0 = c * 64
                    bd = B[1] if ic == 0 else B[3]
                    for h in range(4):
                        hs = slice(h * 128, (h + 1) * 128)
                        mm(B[2][r0:r0 + 64, hs], at[r0:r0 + 64, h, r0:r0 + 64], v[r0:r0 + 64, hs], [at, v], B[2],
                           ic == 0 and h == 0, start=True, stop=False)
                        mm(B[2][r0:r0 + 64, hs], qk[:, h, r0:r0 + 64], Sbf[:, h, :], [qk, Sbf], B[2], False, start=False, stop=True)
                    for h in range(4):
                        hs = slice(h * 128, (h + 1) * 128)
                        mm(bd[:, hs], kd[r0:r0 + 64, hs], v[r0:r0 + 64, hs], [kd, v], bd, h == 0)
                    for h in range(4):
                        hs = slice(h * 128, (h + 1) * 128)
                        P.op("dve", lambda e, h=h, hs=hs, c=c, bd=bd: e.scalar_tensor_tensor(
                            out=S32[:, h, :], in0=S32[:, h, :], scalar=el[:, c * 4 + h:c * 4 + h + 1], in1=bd[:, hs], op0=ALU.mult, op1=ALU.add),
                            reads=[S32, el, bd], writes=[S32])
                    P.op("act", lambda e: e.activation(out=Sbf[:], in_=S32[:], func=AF.Copy), reads=[S32], writes=[Sbf])
                yield
                o = oA.next()
                P.op("act", lambda e: e.activation(out=o[:], in_=B[2][:], func=AF.Copy), reads=[B[2]], writes=[o])
                st(rows(oa_d[di], t0, 128), o, o[:])

            def gdn_step(t, di):
                t0 = t if isinstance(t, DynRow) else t * 128
                ci_cum, ci_rem = (C_UI, C_SL) if di == 0 else (C_LI, C_SU)
                ci_n, ci_dt = (C_SL, C_UI) if di == 0 else (C_SU, C_LI)
                gq = gq_t.next(); gk = gk_t.next(); gv = gv_t.next(); bg = bg_t.next()
                ld(gq, gq[:].rearrange("p a b -> p (a b)"), rows(gq_d, t0, 128))
                ld(gk, gk[:].rearrange("p a b -> p (a b)"), rows(gk_d, t0, 128))
                ld(gv, gv[:].rearrange("p a b -> p (a b)"), rows(gv_d, t0, 128))
                ld(bg, bg[:], rows(bg_d, t0, 128))
                yield
                beta = bg[:, di * 4:di * 4 + 4]
                gg = bg[:, 8 + di * 4:8 + di * 4 + 4]
                mm(B[4][:, 0:4], CF[:, ci_cum, :], gg, [CF, bg], B[4], True)
                mm(B[4][:, 4:8], CF[:, ci_rem, :], gg, [CF, bg], B[4], False)
                mm(B[4][:, 8:12], CF[:, C_SEL0, :], gg, [CF, bg], B[4], False)
                mm(B[4][:, 12:16], CF[:, C_SEL1, :], gg, [CF, bg], B[4], False)
                yield
                ec = ecol.next(); cl = CL.next()
                P.op("act", lambda e: e.activation(out=ec[:], in_=B[4][:, 0:16], func=AF.Exp), reads=[B[4]], writes=[ec])
                P.op("dve", lambda e: e.tensor_tensor(out=cl[:, 0:4], in0=beta, in1=ec[:, 0:4], op=ALU.mult), reads=[bg, ec], writes=[cl])
                P.op("dve", lambda e: e.tensor_scalar(out=cl[:, 4:8], in0=beta, scalar1=-1.0, scalar2=None, op0=ALU.mult), reads=[bg], pwrites=[cl])
                P.op("dve", lambda e: e.tensor_scalar(out=cl[:, 8:12], in0=gg, scalar1=-1.0, scalar2=None, op0=ALU.mult), reads=[bg], pwrites=[cl])
                yield
                yv = Y0v.next(); yk = Y0k.next(); kdc = kdec.next(); qdc = qdec.next(); kb = knb.next(); qb = qnb.next()

                def bc(col_ap):
                    return col_ap.unsqueeze(2).to_broadcast([128, 4, 128])
                P.op("pool", lambda e: e.tensor_tensor(out=yv[:], in0=gv[:], in1=bc(beta), op=ALU.mult), reads=[gv, bg], writes=[yv])
                P.op("dve", lambda e: e.tensor_tensor(out=yk[:], in0=gk[:], in1=bc(cl[:, 0:4]), op=ALU.mult), reads=[gk, cl], writes=[yk])
                P.op("pool", lambda e: e.tensor_tensor(out=kdc[:], in0=gk[:], in1=bc(ec[:, 4:8]), op=ALU.mult), reads=[gk, ec], writes=[kdc])
                P.op("dve", lambda e: e.tensor_tensor(out=qdc[:], in0=gq[:], in1=bc(ec[:, 0:4]), op=ALU.mult), reads=[gq, ec], writes=[qdc])
                P.op("act", lambda e: e.activation(out=kb[:], in_=gk[:], func=AF.Copy), reads=[gk], writes=[kb])
                P.op("act", lambda e: e.activation(out=qb[:], in_=gq[:], func=AF.Copy), reads=[gq], writes=[qb])
                yield
                kqt = KQT.next(); qdt = QDT.next()
                for h in range(4):
                    P.op("pe", lambda e, h=h: e.transpose(out=psb(5, (2 * h) * 128, (2 * h + 1) * 128), in_=kb[:, h, :], identity=IDB[:]),
                         reads=[kb, IDB], writes=[B[5]] if h == 0 else (), pwrites=() if h == 0 else [B[5]])
                    P.op("pe", lambda e, h=h: e.transpose(out=psb(5, (2 * h + 1) * 128, (2 * h + 2) * 128), in_=qb[:, h, :], identity=IDB[:]),
                         reads=[qb, IDB], pwrites=[B[5]])
                for h in range(4):
                    P.op("pe", lambda e, h=h: e.transpose(out=psb(6, h * 128, (h + 1) * 128), in_=qdc[:, h, :], identity=IDB[:]),
                         reads=[qdc, IDB], writes=[B[6]] if h == 0 else (), pwrites=() if h == 0 else [B[6]])
                P.op("dve", lambda e: e.tensor_copy(out=kqt[:].rearrange("p a b c -> p (a b c)"), in_=psb(5, 0, 1024)), reads=[B[5]], writes=[kqt])
                P.op("act", lambda e: e.activation(out=qdt[:].rearrange("p a b -> p (a b)"), in_=psb(6, 0, 512), func=AF.Copy), reads=[B[6]], writes=[qdt])
                yield
                grep = GREP.next(); ngc = NGC.next()
                P.op("pool", lambda e: e.tensor_copy(out=grep[:], in_=bc(gg)), reads=[bg], writes=[grep])
                P.op("pool", lambda e: e.tensor_tensor(out=ngc[:], in0=M4[ci_cum][:], in1=bc(cl[:, 8:12]), op=ALU.mult), reads=[M4[ci_cum], cl], writes=[ngc])
                for h in range(4):
                    hs = slice(h * 128, (h + 1) * 128)
                    mm(B[7][:, hs], grep[:, h, :], CF[:, ci_cum, :], [grep, CF], B[7], h == 0, start=True, stop=False)
                    mm(B[7][:, hs], ngc[:, h, :], ONESF, [ngc, CF], B[7], False, start=False, stop=True)
                yield
                edt = EDT.next(); ed = ED.next(); dm = Dm.next(); dtm = DTm.next()
                f2 = lambda b_: b_[:].rearrange("p a b -> p (a b)")
                P.op("dve", lambda e: e.tensor_tensor(out=f2(edt), in0=B[7][:], in1=f2(NM4[ci_dt]), op=ALU.add),
                     reads=[B[7], NM4[ci_dt]], writes=[edt])
                P.op("dve", lambda e: e.scalar_tensor_tensor(out=f2(ed), in0=B[7][:], scalar=-1.0, in1=f2(NM4[ci_n]), op0=ALU.mult, op1=ALU.add),
                     reads=[B[7], NM4[ci_n]], writes=[ed])
                P.op("act", lambda e: e.activation(out=f2(dtm), in_=f2(edt), func=AF.Exp), reads=[edt], writes=[dtm])
                P.op("act", lambda e: e.activation(out=f2(dm), in_=f2(ed), func=AF.Exp), reads=[ed], writes=[dm])
                yield
                kkb = (B[4], B[5])
                for h in range(4):
                    bk_ = kkb[h // 2]
                    mm(bk_[:, (h % 2) * 256:(h % 2) * 256 + 256], kqt[:, h, 0, :], kqt[:, h, :, :].rearrange("p a b -> p (a b)"), [kqt], bk_, h % 2 == 0)
                yield
                n0 = NN.next(); m0 = MM.next(); p32 = P32.next(); qkm = QKm.next()
                for h in range(4):
                    bk_ = kkb[h // 2]
                    P.op("dve", lambda e, h=h, bk_=bk_: e.scalar_tensor_tensor(
                        out=n0[:, h, :], in0=bk_[:, (h % 2) * 256:(h % 2) * 256 + 128], scalar=cl[:, 4 + h:5 + h], in1=dm[:, h, :],
                        op0=ALU.mult, op1=ALU.mult), reads=[bk_, cl, dm], writes=[n0] if h == 0 else (), pwrites=() if h == 0 else [n0])
                for b2 in range(2):
                    bk_ = kkb[b2]
                    P.op("dve", lambda e, b2=b2, bk_=bk_: e.tensor_tensor(
                        out=qkm[:, 2 * b2:2 * b2 + 2, :], in0=bk_[:].rearrange("p (h w k) -> p h w k", h=2, w=2)[:, :, 1, :],
                        in1=dtm[:, 2 * b2:2 * b2 + 2, :], op=ALU.mult), reads=[bk_, dtm], writes=[qkm] if b2 == 0 else (), pwrites=() if b2 == 0 else [qkm])
                for h in range(4):
                    P.op("pe", lambda e, h=h: e.transpose(out=B[6][:, h * 128:(h + 1) * 128], in_=n0[:, h, :], identity=ident),
                         reads=[n0, CF], writes=[B[6]] if h == 0 else (), pwrites=() if h == 0 else [B[6]])
                P.op("act", lambda e: e.activation(out=f2(m0), in_=B[6][:], func=AF.Copy), reads=[B[6]], writes=[m0])
                P.op("dve", lambda e: e.tensor_tensor(out=f2(p32), in0=B[6][:], in1=f2(I4), op=ALU.add), reads=[B[6], I4], writes=[p32])
                yield
                ncur, mcur = n0, m0
                sets = ((B[4], B[5], B[7]), (B[4], B[5], B[7]))
                for lev in range(1, 6):
                    yield
                    bn, bm, bp = sets[lev % 2]
                    nn_ = NN.next()
                    for h in range(4):
                        mm(bn[:, h * 128:(h + 1) * 128], mcur[:, h, :], ncur[:, h, :], [mcur, ncur], bn, h == 0)
                    P.op("act", lambda e, nn_=nn_, bn=bn: e.activation(out=f2(nn_), in_=bn[:], func=AF.Copy), reads=[bn], writes=[nn_])
                    if lev < 5:
                        mn_ = MM.next()
                        for h in range(4):
                            mm(bm[:, h * 128:(h + 1) * 128], ncur[:, h, :], mcur[:, h, :], [mcur, ncur], bm, h == 0)
                        P.op("dve", lambda e, mn_=mn_, bm=bm: e.tensor_copy(out=f2(mn_), in_=bm[:]), reads=[bm], writes=[mn_])
                    for h in range(4):
                        mm(bp[:, h * 128:(h + 1) * 128], nn_[:, h, :], p32[:, h, :], [nn_, p32], bp, h == 0)
                    P.op("dve", lambda e, bp=bp: e.tensor_tensor(out=f2(p32), in0=f2(p32), in1=bp[:], op=ALU.add), reads=[p32, bp], writes=[p32])
                    ncur = nn_
                    if lev < 5:
                        mcur = mn_
                yield
                tt = TT.next(); u32 = U32.next(); wt = WT.next()
                P.op("act", lambda e: e.activation(out=f2(tt), in_=f2(p32), func=AF.Copy), reads=[p32], writes=[tt])
                for h in range(4):
                    mm(B[4][:, h * 128:(h + 1) * 128], tt[:, h, :], yv[:, h, :], [tt, yv], B[4], h == 0)
                for h in range(4):
                    mm(B[5][:, h * 128:(h + 1) * 128], yk[:, h, :], tt[:, h, :], [tt, yk], B[5], h == 0)
                P.op("act", lambda e: e.activation(out=u32[:], in_=B[4][:], func=AF.Copy), reads=[B[4]], writes=[u32])
                P.op("dve", lambda e: e.tensor_copy(out=f2(wt), in_=B[5][:]), reads=[B[5]], writes=[wt])
                yield
                S32 = SC32[di]; Sbf = SCbf[di]
                vn = VN.next()
                order = (0, 1) if di == 0 else (1, 0)
                for ic, c in enumerate(order):
                    yield
                    r0 = c * 64
                    bv_ = B[4]
                    bd = B[5] if ic == 0 else B[7]
                    for h in range(4):
                        hs = slice(h * 128, (h + 1) * 128)
                        mm(bv_[r0:r0 + 64, hs], wt[:, h, r0:r0 + 64], Sbf[:, h, :], [wt, Sbf], bv_, h == 0)
                    P.op("dve", lambda e, r0=r0, bv_=bv_: e.tensor_tensor(out=vn[r0:r0 + 64, :], in0=u32[r0:r0 + 64, :], in1=bv_[r0:r0 + 64, :], op=ALU.subtract),
                         reads=[u32, bv_], writes=[vn] if ic == 0 else (), pwrites=() if ic == 0 else [vn])
                    for h in range(4):
                        hs = slice(h * 128, (h + 1) * 128)
                        mm(B[6][r0:r0 + 64, hs], qkm[r0:r0 + 64, h, r0:r0 + 64], vn[r0:r0 + 64, hs], [qkm, vn], B[6],
                           ic == 0 and h == 0, start=True, stop=False)
                        mm(B[6][r0:r0 + 64, hs], qdt[:, h, r0:r0 + 64], Sbf[:, h, :], [qdt, Sbf], B[6], False, start=False, stop=True)
                    for h in range(4):
                        hs = slice(h * 128, (h + 1) * 128)
                        mm(bd[:, hs], kdc[r0:r0 + 64, h, :], vn[r0:r0 + 64, hs], [kdc, vn], bd, h == 0)
                    for h in range(4):
                        hs = slice(h * 128, (h + 1) * 128)
                        P.op("dve", lambda e, h=h, hs=hs, c=c, bd=bd: e.scalar_tensor_tensor(
                            out=S32[:, h, :], in0=S32[:, h, :], scalar=ec[:, 8 + c * 4 + h:9 + c * 4 + h], in1=bd[:, hs], op0=ALU.mult, op1=ALU.add),
                            reads=[S32, ec, bd], writes=[S32])
                    P.op("act", lambda e: e.activation(out=f2(Sbf), in_=f2(S32), func=AF.Copy), reads=[S32], writes=[Sbf])
                yield
                o = oC.next()
                P.op("act", lambda e: e.activation(out=o[:], in_=B[6][:], func=AF.Copy), reads=[B[6]], writes=[o])
                st(rows(oc_d[di], t0, 128), o, o[:])

            def run_pair(g1, g2):
                gens = [g1, g2]
                while gens:
                    for g_ in list(gens):
                        try:
                            next(g_)
                        except StopIteration:
                            gens.remove(g_)

            ixf = RowIdx(T, "ixf", [0], 0, 128)
            ixb = RowIdx(T, "ixb", [0], (NT - 1) * 128, -128)

            def p2_body():
                gs4 = [hgrn_step(DynRow(ixf), 0), gdn_step(DynRow(ixf), 0), hgrn_step(DynRow(ixb), 1), gdn_step(DynRow(ixb), 1)]
                for g_ in gs4:
                    next(g_)
                run_pair(gs4[0], gs4[1])
                run_pair(gs4[2], gs4[3])

            if NT > 1:
                loop(0, NT, p2_body, [ixf, ixb])
            else:
                run_pair(hgrn_step(0, 0), gdn_step(0, 0)); run_pair(hgrn_step(0, 1), gdn_step(0, 1))
            end_phase(T)


        def phase_p3(seq, l, xcur, xnext):
            S = seq.S
            BLK = min(512, S)
            NSUB = BLK // 128
            NB = S // BLK
            T = Tiles()
            LNG = T.sb("LNG", [128, 3, D], F32)
            LNB = T.sb("LNB", [128, 3, D], F32)
            NGA = T.sb("NGA", [128, 128], F32)
            NGC = T.sb("NGC", [128, 128], F32)
            memKT = T.sb("memKT", [128, 8, MEM], BF16)
            memV = T.sb("memV", [128, 2, D], BF16)
            xin = T.ring("xin", [128, D], F32, max(NSUB, 2))
            FT = [T.sb(f"FT{i}", [128, 8, max(BLK, MEM)], BF16) for i in range(3)]
            wpan = T.ring("wpan", [128, KC, 512], BF16, 3)
            wfo = T.ring("wfo", [128, 11, 512], BF16, 2)
            HID = T.sb("HID", [128, FKC, BLK], BF16)
            o32 = T.ring("o32", [128, 512], F32, 4)
            g16 = T.ring("g16", [128, 512], BF16, 3)
            mgt = T.ring("mgt", [128, 3072], BF16, 2)
            tmp = T.ring("tmp", [128, 512], F32, 4)
            on16 = T.ring("on16", [128, 512], BF16, 3)
            mixt = T.ring("mixt", [128, D], BF16, 2)
            pbt = T.ring("pbt", [128, 4, MEM], BF16, 2)
            pTt = T.ring("pTt", [128, 8, 128], BF16, 2)
            small = T.ring("small", [128, 16], F32, 8)
            bnst = T.ring("bnst", [128, 2, 6], F32, 2)
            ld(LNG, LNG[:].rearrange("p a b -> p (a b)"), ln_g[l:l + 1, :].partition_broadcast(128))
            ld(LNB, LNB[:].rearrange("p a b -> p (a b)"), ln_b[l:l + 1, :].partition_broadcast(128))
            ld(NGA, NGA[:], hgrn_norm_g[l:l + 1, :].partition_broadcast(128))
            ld(NGC, NGC[:], gdn_norm_g[l:l + 1, :].partition_broadcast(128))
            pr = Ring(PS[0:6])
            ptr = Ring([6, 7])
            f3 = lambda b_: b_[:].rearrange("p a b -> p (a b)")

            def wload(src2d, c0, ncol, nk=KC):
                wp = wpan.next()
                ld(wp, wp[:, 0:nk, 0:ncol], src2d[:, c0:c0 + ncol].rearrange("(kc p) c -> p kc c", p=128))
                return wp

            def tr16(src_list, dst_ft, k0, s):
                n = len(src_list)
                bi = ptr.next()
                for i, (bf_, ap_) in enumerate(src_list):
                    P.op("pe", lambda e, i=i, ap_=ap_, bi=bi: e.transpose(out=psb(bi, i * 128, (i + 1) * 128), in_=ap_, identity=IDB[:]),
                         reads=[bf_, IDB], writes=[PS[bi]] if i == 0 else (), pwrites=() if i == 0 else [PS[bi]])
                P.op("act", lambda e, bi=bi: e.activation(out=dst_ft[:, k0:k0 + n, s * 128:(s + 1) * 128],
                                                         in_=psb(bi, 0, n * 128).rearrange("p (a b) -> p a b", a=n), func=AF.Copy),
                     reads=[PS[bi]], pwrites=[dst_ft])

            def tr32(xt, dst_ft, s):
                for half in range(2):
                    bi = ptr.next()
                    for j in range(4):
                        kc = half * 4 + j
                        P.op("pe", lambda e, j=j, kc=kc, bi=bi: e.transpose(out=PS[bi][:, j * 128:(j + 1) * 128], in_=xt[:, kc * 128:(kc + 1) * 128], identity=ident),
                             reads=[xt, CF], writes=[PS[bi]] if j == 0 else (), pwrites=() if j == 0 else [PS[bi]])
                    eng = "act" if half == 0 else "dve"
                    if eng == "act":
                        P.op("act", lambda e, bi=bi, half=half: e.activation(out=dst_ft[:, half * 4:half * 4 + 4, s * 128:(s + 1) * 128],
                                                                            in_=PS[bi][:].rearrange("p (a b) -> p a b", a=4), func=AF.Copy),
                             reads=[PS[bi]], pwrites=[dst_ft])
                    else:
                        P.op("dve", lambda e, bi=bi, half=half: e.tensor_copy(out=dst_ft[:, half * 4:half * 4 + 4, s * 128:(s + 1) * 128],
                                                                             in_=PS[bi][:].rearrange("p (a b) -> p a b", a=4)),
                             reads=[PS[bi]], pwrites=[dst_ft])

            def layer_norm(xt, idx):
                bs = bnst.next(); sm = small.next()
                for hf in range(2):
                    P.op("dve", lambda e, hf=hf: e.bn_stats(out=bs[:, hf, :], in_=xt[:, hf * 512:(hf + 1) * 512]), reads=[xt],
                         writes=[bs] if hf == 0 else (), pwrites=() if hf == 0 else [bs])
                P.op("dve", lambda e: e.bn_aggr(out=sm[:, 0:2], in_=bs[:]), reads=[bs], writes=[sm])
                P.op("act", lambda e: e.activation(out=sm[:, 2:3], in_=sm[:, 1:2], func=AF.Sqrt, bias=EPS[:, 1:2]), reads=[sm, EPS], writes=[sm])
                P.op("dve", lambda e: e.reciprocal(out=sm[:, 3:4], in_=sm[:, 2:3]), reads=[sm], writes=[sm])
                P.op("dve", lambda e: e.tensor_scalar(out=xt[:], in0=xt[:], scalar1=sm[:, 0:1], scalar2=sm[:, 3:4], op0=ALU.subtract, op1=ALU.mult),
                     reads=[xt, sm], writes=[xt])
                P.op("pool", lambda e: e.tensor_tensor(out=xt[:], in0=xt[:], in1=LNG[:, idx, :], op=ALU.mult), reads=[xt, LNG], writes=[xt])
                P.op("pool", lambda e: e.tensor_tensor(out=xt[:], in0=xt[:], in1=LNB[:, idx, :], op=ALU.add), reads=[xt, LNB], writes=[xt])

            def proj_res(src_ft, wtiles, xts):
                for s in range(NSUB):
                    for half in range(2):
                        ps = pr.next()
                        wp = wtiles[half]
                        for kc in range(KC):
                            mm(ps[:], src_ft[:, kc, s * 128:(s + 1) * 128], wp[:, kc, :], [src_ft, wp], ps, kc == 0, start=(kc == 0), stop=(kc == KC - 1))
                        xt = xts[s]
                        P.op("dve", lambda e, ps=ps, xt=xt, half=half: e.scalar_tensor_tensor(
                            out=xt[:, half * 512:(half + 1) * 512], in0=xt[:, half * 512:(half + 1) * 512], scalar=ALPHA, in1=ps[:],
                            op0=ALU.mult, op1=ALU.add), reads=[xt, ps], writes=[xt])

            def mm(out_ap, lhsT, rhs, rd, bank, first, start=True, stop=True):
                P.op("pe", lambda e: e.matmul(out_ap, lhsT=lhsT, rhs=rhs, start=start, stop=stop),
                     reads=rd, writes=[bank] if first else (), pwrites=() if first else [bank])

            memT = FT[0]
            for mt in range(2):
                xt = xin.next()
                ld(xt, xt[:], seq.mem[mt * 128:(mt + 1) * 128, :])
                tr32(xt, memT, mt)
            for pc in range(4):
                wp = wload(wb["wkv"][l], pc * 512, 512)
                if pc < 2:
                    for c4 in range(4):
                        c = pc * 4 + c4
                        ps = pr.next()
                        for kc in range(KC):
                            mm(ps[:, 0:MEM], wp[:, kc, c4 * 128:(c4 + 1) * 128], memT[:, kc, 0:MEM], [wp, memT], ps, kc == 0,
                               start=(kc == 0), stop=(kc == KC - 1))
                        P.op("act", lambda e, ps=ps, c=c: e.activation(out=memKT[:, c, :], in_=ps[:, 0:MEM], func=AF.Copy), reads=[ps], pwrites=[memKT])
                else:
                    half = pc - 2
                    for mt in range(2):
                        ps = pr.next()
                        for kc in range(KC):
                            mm(ps[:], memT[:, kc, mt * 128:(mt + 1) * 128], wp[:, kc, :], [wp, memT], ps, kc == 0, start=(kc == 0), stop=(kc == KC - 1))
                        P.op("dve", lambda e, ps=ps, mt=mt, half=half: e.tensor_copy(out=memV[:, mt, half * 512:(half + 1) * 512], in_=ps[:]),
                             reads=[ps], pwrites=[memV])

            ix3 = RowIdx(T, "ix3", [s_ * 128 for s_ in range(NSUB)], 0, BLK)

            def p3_body(b=None):
                t0 = DynRow(ix3) if b is None else b * BLK
                xts = []
                for s in range(NSUB):
                    xt = xin.next()
                    ld(xt, xt[:], rows(xcur, t0 + s * 128, 128))
                    xts.append(xt)
                mgs = []
                for s in range(NSUB):
                    r0 = t0 + s * 128
                    outs_n = []
                    for mi, (od, gd, NG) in enumerate(((oa_d, ga_d, NGA), (oc_d, gcg_d, NGC))):
                        of_ = o32.next(); ob_ = o32.next(); gt = g16.next()
                        ld(of_, of_[:], rows(od[0], r0, 128)); ld(ob_, ob_[:], rows(od[1], r0, 128)); ld(gt, gt[:], rows(gd, r0, 128))
                        sq = tmp.next(); sm = small.next(); on = on16.next()
                        P.op("pool", lambda e, of_=of_, ob_=ob_: e.tensor_tensor(out=of_[:], in0=of_[:], in1=ob_[:], op=ALU.add), reads=[of_, ob_], writes=[of_])
                        P.op("pool", lambda e, of_=of_, sq=sq: e.tensor_tensor(out=sq[:], in0=of_[:], in1=of_[:], op=ALU.mult), reads=[of_], writes=[sq])
                        P.op("dve", lambda e, sq=sq, sm=sm: e.tensor_reduce(out=sm[:, 0:4], in_=sq[:].rearrange("p (h d) -> p h d", h=4), axis=AX.X, op=ALU.add),
                             reads=[sq], writes=[sm])
                        P.op("act", lambda e, sm=sm: e.activation(out=sm[:, 4:8], in_=sm[:, 0:4], func=AF.Sqrt, bias=EPS[:, 0:1], scale=1.0 / 128.0),
                             reads=[sm, EPS], writes=[sm])
                        P.op("dve", lambda e, sm=sm: e.reciprocal(out=sm[:, 8:12], in_=sm[:, 4:8]), reads=[sm], writes=[sm])
                        P.op("dve", lambda e, of_=of_, sm=sm: e.tensor_tensor(out=of_[:].rearrange("p (h d) -> p h d", h=4), in0=of_[:].rearrange("p (h d) -> p h d", h=4),
                                                                            in1=sm[:, 8:12].unsqueeze(2).to_broadcast([128, 4, 128]), op=ALU.mult),
                             reads=[of_, sm], writes=[of_])
                        P.op("pool", lambda e, of_=of_, NG=NG: e.tensor_tensor(out=of_[:].rearrange("p (h d) -> p h d", h=4), in0=of_[:].rearrange("p (h d) -> p h d", h=4),
                                                                             in1=NG[:].unsqueeze(1).to_broadcast([128, 4, 128]), op=ALU.mult),
                             reads=[of_, NG], writes=[of_])
                        P.op("dve", lambda e, of_=of_, gt=gt, on=on: e.tensor_tensor(out=on[:], in0=of_[:], in1=gt[:], op=ALU.mult), reads=[of_, gt], writes=[on])
                        outs_n.append(on)
                    obt = g16.next()
                    ld(obt, obt[:], rows(ob_d, r0, 128))
                    mg = mgt.next()
                    for j6 in range(6):
                        ld(mg, mg[:, j6 * 512:(j6 + 1) * 512], rows(mg_d[j6], r0, 128), partial=(j6 > 0))
                    mgs.append(mg)
                    tr16([(outs_n[0], outs_n[0][:, k * 128:(k + 1) * 128]) for k in range(4)] +
                         [(outs_n[1], outs_n[1][:, k * 128:(k + 1) * 128]) for k in range(4)], FT[1], 0, s)
                    tr16([(obt, obt[:, k * 128:(k + 1) * 128]) for k in range(4)], FT[2], 0, s)
                    if s == 0:
                        wbr = []
                        for nm in ("wba", "wbb", "wbc"):
                            wp = wpan.next()
                            ld(wp, wp[:].rearrange("p a b -> p (a b)").rearrange("p (k c) -> p k c", k=4),
                               wb[nm][l].rearrange("(kc p) c -> p kc c", p=128))
                            wbr.append(wp)
                    mx = mixt.next()
                    for half in range(2):
                        pss = []
                        for bi_, (ft, k0) in enumerate(((FT[1], 0), (FT[2], 0), (FT[1], 4))):
                            ps = pr.next()
                            wv = wbr[bi_][:].rearrange("p a b -> p (a b)").rearrange("p (k c) -> p k c", k=4)
                            for kc in range(4):
                                mm(ps[:], ft[:, k0 + kc, s * 128:(s + 1) * 128], wv[:, kc, half * 512:(half + 1) * 512], [ft, wbr[bi_]], ps, kc == 0,
                                   start=(kc == 0), stop=(kc == 3))
                            pss.append(ps)
                        ta = tmp.next(); tb = tmp.next()
                        hs = slice(half * 512, (half + 1) * 512)
                        P.op("dve", lambda e, ps=pss[0], ta=ta, mg=mg, half=half: e.tensor_tensor(out=ta[:], in0=ps[:], in1=mg[:, half * 512:(half + 1) * 512], op=ALU.mult),
                             reads=[pss[0], mg], writes=[ta])
                        P.op("dve", lambda e, ps=pss[1], tb=tb, mg=mg, half=half: e.tensor_tensor(out=tb[:], in0=ps[:], in1=mg[:, 1024 + half * 512:1024 + (half + 1) * 512], op=ALU.mult),
                             reads=[pss[1], mg], writes=[tb])
                        P.op("pool", lambda e, ta=ta, tb=tb: e.tensor_tensor(out=ta[:], in0=ta[:], in1=tb[:], op=ALU.add), reads=[ta, tb], writes=[ta])
                        tc_ = tmp.next()
                        P.op("dve", lambda e, ps=pss[2], tc_=tc_, mg=mg, half=half: e.tensor_tensor(out=tc_[:], in0=ps[:], in1=mg[:, 2048 + half * 512:2048 + (half + 1) * 512], op=ALU.mult),
                             reads=[pss[2], mg], writes=[tc_])
                        P.op("pool", lambda e, ta=ta, tc_=tc_, mx=mx, hs=hs: e.tensor_tensor(out=mx[:, hs], in0=ta[:], in1=tc_[:], op=ALU.add),
                             reads=[ta, tc_], writes=[mx] if half == 0 else (), pwrites=() if half == 0 else [mx])
                    tr16([(mx, mx[:, k * 128:(k + 1) * 128]) for k in range(8)], FT[0], 0, s)
                wm = [wload(wb["wmix"][l], h_ * 512, 512) for h_ in range(2)]
                proj_res(FT[0], wm, xts)
                for s in range(NSUB):
                    layer_norm(xts[s], 0)
                    tr32(xts[s], FT[1], s)
                for pc in range(2):
                    wp = wload(wb["wmq"][l], pc * 512, 512)
                    for c4 in range(4):
                        c = pc * 4 + c4
                        ps = pr.next()
                        for kc in range(KC):
                            mm(ps[:, 0:BLK], wp[:, kc, c4 * 128:(c4 + 1) * 128], FT[1][:, kc, 0:BLK], [wp, FT[1]], ps, kc == 0,
                               start=(kc == 0), stop=(kc == KC - 1))
                        P.op("act", lambda e, ps=ps, c=c: e.activation(out=FT[2][:, c, 0:BLK], in_=ps[:, 0:BLK], func=AF.Copy, scale=1.0 / 16.0),
                             reads=[ps], pwrites=[FT[2]])
                for s in range(NSUB):
                    ss_ = slice(s * 128, (s + 1) * 128)
                    sb2 = [pr.next(), pr.next()]
                    for h in range(4):
                        ps = sb2[h // 2]
                        for dc in range(2):
                            mm(ps[:, (h % 2) * 256:(h % 2) * 256 + 256], FT[2][:, 2 * h + dc, ss_], memKT[:, 2 * h + dc, :], [FT[2], memKT], ps,
                               h % 2 == 0 and dc == 0, start=(dc == 0), stop=(dc == 1))
                    sm = small.next()
                    for bnk in range(2):
                        P.op("dve", lambda e, bnk=bnk, sm=sm, ps=sb2[bnk]: e.tensor_reduce(out=sm[:, 2 * bnk:2 * bnk + 2], in_=ps[:].rearrange("p (h m) -> p h m", h=2),
                                                                                     axis=AX.X, op=ALU.max),
                             reads=[sb2[bnk]], writes=[sm] if bnk == 0 else (), pwrites=() if bnk == 0 else [sm])
                    P.op("dve", lambda e, sm=sm: e.tensor_scalar(out=sm[:, 4:8], in0=sm[:, 0:4], scalar1=-1.0, scalar2=None, op0=ALU.mult), reads=[sm], writes=[sm])
                    pb = pbt.next()
                    for h in range(4):
                        ps = sb2[h // 2]
                        P.op("act", lambda e, h=h, ps=ps, pb=pb, sm=sm: e.activation(out=pb[:, h, :], in_=ps[:, (h % 2) * 256:(h % 2) * 256 + 256], func=AF.Exp,
                                                                                     bias=sm[:, 4 + h:5 + h], accum_out=sm[:, 8 + h:9 + h]),
                             reads=[ps, sm], pwrites=[pb, sm])
                    P.op("dve", lambda e, sm=sm: e.reciprocal(out=sm[:, 12:16], in_=sm[:, 8:12]), reads=[sm], writes=[sm])
                    P.op("dve", lambda e, pb=pb, sm=sm: e.tensor_tensor(out=pb[:], in0=pb[:], in1=sm[:, 12:16].unsqueeze(2).to_broadcast([128, 4, MEM]), op=ALU.mult),
                         reads=[pb, sm], writes=[pb])
                    pT = pTt.next()
                    bi = ptr.next()
                    for i in range(8):
                        h, mt = i // 2, i % 2
                        P.op("pe", lambda e, i=i, h=h, mt=mt, bi=bi, pb=pb: e.transpose(out=psb(bi, i * 128, (i + 1) * 128), in_=pb[:, h, mt * 128:(mt + 1) * 128], identity=IDB[:]),
                             reads=[pb, IDB], writes=[PS[bi]] if i == 0 else (), pwrites=() if i == 0 else [PS[bi]])
                    P.op("dve", lambda e, bi=bi, pT=pT: e.tensor_copy(out=pT[:].rearrange("p a b -> p (a b)"), in_=psb(bi, 0, 1024)), reads=[PS[bi]], writes=[pT])
                    ob2 = [pr.next(), pr.next()]
                    for c in range(8):
                        ps = ob2[c // 4]
                        hd = c // 2
                        for mt in range(2):
                            mm(ps[:, (c % 4) * 128:(c % 4) * 128 + 128], memV[:, mt, c * 128:(c + 1) * 128], pT[:, hd * 2 + mt, :], [memV, pT], ps,
                               c % 4 == 0 and mt == 0, start=(mt == 0), stop=(mt == 1))
                    for bnk in range(2):
                        P.op("act", lambda e, bnk=bnk, ps=ob2[bnk], ss_=ss_: e.activation(out=FT[0][:, bnk * 4:bnk * 4 + 4, ss_], in_=ps[:].rearrange("p (a b) -> p a b", a=4), func=AF.Copy),
                             reads=[ob2[bnk]], pwrites=[FT[0]])
                wo = [wload(wb["wmo"][l], h_ * 512, 512) for h_ in range(2)]
                proj_res(FT[0], wo, xts)
                for s in range(NSUB):
                    layer_norm(xts[s], 1)
                    tr32(xts[s], FT[1], s)
                for jp in range(11):
                    wp = wpan.next()
                    ld(wp, wp[:, :, 0:256], wb["wfi"][l][:, jp * 256:(jp + 1) * 256].rearrange("(kc p) c -> p kc c", p=128))
                    ld(wp, wp[:, :, 256:512], wb["wfi"][l][:, FFN_H + jp * 256:FFN_H + (jp + 1) * 256].rearrange("(kc p) c -> p kc c", p=128), partial=True)
                    for jj in range(2):
                        j = 2 * jp + jj
                        pg = pr.next(); pu = pr.next()
                        for kc in range(KC):
                            mm(pg[:, 0:BLK], wp[:, kc, jj * 128:(jj + 1) * 128], FT[1][:, kc, 0:BLK], [wp, FT[1]], pg, kc == 0, start=(kc == 0), stop=(kc == KC - 1))
                        for kc in range(KC):
                            mm(pu[:, 0:BLK], wp[:, kc, 256 + jj * 128:256 + (jj + 1) * 128], FT[1][:, kc, 0:BLK], [wp, FT[1]], pu, kc == 0, start=(kc == 0), stop=(kc == KC - 1))
                        sg = tmp.next()
                        P.op("act", lambda e, pg=pg, sg=sg: e.activation(out=sg[:, 0:BLK], in_=pg[:, 0:BLK], func=AF.Silu), reads=[pg], writes=[sg])
                        P.op("dve", lambda e, pu=pu, sg=sg, j=j: e.tensor_tensor(out=HID[:, j, 0:BLK], in0=sg[:, 0:BLK], in1=pu[:, 0:BLK], op=ALU.mult),
                             reads=[pu, sg], pwrites=[HID])
                for half in range(2):
                    accs = [PS[half * 4 + s] for s in range(NSUB)]
                    for jg in range(2):
                        wf = wfo.next()
                        ld(wf, wf[:], wb["wfo"][l][jg * 1408:(jg + 1) * 1408, half * 512:(half + 1) * 512].rearrange("(j p) c -> p j c", p=128))
                        for s in range(NSUB):
                            for j11 in range(11):
                                j = jg * 11 + j11
                                mm(accs[s][:], HID[:, j, s * 128:(s + 1) * 128], wf[:, j11, :], [HID, wf], accs[s], j == 0, start=(j == 0), stop=(j == FKC - 1))
                    for s in range(NSUB):
                        xt = xts[s]
                        P.op("dve", lambda e, ps=accs[s], xt=xt, half=half: e.scalar_tensor_tensor(
                            out=xt[:, half * 512:(half + 1) * 512], in0=xt[:, half * 512:(half + 1) * 512], scalar=ALPHA, in1=ps[:],
                            op0=ALU.mult, op1=ALU.add), reads=[xt, accs[s]], writes=[xt])
                for s in range(NSUB):
                    layer_norm(xts[s], 2)
                    st(rows(xnext, t0 + s * 128, 128), xts[s], xts[s][:])

            if NB > 1:
                loop(0, NB, p3_body, [ix3])
            else:
                p3_body(0)
            end_phase(T)

        seqs = []
        if "p" in seqs_enabled:
            seqs.append(Seq("p", SP, x_prompt, mem_prompt, y_prompt))
        if "s" in seqs_enabled:
            seqs.append(Seq("s", SS, x_sample, mem_sample, y_sample))
        for seq in seqs:
            xcur = seq.x_in
            lls = list(range(depth)) if layer_list is None else layer_list
            for l in lls:
                xnext = seq.y_out if l == lls[-1] else xs_d[l % 2]
                phase_p1(seq, l, xcur)
                if stop_after == "p1":
                    break
                phase_p2a(seq, l)
                if stop_after == "p2a":
                    break
                phase_p2(seq, l)
                if stop_after == "p2":
                    break
                phase_p3(seq, l, xcur, xnext)
                if stop_after == ("after", seq.name, l):
                    break
                xcur = xnext
        GT.close()
    return nc, dbg_outputs


def host_constants(SMAX):
    c = np.zeros((128, NCONST, 128), np.float32)
    p = np.arange(128)[:, None]
    f = np.arange(128)[None, :]
    same = (p // 64) == (f // 64)
    c[:, C_ID] = (p == f)
    c[:, C_UI] = same & (f >= p)
    c[:, C_LI] = same & (f <= p)
    c[:, C_SL] = same & (f < p)
    c[:, C_SU] = same & (f > p)
    c[:, C_ONE] = 1.0
    c[:, C_SEL0] = (p < 64)
    c[:, C_SEL1] = (p >= 64)
    inv = (10000.0 ** (-np.arange(0, 64, 2, dtype=np.float32) / 64)).astype(np.float32)
    ang = (np.arange(SMAX, dtype=np.float32)[:, None] * inv[None, :]).astype(np.float32)
    cs, sn = np.cos(ang).astype(np.float32), np.sin(ang).astype(np.float32)
    rope = np.concatenate([cs, cs, -sn, sn], axis=1).astype(np.float32)
    q = np.arange(128)[:, None] + 128
    k = np.arange(384)[None, :]
    band = np.abs(q - k) <= 128
    am = np.zeros((128, 4, 384), np.float32)
    for v in range(4):
        ok = band.copy()
        if v in (1, 3):
            ok &= (k >= 128)
        if v in (2, 3):
            ok &= (k < 256)
        am[:, v] = np.where(ok, 0.0, -30000.0)
    sh = np.zeros((128, 9, 128), np.float32)
    for k in range(5):
        sh[:, k] = (p == f + (k - 2))
    for ii, k in enumerate((0, 1, 3, 4)):
        srow = (-2, -1) if k < 2 else (128, 129)
        for hp in range(2):
            sh[hp, 5 + ii] = (srow[hp] == np.arange(128) + (k - 2))
    return c.reshape(128, NCONST * 128), rope, am.reshape(128, 4 * 384), sh.reshape(128, 9 * 128)


_CACHE = {}


def kernel(**inputs):
    SP, SS = inputs["x_prompt"].shape[1], inputs["x_sample"].shape[1]
    NCORE = inputs["x_sample"].shape[0]
    key = (SP, SS)
    if key not in _CACHE:
        _CACHE[key] = build(SP, SS)[0]
    nc = _CACHE[key]
    cm, rope, am, sh = host_constants(max(SP, SS))
    f32 = lambda a: np.ascontiguousarray(np.asarray(a, dtype=np.float32))
    shared = {
        "x_prompt": f32(inputs["x_prompt"]).reshape(SP, D),
        "mem_prompt": f32(inputs["mem_prompt"]).reshape(MEM, D),
        "w_in": f32(inputs["w_in"]),
        "hgrn_lb_logits": f32(inputs["hgrn_lb_logits"]).reshape(1, -1),
        "hgrn_norm_g": f32(inputs["hgrn_norm_g"]),
        "attn_sink": f32(inputs["attn_sink"]),
        "gdn_conv_w": f32(inputs["gdn_conv_w"]).reshape(DEPTH, -1),
        "gdn_a_log": f32(inputs["gdn_a_log"]).reshape(DEPTH, 8),
        "gdn_dt_bias": f32(inputs["gdn_dt_bias"]).reshape(DEPTH, 8),
        "gdn_norm_g": f32(inputs["gdn_norm_g"]),
        "w_branch_a": f32(inputs["w_branch_a"]),
        "w_branch_b": f32(inputs["w_branch_b"]),
        "w_branch_c": f32(inputs["w_branch_c"]),
        "w_mix_out": f32(inputs["w_mix_out"]),
        "w_mem_q": f32(inputs["w_mem_q"]),
        "w_mem_kv": f32(inputs["w_mem_kv"]),
        "w_mem_o": f32(inputs["w_mem_o"]),
        "w_ffn_in": f32(inputs["w_ffn_in"]),
        "w_ffn_out": f32(inputs["w_ffn_out"]),
        "ln_g": f32(inputs["ln_g"]).reshape(DEPTH, -1),
        "ln_b": f32(inputs["ln_b"]).reshape(DEPTH, -1),
        "consts": cm, "rope": rope, "amask": am, "cshift": sh, "zrow": np.zeros((2, 1536), np.float32), "iota": np.arange(128, dtype=np.float32).reshape(128, 1),
    }
    xs = f32(inputs["x_sample"])
    ms = f32(inputs["mem_sample"])
    in_maps = []
    for c in range(NCORE):
        m = dict(shared)
        m["x_sample"] = xs[c]
        m["mem_sample"] = ms[c]
        in_maps.append(m)
    res = run_bass_kernel_spmd(nc, in_maps, core_ids=list(range(NCORE)))
    yp = np.asarray(res.results[0]["y_prompt"], dtype=np.float32).reshape(1, SP, D)
    ysm = np.stack([np.asarray(res.results[c]["y_sample"], dtype=np.float32) for c in range(NCORE)], 0)
    return (yp, ysm)
```

```python
from contextlib import ExitStack

import numpy as np
import concourse.bass as bass
import concourse.mybir as mybir
from concourse.bass_utils import run_bass_kernel_spmd

F32 = mybir.dt.float32
BF16 = mybir.dt.bfloat16
AF = mybir.ActivationFunctionType
ALU = mybir.AluOpType
AX = mybir.AxisListType

ENGS = ("pe", "act", "dve", "pool", "sp")

D = 1024
KC = 8
IN_COLS = 8464
FFN_H = 2816
FKC = 22
MEM = 256
DEPTH = 4
ALPHA = float((2 * DEPTH) ** 0.25)
OFF = dict(AQ=0, AFF=512, AFB=1024, AI=1536, AG=2048, BQ=2560, BK=3072, BV=3200, CQ=3328,
           CG=4864, CB=5376, CA=5384, MG=5392)
C_ID, C_UI, C_LI, C_SL, C_SU, C_ONE, C_SEL0, C_SEL1 = range(8)
NCONST = 8


class DSem:
    __slots__ = ("h", "count", "sw")

    def __init__(self, h):
        self.h = h
        self.count = 0
        self.sw = False


class Buf:
    __slots__ = ("t", "name", "writers", "readers", "dsem", "excl")

    def __init__(self, t, name):
        self.t = t
        self.name = name
        self.excl = False
        self.writers = {}
        self.readers = {}
        self.dsem = None

    def __getitem__(self, k):
        return self.t[k]


class Op:
    __slots__ = ("eng", "fn", "deps", "needs_sig", "val", "idx", "dma", "dsem", "dval", "kind",
                 "dma_waits", "swdma")

    def __init__(self, eng, fn):
        self.eng = eng
        self.fn = fn
        self.deps = []
        self.needs_sig = False
        self.val = 0
        self.idx = 0
        self.dma = False
        self.dsem = None
        self.dval = 0
        self.kind = "op"
        self.dma_waits = None
        self.swdma = False


class Prog:
    def __init__(self, nc, esem, bar, dsems):
        self.nc = nc
        self.esem = esem
        self.bar = bar
        ds_all = [DSem(h) for h in dsems]
        nsw = 2
        for d_ in ds_all[:nsw]:
            d_.sw = True
        self.free_dsems = {"sw": ds_all[:nsw], "hw": ds_all[nsw:]}
        self.all_dsems = ds_all
        self.streams = {e: [] for e in ENGS}
        self.n = {e: 0 for e in ENGS}
        self.sig = {e: 0 for e in ENGS}
        self.seen = {e: {} for e in ENGS}
        self.nbar = 0
        self.bufs = []
        self.nops = 0
        self.pend_w = []
        self.pend_r = []
        self.pend_i = []
        self.swdummy = None

    def buf(self, t, name):
        b = Buf(t, name)
        self.bufs.append(b)
        return b

    def release(self, bufs):
        ids = set(id(b) for b in bufs)
        for b in bufs:
            if b.dsem is not None:
                for kind, d in b.dsem.items():
                    self.free_dsems[kind].append(d)
                b.dsem = None
        self.bufs = [b for b in self.bufs if id(b) not in ids]

    def _flush_pending(self):
        if not self.pend_w and not self.pend_r:
            return
        w, r = self.pend_w, self.pend_r + self.pend_i
        self.pend_w, self.pend_r, self.pend_i = [], [], []
        self.op("pool", lambda e: e.drain(), reads=r, writes=w)

    def _add_dep(self, op, prod):
        if prod is op or prod.swdma:
            return
        if prod.dma:
            key = ("d", id(prod.dsem))
            v = prod.dval
        else:
            if prod.eng == op.eng and not op.dma and op.eng in ("pe", "sp"):
                return
            key = ("e", prod.eng)
            v = prod.idx
        s = self.seen[op.eng]
        if s.get(key, -1) >= v:
            return
        s[key] = v
        prod.needs_sig = True
        op.deps.append(prod)

    def op(self, eng, fn, reads=(), writes=(), pwrites=(), dma_buf=None, swdma=False):
        if not swdma and (self.pend_w or self.pend_r):
            pend = set(id(b) for b in self.pend_w) | set(id(b) for b in self.pend_r) | set(id(b) for b in self.pend_i)
            if any(id(b) in pend for b in tuple(reads) + tuple(writes) + tuple(pwrites)):
                self._flush_pending()
        o = Op(eng, fn)
        o.swdma = swdma
        o.idx = self.n[eng]
        self.n[eng] += 1
        self.nops += 1
        if dma_buf is not None:
            o.dma = True
            kind = "sw" if eng == "pool" else "hw"
            if dma_buf.dsem is None:
                dma_buf.dsem = {}
            if kind not in dma_buf.dsem:
                dma_buf.dsem[kind] = self.free_dsems[kind].pop()
            o.dsem = dma_buf.dsem[kind]
            o.dsem.count += 16
            o.dval = o.dsem.count
        okey = ("d", id(o.dsem)) if o.dma else ("e", eng)
        for b in reads:
            for w in b.writers.values():
                self._add_dep(o, w)
            if b.excl:
                for r in b.readers.values():
                    if r.eng != eng:
                        self._add_dep(o, r)
        for b in writes:
            for w in b.writers.values():
                self._add_dep(o, w)
            for r in b.readers.values():
                self._add_dep(o, r)
        for b in pwrites:
            for r in b.readers.values():
                self._add_dep(o, r)
        for b in reads:
            b.readers[okey] = o
        for b in writes:
            b.writers = {okey: o}
            b.readers = {}
        for b in pwrites:
            b.writers[okey] = o
        self.streams[eng].append(o)
        return o

    def end_section(self):
        self._flush_pending()
        nc = self.nc
        esem = self.esem
        for e in ENGS:
            c = 0
            for o in self.streams[e]:
                if o.kind == "op" and not o.dma and o.needs_sig:
                    c += 1
                    o.val = c
        streams = self.streams
        self.streams = {e: [] for e in ENGS}
        handles = {"sp": nc.sync, "pe": nc.tensor, "act": nc.scalar, "dve": nc.vector, "pool": nc.gpsimd}
        for e in ENGS:
            eng = handles[e]
            for o in streams[e]:
                for p in o.deps:
                    if p.dma:
                        eng.wait_ge(p.dsem.h, p.dval)
                    else:
                        eng.wait_ge(esem[p.eng], p.val)
                ins = o.fn(eng)
                if o.swdma:
                    ins.then_inc(self.swdummy, 16)
                elif o.dma:
                    ins.then_inc(o.dsem.h, 16)
                elif o.needs_sig:
                    ins.then_inc(esem[e], 1)
        used = [ds for ds in self.all_dsems if ds.count > 0]
        for ds in used:
            nc.sync.wait_ge(ds.h, ds.count)
        nc.all_engine_barrier()
        for e in ENGS:
            nc.sync.sem_clear(esem[e])
        for ds in used:
            if not ds.sw:
                nc.sync.sem_clear(ds.h)
                ds.count = 0
        nc.all_engine_barrier()
        self.seen = {e: {} for e in ENGS}
        for bf in self.bufs:
            bf.writers = {}
            bf.readers = {}


class Ring:
    def __init__(self, items):
        self.items = items
        self.i = 0

    def next(self):
        b = self.items[self.i % len(self.items)]
        self.i += 1
        return b


class Seq:
    def __init__(self, name, S, x_in, mem, y_out):
        self.name = name
        self.S = S
        self.x_in = x_in
        self.mem = mem
        self.y_out = y_out


def build(SP, SS, depth=DEPTH, debug=False, stop_after=None, seqs_enabled=("p", "s"), layer_list=None):
    nc = bass.Bass("TRN2", target_bir_lowering=False)
    SMAX = max(SP, SS)
    dbg_outputs = []

    def din(name, shape, dt=F32):
        return nc.dram_tensor(name, list(shape), dt, kind="ExternalInput").ap()

    def dscr(name, shape, dt):
        if debug:
            dbg_outputs.append(name)
            return nc.dram_tensor(name, list(shape), dt, kind="ExternalOutput").ap()
        return nc.dram_tensor(name, list(shape), dt).ap()

    x_prompt = din("x_prompt", [SP, D])
    x_sample = din("x_sample", [SS, D])
    mem_prompt = din("mem_prompt", [MEM, D])
    mem_sample = din("mem_sample", [MEM, D])
    w_in = din("w_in", [DEPTH, D, IN_COLS])
    lb_logits = din("hgrn_lb_logits", [1, 2 * DEPTH * 512])
    hgrn_norm_g = din("hgrn_norm_g", [DEPTH, 128])
    attn_sink = din("attn_sink", [DEPTH, 8])
    gdn_conv_w = din("gdn_conv_w", [DEPTH, 5 * 1536])
    gdn_a_log = din("gdn_a_log", [DEPTH, 8])
    gdn_dt_bias = din("gdn_dt_bias", [DEPTH, 8])
    gdn_norm_g = din("gdn_norm_g", [DEPTH, 128])
    w_branch_a = din("w_branch_a", [DEPTH, 512, D])
    w_branch_b = din("w_branch_b", [DEPTH, 512, D])
    w_branch_c = din("w_branch_c", [DEPTH, 512, D])
    w_mix_out = din("w_mix_out", [DEPTH, D, D])
    w_mem_q = din("w_mem_q", [DEPTH, D, D])
    w_mem_kv = din("w_mem_kv", [DEPTH, D, 2 * D])
    w_mem_o = din("w_mem_o", [DEPTH, D, D])
    w_ffn_in = din("w_ffn_in", [DEPTH, D, 2 * FFN_H])
    w_ffn_out = din("w_ffn_out", [DEPTH, FFN_H, D])
    ln_g = din("ln_g", [DEPTH, 3 * D])
    ln_b = din("ln_b", [DEPTH, 3 * D])
    consts_d = din("consts", [128, NCONST * 128])
    rope_d = din("rope", [SMAX, 128])
    amask_d = din("amask", [128, 4 * 384])
    cshift_d = din("cshift", [128, 9 * 128])
    zrow_d = din("zrow", [2, 1536])
    iota_d = din("iota", [128, 1])

    y_prompt = nc.dram_tensor("y_prompt", [SP, D], F32, kind="ExternalOutput").ap()
    y_sample = nc.dram_tensor("y_sample", [SS, D], F32, kind="ExternalOutput").ap()

    wb = {}
    wsrc = dict(w_in=w_in, wba=w_branch_a, wbb=w_branch_b, wbc=w_branch_c, wmix=w_mix_out,
                wmq=w_mem_q, wkv=w_mem_kv, wmo=w_mem_o, wfi=w_ffn_in, wfo=w_ffn_out)
    for k, v in wsrc.items():
        wb[k] = dscr(k + "_bf", list(v.shape), BF16) if (debug and k == "w_in") else nc.dram_tensor(k + "_bf", list(v.shape), BF16).ap()
    S_ = SMAX
    hq = dscr("s_hq", [S_, 512], BF16)
    hlf = [dscr(f"s_hlf{i}", [S_, 512], F32) for i in range(2)]
    hk = [dscr(f"s_hk{i}", [S_, 512], BF16) for i in range(2)]
    hv = dscr("s_hv", [S_, 512], BF16)
    ga_d = dscr("s_ga", [S_, 512], BF16)
    bq_d = dscr("s_bq", [S_, 512], BF16)
    bkv_d = dscr("s_bkv", [S_, 256], BF16)
    cq_d = [dscr(f"s_cq{i}", [S_, 512], F32) for i in range(3)]
    gcg_d = dscr("s_gcg", [S_, 512], BF16)
    bg_d = dscr("s_bg", [S_, 16], F32)
    mg_d = [dscr(f"s_mg{i}", [S_, 512], BF16) for i in range(6)]
    gq_d = dscr("s_gq", [S_, 512], F32)
    gk_d = dscr("s_gk", [S_, 512], F32)
    gv_d = dscr("s_gv", [S_, 512], BF16)
    ob_d = dscr("s_ob", [S_, 512], BF16)
    oa_d = [dscr(f"s_oa{i}", [S_, 512], F32) for i in range(2)]
    oc_d = [dscr(f"s_oc{i}", [S_, 512], F32) for i in range(2)]
    xs_d = [dscr("s_xa", [S_, D], F32), dscr("s_xb", [S_, D], F32)]

    uid = [0]

    with ExitStack() as gs:
        esem = {e: gs.enter_context(nc.semaphore("s_" + e)) for e in ENGS}
        bar = gs.enter_context(nc.semaphore("bar"))
        dsems = [gs.enter_context(nc.semaphore(f"d{i}")) for i in range(56)]
        P = Prog(nc, esem, bar, dsems)
        swpool = [gs.enter_context(nc.semaphore(f"swd{i}")) for i in range(36)]
        P.swdummy = swpool[0]

        class Tiles:
            def __init__(self):
                self.es = ExitStack()
                self.bufs = []

            def sb(self, name, shape, dt):
                uid[0] += 1
                t = self.es.enter_context(nc.sbuf_tensor(f"{name}_{uid[0]}", list(shape), dt))
                b = P.buf(t, name)
                self.bufs.append(b)
                return b

            def ring(self, name, shape, dt, n):
                return Ring([self.sb(f"{name}{i}", shape, dt) for i in range(n)])

            def close(self):
                P.release(self.bufs)
                self.es.close()

        GT = Tiles()
        PSALL = gs.enter_context(nc.psum_tensor("psall", [128, 8 * 512], F32))
        PS = [P.buf(PSALL[:, i * 512:(i + 1) * 512], f"ps{i}") for i in range(8)]
        for b_ in PS:
            b_.excl = True
        PSALLB = PSALL.bitcast(BF16)
        CSH = GT.sb("CSH", [128, 9, 128], BF16)
        CSHF = GT.sb("CSHF", [128, 9, 128], F32)
        CF = GT.sb("CF", [128, NCONST, 128], F32)
        IDB = GT.sb("IDB", [128, 128], BF16)
        EPS = GT.sb("EPS", [128, 2], F32)
        IOTA = GT.sb("IOTA", [128, 1], F32)
        ddummy = P.buf(None, "wcast")

        def _idma(sb_buf, sb_ap, dyn, is_load):
            nsub = dyn.n // 128 if dyn.multi else 1
            for s_ in range(nsub):
                if dyn.multi:
                    sap = sb_ap[:, s_, :] if nsub > 1 or len(sb_ap.shape) == 3 else sb_ap
                else:
                    sap = sb_ap
                icol = dyn.row.idx.col(dyn.row.off + s_ * 128)
                npart = sap.shape[0]
                ioff = bass.IndirectOffsetOnAxis(ap=icol[0:npart, :], axis=0)
                dr = dyn.dram()
                if is_load:
                    P.op("pool", lambda e, sap=sap, dr=dr, ioff=ioff: e.indirect_dma_start(out=sap, out_offset=None, in_=dr, in_offset=ioff),
                         reads=[dyn.row.idx.Ibuf], pwrites=[sb_buf], swdma=True)
                else:
                    P.op("pool", lambda e, sap=sap, dr=dr, ioff=ioff: e.indirect_dma_start(out=dr, out_offset=ioff, in_=sap, in_offset=None),
                         reads=[dyn.row.idx.Ibuf, sb_buf], swdma=True)
            if not any(b is dyn.row.idx.Ibuf for b in P.pend_i):
                P.pend_i.append(dyn.row.idx.Ibuf)
            if is_load:
                if not any(b is sb_buf for b in P.pend_w):
                    P.pend_w.append(sb_buf)
            else:
                if not any(b is sb_buf for b in P.pend_r):
                    P.pend_r.append(sb_buf)

        def ld(dst, dst_ap, src_ap, partial=False, eng="sp"):
            if isinstance(src_ap, DynAP):
                _idma(dst, dst_ap, src_ap, True)
                return
            if partial:
                P.op(eng, lambda e: e.dma_start(out=dst_ap, in_=src_ap), pwrites=[dst], dma_buf=dst)
            else:
                P.op(eng, lambda e: e.dma_start(out=dst_ap, in_=src_ap), writes=[dst], dma_buf=dst)

        def st(dst_ap, src, src_ap, eng="sp"):
            if isinstance(dst_ap, DynAP):
                _idma(src, src_ap, dst_ap, False)
                return
            P.op(eng, lambda e: e.dma_start(out=dst_ap, in_=src_ap), reads=[src], dma_buf=src)

        def end_phase(tiles):
            P.end_section()
            if tiles is not None:
                tiles.close()

        class DynRow:
            def __init__(self, idx, off=0):
                self.idx = idx
                self.off = off

            def __add__(self, k):
                return DynRow(self.idx, self.off + k)

            def __sub__(self, k):
                return DynRow(self.idx, self.off - k)

        class DynAP:
            def __init__(self, ap, row, n, c_lo=None, c_hi=None, multi=False):
                self.ap, self.row, self.n, self.c_lo, self.c_hi, self.multi = ap, row, n, c_lo, c_hi, multi

            def __getitem__(self, key):
                cs = key[1]
                return DynAP(self.ap, self.row, self.n, cs.start, cs.stop, self.multi)

            def rearrange(self, *_a, **_k):
                return DynAP(self.ap, self.row, self.n, self.c_lo, self.c_hi, True)

            def dram(self):
                return self.ap if self.c_lo is None else self.ap[:, self.c_lo:self.c_hi]

        class RowIdx:
            def __init__(self, T, name, offs, base, step):
                self.offs = list(offs)
                self.step = step
                n = len(self.offs)
                self.F = T.sb(name + "F", [128, n], F32)
                self.I = T.sb(name + "I", [128, n], mybir.dt.int32)
                self.Ibuf = self.I
                for j, o_ in enumerate(self.offs):
                    P.op("dve", lambda e, j=j, o_=o_: e.tensor_scalar(out=self.F[:, j:j + 1], in0=IOTA[:, 0:1], scalar1=float(base + o_), scalar2=None, op0=ALU.add),
                         reads=[IOTA], writes=[self.F] if j == 0 else (), pwrites=() if j == 0 else [self.F])
                P.op("dve", lambda e: e.tensor_copy(out=self.I[:], in_=self.F[:]), reads=[self.F], writes=[self.I])

            def col(self, off):
                j = self.offs.index(off)
                return self.I[:, j:j + 1]

            def advance(self):
                P.op("dve", lambda e: e.tensor_scalar(out=self.F[:], in0=self.F[:], scalar1=float(self.step), scalar2=None, op0=ALU.add),
                     reads=[self.F], writes=[self.F])
                P.op("dve", lambda e: e.tensor_copy(out=self.I[:], in_=self.F[:]), reads=[self.F, self.I], writes=[self.I])

        def rows(ap, r0, n):
            if isinstance(r0, DynRow):
                return DynAP(ap, r0, n)
            return ap[r0:r0 + n]

        loop_regs = nc.alloc_registers("mk_loop_i", engines=mybir.ALL_ENGINES)
        loop_cnt = [0]

        def loop(n0, n1, body, idxs=()):
            if n1 - n0 <= 0:
                return
            P.end_section()
            loop_cnt[0] += 1
            P.swdummy = swpool[loop_cnt[0] % len(swpool)]
            ls, le = f"mk_loop{loop_cnt[0]}_loop", f"mk_loop{loop_cnt[0]}_end"
            engines = mybir.ALL_ENGINES
            nc.regs_mov(loop_regs, n0)
            nc.br(ls, engines=engines)
            with nc.body(ls, valid_engines=engines):
                body()
                for ix in idxs:
                    ix.advance()
                P.end_section()
                nc.regs_alu(loop_regs, loop_regs, 1, op=ALU.add)
                nc.br_lt(loop_regs, n1, on_true=ls, on_false=le, engines=engines)
            nc.switch_bb(le)

        ld(CF, CF[:].rearrange("p a b -> p (a b)"), consts_d[:, :])
        P.op("dve", lambda e: e.tensor_copy(out=IDB[:], in_=CF[:, C_ID, :]), reads=[CF], writes=[IDB])
        ld(IOTA, IOTA[:], iota_d[:, :])
        ld(CSHF, CSHF[:].rearrange("p a b -> p (a b)"), cshift_d[:, :])
        P.op("dve", lambda e: e.tensor_copy(out=CSH[:], in_=CSHF[:]), reads=[CSHF], writes=[CSH])
        P.op("pool", lambda e: e.memset(EPS[:, 0:1], 1e-6), writes=[EPS])
        P.op("pool", lambda e: e.memset(EPS[:, 1:2], 1e-5), pwrites=[EPS])
        for k, v in wsrc.items():
            L, R, Cc = v.shape
            for l in range(depth):
                for r0 in range(0, R, 128):
                    P.op("pool", lambda e, k=k, v=v, l=l, r0=r0: e.dma_start(
                        out=wb[k][l, r0:r0 + 128, :], in_=v[l, r0:r0 + 128, :]), dma_buf=ddummy)
        end_phase(None)

        ident = CF[:, C_ID, :]

        def phase_p1(seq, l, xcur):
            S = seq.S
            BLK = min(512, S)
            NSUB = BLK // 128
            NB = S // BLK
            T = Tiles()
            xin = T.ring("xin", [128, D], F32, 2 * NSUB)
            xT = T.ring("xT", [128, KC, BLK], BF16, 2)
            wpan = T.ring("wpan", [128, KC, 512], BF16, 3)
            stg16 = T.ring("stg16", [128, NSUB, 512], BF16, 4)
            stg32 = T.ring("stg32", [128, NSUB, 512], F32, 3)
            tmp = T.ring("tmp", [128, 512], F32, 8)
            ropet = T.ring("ropet", [128, NSUB, 128], F32, 2)
            LG = T.sb("LG", [128, 2, DEPTH, 512], F32)
            MX = T.sb("MX", [128, 2, 512], F32)
            SSUM = T.sb("SSUM", [128, 2, 512], F32)
            LB = T.sb("LB", [128, 2, 512], F32)
            OML = T.sb("OML", [128, 2, 512], F32)
            DTB = T.sb("DTB", [128, 8], F32)
            NEGA = T.sb("NEGA", [128, 8], F32)
            pst = Ring(PS[0:2])
            psm = Ring(PS[2:8])

            ld(LG, LG[:].rearrange("p a b c -> p (a b c)"), lb_logits.partition_broadcast(128))
            P.op("dve", lambda e: e.tensor_tensor(out=MX[:], in0=LG[:, :, 0, :], in1=LG[:, :, 1, :], op=ALU.max),
                 reads=[LG], writes=[MX])
            for m in (2, 3):
                P.op("dve", lambda e, m=m: e.tensor_tensor(out=MX[:], in0=MX[:], in1=LG[:, :, m, :], op=ALU.max),
                     reads=[LG, MX], writes=[MX])
            for m in range(DEPTH):
                P.op("dve", lambda e, m=m: e.tensor_tensor(out=LG[:, :, m, :], in0=LG[:, :, m, :], in1=MX[:],
                                                         op=ALU.subtract), reads=[LG, MX], writes=[LG])
            P.op("act", lambda e: e.activation(out=LG[:], in_=LG[:], func=AF.Exp), reads=[LG], writes=[LG])
            P.op("dve", lambda e: e.tensor_tensor(out=SSUM[:], in0=LG[:, :, 0, :], in1=LG[:, :, 1, :], op=ALU.add),
                 reads=[LG], writes=[SSUM])
            for m in (2, 3):
                P.op("dve", lambda e, m=m: e.tensor_tensor(out=SSUM[:], in0=SSUM[:], in1=LG[:, :, m, :], op=ALU.add),
                     reads=[LG, SSUM], writes=[SSUM])
            P.op("dve", lambda e: e.reciprocal(out=SSUM[:], in_=SSUM[:]), reads=[SSUM], writes=[SSUM])
            if l == 0:
                P.op("pool", lambda e: e.memset(LB[:], 0.0), writes=[LB])
            else:
                P.op("dve", lambda e: e.tensor_copy(out=LB[:], in_=LG[:, :, 1, :]), reads=[LG], writes=[LB])
                for m in range(2, l + 1):
                    P.op("dve", lambda e, m=m: e.tensor_tensor(out=LB[:], in0=LB[:], in1=LG[:, :, m, :], op=ALU.add),
                         reads=[LG, LB], writes=[LB])
                P.op("dve", lambda e: e.tensor_tensor(out=LB[:], in0=LB[:], in1=SSUM[:], op=ALU.mult),
                     reads=[LB, SSUM], writes=[LB])
            P.op("dve", lambda e: e.tensor_scalar(out=OML[:], in0=LB[:], scalar1=-1.0, scalar2=1.0,
                                                  op0=ALU.mult, op1=ALU.add), reads=[LB], writes=[OML])
            ld(DTB, DTB[:], gdn_dt_bias[l:l + 1, :].partition_broadcast(128))
            ld(NEGA, NEGA[:], gdn_a_log[l:l + 1, :].partition_broadcast(128))
            P.op("act", lambda e: e.activation(out=NEGA[:], in_=NEGA[:], func=AF.Exp), reads=[NEGA], writes=[NEGA])
            P.op("dve", lambda e: e.tensor_scalar(out=NEGA[:], in0=NEGA[:], scalar1=-1.0, scalar2=None, op0=ALU.mult),
                 reads=[NEGA], writes=[NEGA])

            panels = [("aq", OFF["AQ"], 512), ("aff", OFF["AFF"], 512), ("afb", OFF["AFB"], 512),
                      ("ai", OFF["AI"], 512), ("ag", OFF["AG"], 512), ("bq", OFF["BQ"], 512),
                      ("bkv", OFF["BK"], 256), ("cq0", OFF["CQ"], 512), ("cq1", OFF["CQ"] + 512, 512),
                      ("cq2", OFF["CQ"] + 1024, 512), ("cg", OFF["CG"], 512), ("bg", OFF["CB"], 16)]
            for i in range(6):
                panels.append((f"mg{i}", OFF["MG"] + 512 * i, 512))

            ix1 = RowIdx(T, "ix1", [s_ * 128 for s_ in range(NSUB)], 0, BLK)

            def p1_body(b=None):
                t0 = DynRow(ix1) if b is None else b * BLK
                xTt = xT.next()
                rt = ropet.next()
                ld(rt, rt[:], rows(rope_d, t0, BLK).rearrange("(s p) c -> p s c", p=128))
                for s in range(NSUB):
                    xt = xin.next()
                    ld(xt, xt[:], rows(xcur, t0 + s * 128, 128))
                    for half in range(2):
                        ps = pst.next()
                        for j in range(4):
                            kc = half * 4 + j
                            P.op("pe", lambda e, ps=ps, j=j, kc=kc, xt=xt: e.transpose(
                                out=ps[:, j * 128:(j + 1) * 128], in_=xt[:, kc * 128:(kc + 1) * 128], identity=ident),
                                reads=[xt, CF], writes=[ps] if j == 0 else (), pwrites=() if j == 0 else [ps])
                        eng = "act" if half == 0 else "dve"
                        if eng == "act":
                            P.op("act", lambda e, ps=ps, half=half, s=s, xTt=xTt: e.activation(
                                out=xTt[:, half * 4:half * 4 + 4, s * 128:(s + 1) * 128],
                                in_=ps[:].rearrange("p (a b) -> p a b", a=4), func=AF.Copy),
                                reads=[ps], pwrites=[xTt])
                        else:
                            P.op("dve", lambda e, ps=ps, half=half, s=s, xTt=xTt: e.tensor_copy(
                                out=xTt[:, half * 4:half * 4 + 4, s * 128:(s + 1) * 128],
                                in_=ps[:].rearrange("p (a b) -> p a b", a=4)),
                                reads=[ps], pwrites=[xTt])
                for (kind, c0, ncol) in panels:
                    wp = wpan.next()
                    ld(wp, wp[:, :, 0:ncol], wb["w_in"][l, :, c0:c0 + ncol].rearrange("(kc p) c -> p kc c", p=128))
                    if kind in ("aq", "ai", "ag", "bq", "cg") or kind.startswith("mg"):
                        so = stg16.next()
                    elif kind in ("aff", "afb"):
                        so = stg32.next()
                        so2 = stg16.next()
                    elif kind == "bkv":
                        so = stg16.next()
                    elif kind.startswith("cq") or kind == "bg":
                        so = stg32.next()
                    for s in range(NSUB):
                        ps = psm.next()
                        for kc in range(KC):
                            P.op("pe", lambda e, ps=ps, kc=kc, s=s, wp=wp, ncol=ncol, xTt=xTt: e.matmul(
                                ps[:, 0:ncol], lhsT=xTt[:, kc, s * 128:(s + 1) * 128], rhs=wp[:, kc, 0:ncol],
                                start=(kc == 0), stop=(kc == KC - 1)),
                                reads=[xTt, wp], writes=[ps] if kc == 0 else (), pwrites=() if kc == 0 else [ps])
                        if kind in ("aq", "ag", "cg"):
                            P.op("act", lambda e, ps=ps, so=so, s=s: e.activation(out=so[:, s, :], in_=ps[:], func=AF.Silu),
                                 reads=[ps], pwrites=[so])
                        elif kind.startswith("mg"):
                            P.op("act", lambda e, ps=ps, so=so, s=s: e.activation(out=so[:, s, :], in_=ps[:], func=AF.Sigmoid),
                                 reads=[ps], pwrites=[so])
                        elif kind == "ai":
                            P.op("dve", lambda e, ps=ps, so=so, s=s: e.tensor_copy(out=so[:, s, :], in_=ps[:]),
                                 reads=[ps], pwrites=[so])
                        elif kind.startswith("cq"):
                            P.op("act", lambda e, ps=ps, so=so, s=s: e.activation(out=so[:, s, :], in_=ps[:], func=AF.Copy),
                                 reads=[ps], pwrites=[so])
                        elif kind in ("aff", "afb"):
                            di = 0 if kind == "aff" else 1
                            sg = tmp.next()
                            ff = tmp.next()
                            P.op("act", lambda e, ps=ps, sg=sg: e.activation(out=sg[:], in_=ps[:], func=AF.Sigmoid),
                                 reads=[ps], writes=[sg])
                            P.op("dve", lambda e, sg=sg, di=di: e.tensor_tensor(out=sg[:], in0=sg[:], in1=OML[:, di, :], op=ALU.mult),
                                 reads=[sg, OML], writes=[sg])
                            P.op("pool", lambda e, sg=sg, ff=ff, di=di: e.tensor_tensor(out=ff[:], in0=sg[:], in1=LB[:, di, :], op=ALU.add),
                                 reads=[sg, LB], writes=[ff])
                            P.op("act", lambda e, ff=ff, so=so, s=s: e.activation(out=so[:, s, :], in_=ff[:], func=AF.Ln),
                                 reads=[ff], pwrites=[so])
                            P.op("pool", lambda e, sg=sg, so2=so2, s=s, di=di: e.tensor_tensor(
                                out=so2[:, s, :], in0=OML[:, di, :], in1=sg[:], op=ALU.subtract),
                                reads=[sg, OML], pwrites=[so2])
                        elif kind in ("bq", "bkv"):
                            nh = 8 if kind == "bq" else 2
                            w = nh * 64
                            ta = tmp.next()
                            tb = tmp.next()
                            cc = rt[:, s, 0:64].unsqueeze(1).to_broadcast([128, nh, 64])
                            nsin = rt[:, s, 64:96].unsqueeze(1).to_broadcast([128, nh, 32])
                            psin = rt[:, s, 96:128].unsqueeze(1).to_broadcast([128, nh, 32])
                            P.op("dve", lambda e, ps=ps, ta=ta, cc=cc, nh=nh, w=w: e.tensor_tensor(
                                out=ta[:, 0:w].rearrange("p (h d) -> p h d", h=nh),
                                in0=ps[:, 0:w].rearrange("p (h d) -> p h d", h=nh), in1=cc, op=ALU.mult),
                                reads=[ps, rt], writes=[ta])
                            P.op("dve", lambda e, ps=ps, tb=tb, nsin=nsin, nh=nh, w=w: e.tensor_tensor(
                                out=tb[:, 0:w].rearrange("p (h d) -> p h d", h=nh)[:, :, 0:32],
                                in0=ps[:, 0:w].rearrange("p (h d) -> p h d", h=nh)[:, :, 32:64], in1=nsin, op=ALU.mult),
                                reads=[ps, rt], writes=[tb])
                            P.op("dve", lambda e, ps=ps, tb=tb, psin=psin, nh=nh, w=w: e.tensor_tensor(
                                out=tb[:, 0:w].rearrange("p (h d) -> p h d", h=nh)[:, :, 32:64],
                                in0=ps[:, 0:w].rearrange("p (h d) -> p h d", h=nh)[:, :, 0:32], in1=psin, op=ALU.mult),
                                reads=[ps, rt], pwrites=[tb])
                            P.op("pool", lambda e, ta=ta, tb=tb, so=so, s=s, w=w: e.tensor_tensor(
                                out=so[:, s, 0:w], in0=ta[:, 0:w], in1=tb[:, 0:w], op=ALU.add),
                                reads=[ta, tb], pwrites=[so])
                            if kind == "bkv":
                                P.op("act", lambda e, ps=ps, so=so, s=s: e.activation(
                                    out=so[:, s, 128:256], in_=ps[:, 128:256], func=AF.Copy), reads=[ps], pwrites=[so])
                        elif kind == "bg":
                            ta = tmp.next()
                            P.op("act", lambda e, ps=ps, so=so, s=s: e.activation(out=so[:, s, 0:8], in_=ps[:, 0:8], func=AF.Sigmoid),
                                 reads=[ps], pwrites=[so])
                            P.op("dve", lambda e, ps=ps, ta=ta: e.tensor_tensor(out=ta[:, 0:8], in0=ps[:, 8:16], in1=DTB[:], op=ALU.add),
                                 reads=[ps, DTB], writes=[ta])
                            P.op("act", lambda e, ta=ta: e.activation(out=ta[:, 0:8], in_=ta[:, 0:8], func=AF.Exp),
                                 reads=[ta], writes=[ta])
                            P.op("act", lambda e, ta=ta: e.activation(out=ta[:, 0:8], in_=ta[:, 0:8], func=AF.Ln, bias=CF[:, C_ONE, 0:1]),
                                 reads=[ta, CF], writes=[ta])
                            P.op("dve", lambda e, ta=ta, so=so, s=s: e.tensor_tensor(out=so[:, s, 8:16], in0=ta[:, 0:8], in1=NEGA[:], op=ALU.mult),
                                 reads=[ta, NEGA], pwrites=[so])

                    def dst(ap2d, c_lo, c_hi):
                        return rows(ap2d, t0, BLK).rearrange("(s p) c -> p s c", p=128)

                    if kind == "aq":
                        st(dst(hq, 0, 512), so, so[:])
                    elif kind in ("aff", "afb"):
                        di = 0 if kind == "aff" else 1
                        st(dst(hlf[di], 0, 512), so, so[:])
                        st(dst(hk[di], 0, 512), so2, so2[:])
                    elif kind == "ai":
                        st(dst(hv, 0, 512), so, so[:])
                    elif kind == "ag":
                        st(dst(ga_d, 0, 512), so, so[:])
                    elif kind == "bq":
                        st(dst(bq_d, 0, 512), so, so[:])
                    elif kind == "bkv":
                        st(dst(bkv_d, 0, 256), so, so[:, :, 0:256])
                    elif kind.startswith("cq"):
                        i = int(kind[2])
                        st(dst(cq_d[i], 0, 512), so, so[:])
                    elif kind == "cg":
                        st(dst(gcg_d, 0, 512), so, so[:])
                    elif kind == "bg":
                        st(dst(bg_d, 0, 16), so, so[:, :, 0:16])
                    elif kind.startswith("mg"):
                        i = int(kind[2])
                        st(dst(mg_d[i], 0, 512), so, so[:])

            if NB > 1:
                loop(0, NB, p1_body, [ix1])
            else:
                p1_body(0)
            end_phase(T)


        def psb(i, lo, hi):
            return PSALLB[:, i * 1024 + lo:i * 1024 + hi]

        def phase_p2a(seq, l):
            S = seq.S
            NT = S // 128
            T = Tiles()
            CW = T.sb("CW", [128, 5, 1536], F32)
            AM = T.sb("AM", [128, 4, 384], F32)
            SINK = T.sb("SINK", [128, 8], F32)
            NSINK = T.sb("NSINK", [128, 8], F32)
            cqm = T.ring("cqm", [128, 1536], F32, 2)
            halo = T.ring("halo", [2, 1536], F32, 2)
            xw = T.ring("xw", [128, 5, 1536], BF16, 2)
            hw = T.ring("hw", [2, 2, 1536], BF16, 2)
            act32 = T.ring("a32", [128, 512], F32, 4)
            sq32 = T.ring("sq32", [128, 512], F32, 2)
            st32 = T.ring("st32", [128, 512], F32, 4)
            st16 = T.ring("st16", [128, 512], BF16, 4)
            small = T.ring("small", [128, 16], F32, 12)
            qt_in = T.ring("qtin", [128, 512], BF16, 2)
            kvt = T.ring("kvt", [128, 256], BF16, 6)
            KD = T.ring("KD", [128, 2, 128], BF16, 3)
            KT = T.ring("KT", [128, 2, 384], BF16, 2)
            QT = T.ring("QT", [128, 4, 128], BF16, 2)
            SC = T.ring("SC", [128, 4, 384], F32, 2)
            PB = T.ring("PB", [128, 4, 384], BF16, 2)
            PT = T.ring("PT", [128, 12, 128], BF16, 2)
            ld(CW, CW[:].rearrange("p a b -> p (a b)"), gdn_conv_w[l:l + 1, :].partition_broadcast(128))
            ld(AM, AM[:].rearrange("p a b -> p (a b)"), amask_d[:, :])
            ld(SINK, SINK[:], attn_sink[l:l + 1, :].partition_broadcast(128))
            P.op("dve", lambda e: e.tensor_scalar(out=NSINK[:], in0=SINK[:], scalar1=-1.0, scalar2=None, op0=ALU.mult),
                 reads=[SINK], writes=[NSINK])
            convps = Ring(PS[7:8])
            trps = Ring([4, 5])
            pvps = Ring(PS[6:7])
            ix2 = RowIdx(T, "ix2", [0, -128, 128, -2], 128, 128)

            def p2a_body(t, is_first, is_last):
                t0 = DynRow(ix2) if t is None else t * 128
                cm_ = cqm.next()
                hlp = halo.next()
                hln = halo.next()
                for j3 in range(3):
                    cs3 = slice(j3 * 512, (j3 + 1) * 512)
                    ld(cm_, cm_[:, cs3], rows(cq_d[j3], t0, 128), partial=(j3 > 0))
                    ld(hlp, hlp[0:2, cs3], rows(cq_d[j3], t0 - 2, 2) if not is_first else zrow_d[:, cs3], partial=(j3 > 0))
                    ld(hln, hln[0:2, cs3], rows(cq_d[j3], t0 + 128, 2) if not is_last else zrow_d[:, cs3], partial=(j3 > 0))
                xw_ = xw.next()
                hwp = hw.next()
                hwn = hw.next()
                for k in range(5):
                    eng = "pool" if k % 2 == 0 else "dve"
                    P.op(eng, lambda e, k=k, xw_=xw_, cm_=cm_: e.tensor_tensor(out=xw_[:, k, :], in0=cm_[:], in1=CW[:, k, :], op=ALU.mult),
                         reads=[cm_, CW], pwrites=[xw_])
                for k in (0, 1, 3, 4):
                    hsrc, hdst = (hlp, hwp) if k < 2 else (hln, hwn)
                    P.op("pool", lambda e, k=k, hsrc=hsrc, hdst=hdst: e.tensor_tensor(out=hdst[:, k % 3, :], in0=hsrc[:], in1=CW[0:2, k, :], op=ALU.mult),
                         reads=[hsrc, CW], pwrites=[hdst])
                outs3 = []
                for j in range(3):
                    ps = convps.next()
                    first = True
                    for k in range(5):
                        P.op("pe", lambda e, ps=ps, k=k, j=j, xw_=xw_, first=first: e.matmul(
                            ps[:], lhsT=CSH[:, k, :], rhs=xw_[:, k, j * 512:(j + 1) * 512], start=first, stop=False),
                            reads=[CSH, xw_], writes=[ps] if first else (), pwrites=() if first else [ps])
                        first = False
                    for ii, k in enumerate((0, 1, 3, 4)):
                        hsrc = hwp if k < 2 else hwn
                        P.op("pe", lambda e, ps=ps, k=k, ii=ii, j=j, hsrc=hsrc: e.matmul(
                            ps[:], lhsT=CSH[0:2, 5 + ii, :], rhs=hsrc[:, k % 3, j * 512:(j + 1) * 512], start=False, stop=(ii == 3)),
                            reads=[CSH, hsrc], pwrites=[ps])
                    if j < 2:
                        a_ = act32.next()
                        P.op("act", lambda e, ps=ps, a_=a_: e.activation(out=a_[:], in_=ps[:], func=AF.Silu), reads=[ps], writes=[a_])
                        outs3.append(a_)
                    else:
                        so = st16.next()
                        P.op("act", lambda e, ps=ps, so=so: e.activation(out=so[:], in_=ps[:], func=AF.Silu), reads=[ps], writes=[so])
                        st(rows(gv_d, t0, 128), so, so[:])
                for j, a_ in enumerate(outs3):
                    sq = sq32.next()
                    sm = small.next()
                    so = st32.next()
                    P.op("pool", lambda e, sq=sq, a_=a_: e.tensor_tensor(out=sq[:], in0=a_[:], in1=a_[:], op=ALU.mult), reads=[a_], writes=[sq])
                    P.op("dve", lambda e, sq=sq, sm=sm: e.tensor_reduce(out=sm[:, 0:4], in_=sq[:].rearrange("p (h d) -> p h d", h=4), axis=AX.X, op=ALU.add),
                         reads=[sq], writes=[sm])
                    P.op("act", lambda e, sm=sm: e.activation(out=sm[:, 4:8], in_=sm[:, 0:4], func=AF.Sqrt, bias=EPS[:, 0:1]),
                         reads=[sm, EPS], writes=[sm])
                    P.op("dve", lambda e, sm=sm: e.reciprocal(out=sm[:, 8:12], in_=sm[:, 4:8]), reads=[sm], writes=[sm])
                    if j == 0:
                        P.op("dve", lambda e, sm=sm: e.tensor_scalar(out=sm[:, 8:12], in0=sm[:, 8:12], scalar1=float(128 ** -0.5), scalar2=None, op0=ALU.mult),
                             reads=[sm], writes=[sm])
                    P.op("dve", lambda e, a_=a_, sm=sm, so=so: e.tensor_tensor(
                        out=so[:].rearrange("p (h d) -> p h d", h=4), in0=a_[:].rearrange("p (h d) -> p h d", h=4),
                        in1=sm[:, 8:12].unsqueeze(2).to_broadcast([128, 4, 128]), op=ALU.mult), reads=[a_, sm], writes=[so])
                    st(rows(gq_d if j == 0 else gk_d, t0, 128), so, so[:])

                variant = 0
                if is_first and is_last:
                    variant = 3
                elif is_first:
                    variant = 1
                elif is_last:
                    variant = 2
                qi = qt_in.next()
                ld(qi, qi[:], rows(bq_d, t0, 128))
                kvs = []
                for j in range(3):
                    kv_ = kvt.next()
                    if (j == 0 and is_first) or (j == 2 and is_last):
                        P.op("pool", lambda e, kv_=kv_: e.memset(kv_[:], 0.0), writes=[kv_])
                    else:
                        ld(kv_, kv_[:], rows(bkv_d, t0 + (j - 1) * 128, 128))
                    kvs.append(kv_)
                KT_ = KT.next()
                bi = trps.next()
                for j in range(3):
                    kd = KD.next()
                    P.op("pool", lambda e, kd=kd, kv_=kvs[j]: e.tensor_copy(
                        out=kd[:].rearrange("p g (r d) -> p g r d", r=2),
                        in_=kv_[:, 0:128].rearrange("p (g d) -> p g d", g=2).unsqueeze(2).to_broadcast([128, 2, 2, 64])),
                        reads=[kvs[j]], writes=[kd])
                    for g in range(2):
                        idx = j * 2 + g
                        P.op("pe", lambda e, kd=kd, g=g, idx=idx, bi=bi: e.transpose(
                            out=psb(bi, idx * 128, (idx + 1) * 128), in_=kd[:, g, :], identity=IDB[:]),
                            reads=[kd, IDB], writes=[PS[bi]] if idx == 0 else (), pwrites=() if idx == 0 else [PS[bi]])
                P.op("act", lambda e, bi=bi, KT_=KT_: e.activation(
                    out=KT_[:].rearrange("p g (j k) -> p j g k", j=3),
                    in_=psb(bi, 0, 768).rearrange("p (j g k) -> p j g k", j=3, g=2), func=AF.Copy),
                    reads=[PS[bi]], writes=[KT_])
                QT_ = QT.next()
                bi = trps.next()
                for pr in range(4):
                    P.op("pe", lambda e, pr=pr, bi=bi, qi=qi: e.transpose(
                        out=psb(bi, pr * 128, (pr + 1) * 128), in_=qi[:, pr * 128:(pr + 1) * 128], identity=IDB[:]),
                        reads=[qi, IDB], writes=[PS[bi]] if pr == 0 else (), pwrites=() if pr == 0 else [PS[bi]])
                P.op("dve", lambda e, bi=bi, QT_=QT_: e.tensor_copy(out=QT_[:].rearrange("p a b -> p (a b)"), in_=psb(bi, 0, 512)),
                     reads=[PS[bi]], writes=[QT_])
                ob_t = st16.next()
                for g in range(2):
                    for hh4 in range(4):
                        h = g * 4 + hh4
                        pr, half = h // 2, h % 2
                        P.op("pe", lambda e, hh4=hh4, pr=pr, half=half, g=g, QT_=QT_, KT_=KT_: e.matmul(
                            PSALL[:, hh4 * 512:hh4 * 512 + 384], lhsT=QT_[half * 64:(half + 1) * 64, pr, :],
                            rhs=KT_[half * 64:(half + 1) * 64, g, :], start=True, stop=True),
                            reads=[QT_, KT_], writes=[PS[hh4]])
                    sc = SC.next()
                    P.op("dve", lambda e, sc=sc, variant=variant: e.tensor_tensor(
                        out=sc[:], in0=PSALL[:, 0:2048].rearrange("p (h k) -> p h k", h=4)[:, :, 0:384],
                        in1=AM[:, variant, :].unsqueeze(1).to_broadcast([128, 4, 384]), op=ALU.add),
                        reads=PS[0:4] + [AM], writes=[sc])
                    sm = small.next()
                    P.op("dve", lambda e, sc=sc, sm=sm: e.tensor_reduce(out=sm[:, 0:4], in_=sc[:], axis=AX.X, op=ALU.max),
                         reads=[sc], writes=[sm])
                    P.op("dve", lambda e, sm=sm: e.tensor_scalar(out=sm[:, 0:4], in0=sm[:, 0:4], scalar1=-0.125, scalar2=None, op0=ALU.mult),
                         reads=[sm], writes=[sm])
                    P.op("dve", lambda e, sm=sm, g=g: e.tensor_tensor(out=sm[:, 0:4], in0=sm[:, 0:4], in1=NSINK[:, g * 4:(g + 1) * 4], op=ALU.min),
                         reads=[sm, NSINK], writes=[sm])
                    pb = PB.next()
                    for hh4 in range(4):
                        P.op("act", lambda e, hh4=hh4, sc=sc, pb=pb, sm=sm: e.activation(
                            out=pb[:, hh4, :], in_=sc[:, hh4, :], func=AF.Exp, scale=0.125, bias=sm[:, hh4:hh4 + 1],
                            accum_out=sm[:, 4 + hh4:5 + hh4]), reads=[sc, sm], pwrites=[pb, sm])
                    P.op("dve", lambda e, sm=sm, g=g: e.tensor_tensor(out=sm[:, 8:12], in0=sm[:, 0:4], in1=SINK[:, g * 4:(g + 1) * 4], op=ALU.add),
                         reads=[sm, SINK], writes=[sm])
                    P.op("act", lambda e, sm=sm: e.activation(out=sm[:, 8:12], in_=sm[:, 8:12], func=AF.Exp), reads=[sm], writes=[sm])
                    P.op("dve", lambda e, sm=sm: e.tensor_tensor(out=sm[:, 8:12], in0=sm[:, 8:12], in1=sm[:, 4:8], op=ALU.add),
                         reads=[sm], writes=[sm])
                    P.op("dve", lambda e, sm=sm: e.reciprocal(out=sm[:, 12:16], in_=sm[:, 8:12]), reads=[sm], writes=[sm])
                    pt = PT.next()
                    for half2 in range(2):
                        bi = trps.next()
                        for q6 in range(6):
                            blk = half2 * 6 + q6
                            hh4, j = blk // 3, blk % 3
                            P.op("pe", lambda e, bi=bi, q6=q6, hh4=hh4, j=j, pb=pb: e.transpose(
                                out=psb(bi, q6 * 128, (q6 + 1) * 128), in_=pb[:, hh4, j * 128:(j + 1) * 128], identity=IDB[:]),
                                reads=[pb, IDB], writes=[PS[bi]] if q6 == 0 else (), pwrites=() if q6 == 0 else [PS[bi]])
                        eng = "act" if half2 == 0 else "dve"
                        if eng == "act":
                            P.op("act", lambda e, bi=bi, pt=pt, half2=half2: e.activation(
                                out=pt[:, half2 * 6:half2 * 6 + 6, :].rearrange("p a b -> p (a b)"), in_=psb(bi, 0, 768), func=AF.Copy),
                                reads=[PS[bi]], pwrites=[pt])
                        else:
                            P.op("dve", lambda e, bi=bi, pt=pt, half2=half2: e.tensor_copy(
                                out=pt[:, half2 * 6:half2 * 6 + 6, :].rearrange("p a b -> p (a b)"), in_=psb(bi, 0, 768)),
                                reads=[PS[bi]], pwrites=[pt])
                    pv = pvps.next()
                    for hh4 in range(4):
                        for j in range(3):
                            P.op("pe", lambda e, hh4=hh4, j=j, pv=pv, pt=pt, kv_=kvs[j], g=g: e.matmul(
                                pv[:, hh4 * 64:(hh4 + 1) * 64], lhsT=pt[:, hh4 * 3 + j, :],
                                rhs=kv_[:, 128 + g * 64:128 + (g + 1) * 64], start=(j == 0), stop=(j == 2)),
                                reads=[pt, kvs[j]], writes=[pv] if (hh4 == 0 and j == 0) else (),
                                pwrites=() if (hh4 == 0 and j == 0) else [pv])
                    P.op("dve", lambda e, pv=pv, sm=sm, g=g, ob_t=ob_t: e.tensor_tensor(
                        out=ob_t[:, g * 256:(g + 1) * 256].rearrange("p (h d) -> p h d", h=4),
                        in0=pv[:, 0:256].rearrange("p (h d) -> p h d", h=4),
                        in1=sm[:, 12:16].unsqueeze(2).to_broadcast([128, 4, 64]), op=ALU.mult),
                        reads=[pv, sm], pwrites=[ob_t])
                st(rows(ob_d, t0, 128), ob_t, ob_t[:])

            p2a_body(0, True, NT == 1)
            if NT - 2 >= 2:
                loop(1, NT - 1, lambda: p2a_body(None, False, False), [ix2])
            else:
                for t_ in range(1, NT - 1):
                    p2a_body(t_, False, False)
            if NT > 1:
                p2a_body(NT - 1, False, True)
            end_phase(T)


        def phase_p2(seq, l):
            S = seq.S
            NT = S // 128
            T = Tiles()
            M4 = {}
            for ci in (C_UI, C_LI, C_SL, C_SU, C_ID):
                m = T.sb(f"M4_{ci}", [128, 4, 128], F32)
                P.op("pool", lambda e, m=m, ci=ci: e.tensor_copy(out=m[:], in_=CF[:, ci, :].unsqueeze(1).to_broadcast([128, 4, 128])),
                     reads=[CF], writes=[m])
                M4[ci] = m
            I4 = M4[C_ID]
            NM4 = {}
            for ci in (C_UI, C_LI, C_SL, C_SU):
                m = T.sb(f"NM4_{ci}", [128, 4, 128], F32)
                P.op("dve", lambda e, m=m, ci=ci: e.tensor_scalar(out=m[:], in0=M4[ci][:], scalar1=30000.0, scalar2=-30000.0,
                                                              op0=ALU.mult, op1=ALU.add), reads=[M4[ci]], writes=[m])
                NM4[ci] = m
            ONESF = CF[:, C_ONE, :]
            SA32 = [T.sb(f"SA32_{d}", [128, 4, 128], F32) for d in range(2)]
            SAbf = [T.sb(f"SAbf_{d}", [128, 4, 128], BF16) for d in range(2)]
            SC32 = [T.sb(f"SC32_{d}", [128, 4, 128], F32) for d in range(2)]
            SCbf = [T.sb(f"SCbf_{d}", [128, 4, 128], BF16) for d in range(2)]
            for b_ in SA32 + SAbf + SC32 + SCbf:
                P.op("pool", lambda e, b_=b_: e.memset(b_[:], 0.0), writes=[b_])
            R = lambda name, shape, dt, n=2: T.ring(name, shape, dt, n)
            hq_t = R("hq_t", [128, 512], BF16); hlf_t = R("hlf_t", [128, 512], F32)
            hk_t = R("hk_t", [128, 512], BF16); hv_t = R("hv_t", [128, 512], BF16)
            eG = R("eG", [128, 512], F32); enG = R("enG", [128, 512], F32); eR = R("eR", [128, 512], F32)
            qt_t = R("qt_t", [128, 512], BF16); kt_t = R("kt_t", [128, 512], BF16); kd_t = R("kd_t", [128, 512], BF16)
            QKT = R("QKT", [128, 8, 128], BF16); AT = R("AT", [128, 4, 128], BF16)
            egl = R("egl", [128, 8], F32); oA = R("oA", [128, 512], F32)
            gq_t = R("gq_t", [128, 4, 128], F32); gk_t = R("gk_t", [128, 4, 128], F32); gv_t = R("gv_t", [128, 4, 128], BF16)
            bg_t = R("bg_t", [128, 16], F32); ecol = R("ecol", [128, 16], F32); CL = R("CL", [128, 16], F32)
            Y0v = R("Y0v", [128, 4, 128], BF16); Y0k = R("Y0k", [128, 4, 128], BF16)
            kdec = R("kdec", [128, 4, 128], BF16); qdec = R("qdec", [128, 4, 128], BF16)
            knb = R("knb", [128, 4, 128], BF16); qnb = R("qnb", [128, 4, 128], BF16)
            KQT = R("KQT", [128, 4, 2, 128], BF16); QDT = R("QDT", [128, 4, 128], BF16)
            GREP = R("GREP", [128, 4, 128], F32); NGC = R("NGC", [128, 4, 128], F32)
            EDT = R("EDT", [128, 4, 128], F32); ED = R("ED", [128, 4, 128], F32)
            Dm = R("Dm", [128, 4, 128], F32); DTm = R("DTm", [128, 4, 128], F32)
            NN = R("NN", [128, 4, 128], F32, 4); MM = R("MM", [128, 4, 128], F32, 4)
            P32 = R("P32", [128, 4, 128], F32); QKm = R("QKm", [128, 4, 128], BF16)
            TT = R("TT", [128, 4, 128], BF16); U32 = R("U32", [128, 512], F32); WT = R("WT", [128, 4, 128], BF16)
            VN = R("VN", [128, 512], BF16); oC = R("oC", [128, 512], F32)
            B = PS

            def mm(out_ap, lhsT, rhs, rd, bank, first, start=True, stop=True):
                P.op("pe", lambda e: e.matmul(out_ap, lhsT=lhsT, rhs=rhs, start=start, stop=stop),
                     reads=rd, writes=[bank] if first else (), pwrites=() if first else [bank])

            def hgrn_step(t, di):
                t0 = t if isinstance(t, DynRow) else t * 128
                ci_cum, ci_rem = (C_UI, C_SL) if di == 0 else (C_LI, C_SU)
                q = hq_t.next(); lf = hlf_t.next(); k = hk_t.next(); v = hv_t.next()
                ld(q, q[:], rows(hq, t0, 128)); ld(lf, lf[:], rows(hlf[di], t0, 128))
                ld(k, k[:], rows(hk[di], t0, 128)); ld(v, v[:], rows(hv, t0, 128))
                yield
                mm(B[0][:], CF[:, ci_cum, :], lf[:], [CF, lf], B[0], True)
                mm(B[1][:], CF[:, ci_rem, :], lf[:], [CF, lf], B[1], True)
                for c in range(2):
                    for h in range(4):
                        mm(B[2][:, c * 4 + h:c * 4 + h + 1], lf[c * 64:(c + 1) * 64, h * 128:(h + 1) * 128],
                           CF[c * 64:(c + 1) * 64, C_ONE, 0:1], [CF, lf], B[2], c == 0 and h == 0)
                yield
                eg = eG.next(); eng_ = enG.next(); er = eR.next(); el = egl.next()
                P.op("act", lambda e: e.activation(out=eg[:], in_=B[0][:], func=AF.Exp), reads=[B[0]], writes=[eg])
                P.op("act", lambda e: e.activation(out=eng_[:], in_=B[0][:], func=AF.Exp, scale=-1.0), reads=[B[0]], writes=[eng_])
                P.op("act", lambda e: e.activation(out=er[:], in_=B[1][:], func=AF.Exp), reads=[B[1]], writes=[er])
                P.op("act", lambda e: e.activation(out=el[:], in_=B[2][:, 0:8], func=AF.Exp), reads=[B[2]], writes=[el])
                yield
                qt = qt_t.next(); kt = kt_t.next(); kd = kd_t.next()
                P.op("dve", lambda e: e.scalar_tensor_tensor(out=qt[:], in0=q[:], scalar=float(128 ** -0.5), in1=eg[:], op0=ALU.mult, op1=ALU.mult),
                     reads=[q, eg], writes=[qt])
                P.op("pool", lambda e: e.tensor_tensor(out=kt[:], in0=k[:], in1=eng_[:], op=ALU.mult), reads=[k, eng_], writes=[kt])
                P.op("pool", lambda e: e.tensor_tensor(out=kd[:], in0=k[:], in1=er[:], op=ALU.mult), reads=[k, er], writes=[kd])
                yield
                qk = QKT.next()
                for h in range(4):
                    P.op("pe", lambda e, h=h: e.transpose(out=psb(3, h * 128, (h + 1) * 128), in_=qt[:, h * 128:(h + 1) * 128], identity=IDB[:]),
                         reads=[qt, IDB], writes=[B[3]] if h == 0 else (), pwrites=() if h == 0 else [B[3]])
                for h in range(4):
                    P.op("pe", lambda e, h=h: e.transpose(out=psb(3, (4 + h) * 128, (5 + h) * 128), in_=kt[:, h * 128:(h + 1) * 128], identity=IDB[:]),
                         reads=[kt, IDB], pwrites=[B[3]])
                P.op("dve", lambda e: e.tensor_copy(out=qk[:].rearrange("p a b -> p (a b)"), in_=psb(3, 0, 1024)), reads=[B[3]], writes=[qk])
                for h in range(4):
                    mm(B[0][:, h * 128:(h + 1) * 128], qk[:, 4 + h, :], qk[:, h, :], [qk], B[0], h == 0)
                yield
                at = AT.next()
                hm = M4[C_UI if di == 0 else C_LI]
                P.op("dve", lambda e: e.tensor_tensor(out=at[:].rearrange("p a b -> p (a b)"), in0=B[0][:], in1=hm[:].rearrange("p a b -> p (a b)"), op=ALU.mult),
                     reads=[B[0], hm], writes=[at])
                S32 = SA32[di]; Sbf = SAbf[di]
                order = (0, 1) if di == 0 else (1, 0)
                for ic, c in enumerate(order):
                    yield
                    r0 = c * 64
                    bd = B[1] if ic == 0 else B[3]
                    for h in range(4):
                        hs = slice(h * 128, (h + 1) * 128)
                        mm(B[2][r0:r0 + 64, hs], at[r0:r0 + 64, h, r0:r0 + 64], v[r0:r0 + 64, hs], [at, v], B[2],
                           ic == 0 and h == 0, start=True, stop=False)
                        mm(B[2][r0:r0 + 64, hs], qk[:, h, r0:r0 + 64], Sbf[:, h, :], [qk, Sbf], B[2], False, start=False, stop=True)
                    for h in range(4):
                        hs = slice(h * 128, (h + 1) * 128)
                        mm(bd[:, hs], kd[r0:r0 + 64, hs], v[r0:r0 + 64, hs], [kd, v], bd, h == 0)
                    for h in range(4):
                        hs = slice(h * 128, (h + 1) * 128)
                        P.op("dve", lambda e, h=h, hs=hs, c=c, bd=bd: e.scalar_tensor_tensor(
                            out=S32[:, h, :], in0=S32[:, h, :], scalar=el[:, c * 4 + h:c * 4 + h + 1], in1=bd[:, hs], op0=ALU.mult, op1=ALU.add),
                            reads=[S32, el, bd], writes=[S32])
                    P.op("act", lambda e: e.activation(out=Sbf[:], in_=S32[:], func=AF.Copy), reads=[S32], writes=[Sbf])
                yield
                o = oA.next()
                P.op("act", lambda e: e.activation(out=o[:], in_=B[2][:], func=AF.Copy), reads=[B[2]], writes=[o])
                st(rows(oa_d[di], t0, 128), o, o[:])

            def gdn_step(t, di):
                t0 = t if isinstance(t, DynRow) else t * 128
                ci_cum, ci_rem = (C_UI, C_SL) if di == 0 else (C_LI, C_SU)
                ci_n, ci_dt = (C_SL, C_UI) if di == 0 else (C_SU, C_LI)
                gq = gq_t.next(); gk = gk_t.next(); gv = gv_t.next(); bg = bg_t.next()
                ld(gq, gq[:].rearrange("p a b -> p (a b)"), rows(gq_d, t0, 128))
                ld(gk, gk[:].rearrange("p a b -> p (a b)"), rows(gk_d, t0, 128))
                ld(gv, gv[:].rearrange("p a b -> p (a b)"), rows(gv_d, t0, 128))
                ld(bg, bg[:], rows(bg_d, t0, 128))
                yield
                beta = bg[:, di * 4:di * 4 + 4]
                gg = bg[:, 8 + di * 4:8 + di * 4 + 4]
                mm(B[4][:, 0:4], CF[:, ci_cum, :], gg, [CF, bg], B[4], True)
                mm(B[4][:, 4:8], CF[:, ci_rem, :], gg, [CF, bg], B[4], False)
                mm(B[4][:, 8:12], CF[:, C_SEL0, :], gg, [CF, bg], B[4], False)
                mm(B[4][:, 12:16], CF[:, C_SEL1, :], gg, [CF, bg], B[4], False)
                yield
                ec = ecol.next(); cl = CL.next()
                P.op("act", lambda e: e.activation(out=ec[:], in_=B[4][:, 0:16], func=AF.Exp), reads=[B[4]], writes=[ec])
                P.op("dve", lambda e: e.tensor_tensor(out=cl[:, 0:4], in0=beta, in1=ec[:, 0:4], op=ALU.mult), reads=[bg, ec], writes=[cl])
                P.op("dve", lambda e: e.tensor_scalar(out=cl[:, 4:8], in0=beta, scalar1=-1.0, scalar2=None, op0=ALU.mult), reads=[bg], pwrites=[cl])
                P.op("dve", lambda e: e.tensor_scalar(out=cl[:, 8:12], in0=gg, scalar1=-1.0, scalar2=None, op0=ALU.mult), reads=[bg], pwrites=[cl])
                yield
                yv = Y0v.next(); yk = Y0k.next(); kdc = kdec.next(); qdc = qdec.next(); kb = knb.next(); qb = qnb.next()

                def bc(col_ap):
                    return col_ap.unsqueeze(2).to_broadcast([128, 4, 128])
                P.op("pool", lambda e: e.tensor_tensor(out=yv[:], in0=gv[:], in1=bc(beta), op=ALU.mult), reads=[gv, bg], writes=[yv])
                P.op("dve", lambda e: e.tensor_tensor(out=yk[:], in0=gk[:], in1=bc(cl[:, 0:4]), op=ALU.mult), reads=[gk, cl], writes=[yk])
                P.op("pool", lambda e: e.tensor_tensor(out=kdc[:], in0=gk[:], in1=bc(ec[:, 4:8]), op=ALU.mult), reads=[gk, ec], writes=[kdc])
                P.op("dve", lambda e: e.tensor_tensor(out=qdc[:], in0=gq[:], in1=bc(ec[:, 0:4]), op=ALU.mult), reads=[gq, ec], writes=[qdc])
                P.op("act", lambda e: e.activation(out=kb[:], in_=gk[:], func=AF.Copy), reads=[gk], writes=[kb])
                P.op("act", lambda e: e.activation(out=qb[:], in_=gq[:], func=AF.Copy), reads=[gq], writes=[qb])
                yield
                kqt = KQT.next(); qdt = QDT.next()
                for h in range(4):
                    P.op("pe", lambda e, h=h: e.transpose(out=psb(5, (2 * h) * 128, (2 * h + 1) * 128), in_=kb[:, h, :], identity=IDB[:]),
                         reads=[kb, IDB], writes=[B[5]] if h == 0 else (), pwrites=() if h == 0 else [B[5]])
                    P.op("pe", lambda e, h=h: e.transpose(out=psb(5, (2 * h + 1) * 128, (2 * h + 2) * 128), in_=qb[:, h, :], identity=IDB[:]),
                         reads=[qb, IDB], pwrites=[B[5]])
                for h in range(4):
                    P.op("pe", lambda e, h=h: e.transpose(out=psb(6, h * 128, (h + 1) * 128), in_=qdc[:, h, :], identity=IDB[:]),
                         reads=[qdc, IDB], writes=[B[6]] if h == 0 else (), pwrites=() if h == 0 else [B[6]])
                P.op("dve", lambda e: e.tensor_copy(out=kqt[:].rearrange("p a b c -> p (a b c)"), in_=psb(5, 0, 1024)), reads=[B[5]], writes=[kqt])
                P.op("act", lambda e: e.activation(out=qdt[:].rearrange("p a b -> p (a b)"), in_=psb(6, 0, 512), func=AF.Copy), reads=[B[6]], writes=[qdt])
                yield
                grep = GREP.next(); ngc = NGC.next()
                P.op("pool", lambda e: e.tensor_copy(out=grep[:], in_=bc(gg)), reads=[bg], writes=[grep])
                P.op("pool", lambda e: e.tensor_tensor(out=ngc[:], in0=M4[ci_cum][:], in1=bc(cl[:, 8:12]), op=ALU.mult), reads=[M4[ci_cum], cl], writes=[ngc])
                for h in range(4):
                    hs = slice(h * 128, (h + 1) * 128)
                    mm(B[7][:, hs], grep[:, h, :], CF[:, ci_cum, :], [grep, CF], B[7], h == 0, start=True, stop=False)
                    mm(B[7][:, hs], ngc[:, h, :], ONESF, [ngc, CF], B[7], False, start=False, stop=True)
                yield
                edt = EDT.next(); ed = ED.next(); dm = Dm.next(); dtm = DTm.next()
                f2 = lambda b_: b_[:].rearrange("p a b -> p (a b)")
                P.op("dve", lambda e: e.tensor_tensor(out=f2(edt), in0=B[7][:], in1=f2(NM4[ci_dt]), op=ALU.add),
                     reads=[B[7], NM4[ci_dt]], writes=[edt])
                P.op("dve", lambda e: e.scalar_tensor_tensor(out=f2(ed), in0=B[7][:], scalar=-1.0, in1=f2(NM4[ci_n]), op0=ALU.mult, op1=ALU.add),
                     reads=[B[7], NM4[ci_n]], writes=[ed])
                P.op("act", lambda e: e.activation(out=f2(dtm), in_=f2(edt), func=AF.Exp), reads=[edt], writes=[dtm])
                P.op("act", lambda e: e.activation(out=f2(dm), in_=f2(ed), func=AF.Exp), reads=[ed], writes=[dm])
                yield
                kkb = (B[4], B[5])
                for h in range(4):
                    bk_ = kkb[h // 2]
                    mm(bk_[:, (h % 2) * 256:(h % 2) * 256 + 256], kqt[:, h, 0, :], kqt[:, h, :, :].rearrange("p a b -> p (a b)"), [kqt], bk_, h % 2 == 0)
                yield
                n0 = NN.next(); m0 = MM.next(); p32 = P32.next(); qkm = QKm.next()
                for h in range(4):
                    bk_ = kkb[h // 2]
                    P.op("dve", lambda e, h=h, bk_=bk_: e.scalar_tensor_tensor(
                        out=n0[:, h, :], in0=bk_[:, (h % 2) * 256:(h % 2) * 256 + 128], scalar=cl[:, 4 + h:5 + h], in1=dm[:, h, :],
                        op0=ALU.mult, op1=ALU.mult), reads=[bk_, cl, dm], writes=[n0] if h == 0 else (), pwrites=() if h == 0 else [n0])
                for b2 in range(2):
                    bk_ = kkb[b2]
                    P.op("dve", lambda e, b2=b2, bk_=bk_: e.tensor_tensor(
                        out=qkm[:, 2 * b2:2 * b2 + 2, :], in0=bk_[:].rearrange("p (h w k) -> p h w k", h=2, w=2)[:, :, 1, :],
                        in1=dtm[:, 2 * b2:2 * b2 + 2, :], op=ALU.mult), reads=[bk_, dtm], writes=[qkm] if b2 == 0 else (), pwrites=() if b2 == 0 else [qkm])
                for h in range(4):
                    P.op("pe", lambda e, h=h: e.transpose(out=B[6][:, h * 128:(h + 1) * 128], in_=n0[:, h, :], identity=ident),
                         reads=[n0, CF], writes=[B[6]] if h == 0 else (), pwrites=() if h == 0 else [B[6]])
                P.op("act", lambda e: e.activation(out=f2(m0), in_=B[6][:], func=AF.Copy), reads=[B[6]], writes=[m0])
                P.op("dve", lambda e: e.tensor_tensor(out=f2(p32), in0=B[6][:], in1=f2(I4), op=ALU.add), reads=[B[6], I4], writes=[p32])
                yield
                ncur, mcur = n0, m0
                sets = ((B[4], B[5], B[7]), (B[4], B[5], B[7]))
                for lev in range(1, 6):
                    yield
                    bn, bm, bp = sets[lev % 2]
                    nn_ = NN.next()
                    for h in range(4):
                        mm(bn[:, h * 128:(h + 1) * 128], mcur[:, h, :], ncur[:, h, :], [mcur, ncur], bn, h == 0)
                    P.op("act", lambda e, nn_=nn_, bn=bn: e.activation(out=f2(nn_), in_=bn[:], func=AF.Copy), reads=[bn], writes=[nn_])
                    if lev < 5:
                        mn_ = MM.next()
                        for h in range(4):
                            mm(bm[:, h * 128:(h + 1) * 128], ncur[:, h, :], mcur[:, h, :], [mcur, ncur], bm, h == 0)
                        P.op("dve", lambda e, mn_=mn_, bm=bm: e.tensor_copy(out=f2(mn_), in_=bm[:]), reads=[bm], writes=[mn_])
                    for h in range(4):
                        mm(bp[:, h * 128:(h + 1) * 128], nn_[:, h, :], p32[:, h, :], [nn_, p32], bp, h == 0)
                    P.op("dve", lambda e, bp=bp: e.tensor_tensor(out=f2(p32), in0=f2(p32), in1=bp[:], op=ALU.add), reads=[p32, bp], writes=[p32])
                    ncur = nn_
                    if lev < 5:
                        mcur = mn_
                yield
                tt = TT.next(); u32 = U32.next(); wt = WT.next()
                P.op("act", lambda e: e.activation(out=f2(tt), in_=f2(p32), func=AF.Copy), reads=[p32], writes=[tt])
                for h in range(4):
                    mm(B[4][:, h * 128:(h + 1) * 128], tt[:, h, :], yv[:, h, :], [tt, yv], B[4], h == 0)
                for h in range(4):
                    mm(B[5][:, h * 128:(h + 1) * 128], yk[:, h, :], tt[:, h, :], [tt, yk], B[5], h == 0)
                P.op("act", lambda e: e.activation(out=u32[:], in_=B[4][:], func=AF.Copy), reads=[B[4]], writes=[u32])
                P.op("dve", lambda e: e.tensor_copy(out=f2(wt), in_=B[5][:]), reads=[B[5]], writes=[wt])
                yield
                S32 = SC32[di]; Sbf = SCbf[di]
                vn = VN.next()
                order = (0, 1) if di == 0 else (1, 0)
                for ic, c in enumerate(order):
                    yield
                    r0 = c * 64
                    bv_ = B[4]
                    bd = B[5] if ic == 0 else B[7]
                    for h in range(4):
                        hs = slice(h * 128, (h + 1) * 128)
                        mm(bv_[r0:r0 + 64, hs], wt[:, h, r0:r0 + 64], Sbf[:, h, :], [wt, Sbf], bv_, h == 0)
                    P.op("dve", lambda e, r0=r0, bv_=bv_: e.tensor_tensor(out=vn[r0:r0 + 64, :], in0=u32[r0:r0 + 64, :], in1=bv_[r0:r0 + 64, :], op=ALU.subtract),
                         reads=[u32, bv_], writes=[vn] if ic == 0 else (), pwrites=() if ic == 0 else [vn])
                    for h in range(4):
                        hs = slice(h * 128, (h + 1) * 128)
                        mm(B[6][r0:r0 + 64, hs], qkm[r0:r0 + 64, h, r0:r0 + 64], vn[r0:r0 + 64, hs], [qkm, vn], B[6],
                           ic == 0 and h == 0, start=True, stop=False)
                        mm(B[6][r0:r0 + 64, hs], qdt[:, h, r0:r0 + 64], Sbf[:, h, :], [qdt, Sbf], B[6], False, start=False, stop=True)
                    for h in range(4):
                        hs = slice(h * 128, (h + 1) * 128)
                        mm(bd[:, hs], kdc[r0:r0 + 64, h, :], vn[r0:r0 + 64, hs], [kdc, vn], bd, h == 0)
                    for h in range(4):
                        hs = slice(h * 128, (h + 1) * 128)
                        P.op("dve", lambda e, h=h, hs=hs, c=c, bd=bd: e.scalar_tensor_tensor(
                            out=S32[:, h, :], in0=S32[:, h, :], scalar=ec[:, 8 + c * 4 + h:9 + c * 4 + h], in1=bd[:, hs], op0=ALU.mult, op1=ALU.add),
                            reads=[S32, ec, bd], writes=[S32])
                    P.op("act", lambda e: e.activation(out=f2(Sbf), in_=f2(S32), func=AF.Copy), reads=[S32], writes=[Sbf])
                yield
                o = oC.next()
                P.op("act", lambda e: e.activation(out=o[:], in_=B[6][:], func=AF.Copy), reads=[B[6]], writes=[o])
                st(rows(oc_d[di], t0, 128), o, o[:])

            def run_pair(g1, g2):
                gens = [g1, g2]
                while gens:
                    for g_ in list(gens):
                        try:
                            next(g_)
                        except StopIteration:
                            gens.remove(g_)

            ixf = RowIdx(T, "ixf", [0], 0, 128)
            ixb = RowIdx(T, "ixb", [0], (NT - 1) * 128, -128)

            def p2_body():
                gs4 = [hgrn_step(DynRow(ixf), 0), gdn_step(DynRow(ixf), 0), hgrn_step(DynRow(ixb), 1), gdn_step(DynRow(ixb), 1)]
                for g_ in gs4:
                    next(g_)
                run_pair(gs4[0], gs4[1])
                run_pair(gs4[2], gs4[3])

            if NT > 1:
                loop(0, NT, p2_body, [ixf, ixb])
            else:
                run_pair(hgrn_step(0, 0), gdn_step(0, 0)); run_pair(hgrn_step(0, 1), gdn_step(0, 1))
            end_phase(T)


        def phase_p3(seq, l, xcur, xnext):
            S = seq.S
            BLK = min(512, S)
            NSUB = BLK // 128
            NB = S // BLK
            T = Tiles()
            LNG = T.sb("LNG", [128, 3, D], F32)
            LNB = T.sb("LNB", [128, 3, D], F32)
            NGA = T.sb("NGA", [128, 128], F32)
            NGC = T.sb("NGC", [128, 128], F32)
            memKT = T.sb("memKT", [128, 8, MEM], BF16)
            memV = T.sb("memV", [128, 2, D], BF16)
            xin = T.ring("xin", [128, D], F32, max(NSUB, 2))
            FT = [T.sb(f"FT{i}", [128, 8, max(BLK, MEM)], BF16) for i in range(3)]
            wpan = T.ring("wpan", [128, KC, 512], BF16, 3)
            wfo = T.ring("wfo", [128, 11, 512], BF16, 2)
            HID = T.sb("HID", [128, FKC, BLK], BF16)
            o32 = T.ring("o32", [128, 512], F32, 4)
            g16 = T.ring("g16", [128, 512], BF16, 3)
            mgt = T.ring("mgt", [128, 3072], BF16, 2)
            tmp = T.ring("tmp", [128, 512], F32, 4)
            on16 = T.ring("on16", [128, 512], BF16, 3)
            mixt = T.ring("mixt", [128, D], BF16, 2)
            pbt = T.ring("pbt", [128, 4, MEM], BF16, 2)
            pTt = T.ring("pTt", [128, 8, 128], BF16, 2)
            small = T.ring("small", [128, 16], F32, 8)
            bnst = T.ring("bnst", [128, 2, 6], F32, 4)
            ld(LNG, LNG[:].rearrange("p a b -> p (a b)"), ln_g[l:l + 1, :].partition_broadcast(128))
            ld(LNB, LNB[:].rearrange("p a b -> p (a b)"), ln_b[l:l + 1, :].partition_broadcast(128))
            ld(NGA, NGA[:], hgrn_norm_g[l:l + 1, :].partition_broadcast(128))
            ld(NGC, NGC[:], gdn_norm_g[l:l + 1, :].partition_broadcast(128))
            pr = Ring(PS[0:6])
            ptr = Ring([6, 7])
            f3 = lambda b_: b_[:].rearrange("p a b -> p (a b)")

            def wload(src2d, c0, ncol, nk=KC):
                wp = wpan.next()
                ld(wp, wp[:, 0:nk, 0:ncol], src2d[:, c0:c0 + ncol].rearrange("(kc p) c -> p kc c", p=128))
                return wp

            def tr16(src_list, dst_ft, k0, s):
                n = len(src_list)
                bi = ptr.next()
                for i, (bf_, ap_) in enumerate(src_list):
                    P.op("pe", lambda e, i=i, ap_=ap_, bi=bi: e.transpose(out=psb(bi, i * 128, (i + 1) * 128), in_=ap_, identity=IDB[:]),
                         reads=[bf_, IDB], writes=[PS[bi]] if i == 0 else (), pwrites=() if i == 0 else [PS[bi]])
                P.op("act", lambda e, bi=bi: e.activation(out=dst_ft[:, k0:k0 + n, s * 128:(s + 1) * 128],
                                                         in_=psb(bi, 0, n * 128).rearrange("p (a b) -> p a b", a=n), func=AF.Copy),
                     reads=[PS[bi]], pwrites=[dst_ft])

            def tr32(xt, dst_ft, s):
                for half in range(2):
                    bi = ptr.next()
                    for j in range(4):
                        kc = half * 4 + j
                        P.op("pe", lambda e, j=j, kc=kc, bi=bi: e.transpose(out=PS[bi][:, j * 128:(j + 1) * 128], in_=xt[:, kc * 128:(kc + 1) * 128], identity=ident),
                             reads=[xt, CF], writes=[PS[bi]] if j == 0 else (), pwrites=() if j == 0 else [PS[bi]])
                    eng = "act" if half == 0 else "dve"
                    if eng == "act":
                        P.op("act", lambda e, bi=bi, half=half: e.activation(out=dst_ft[:, half * 4:half * 4 + 4, s * 128:(s + 1) * 128],
                                                                            in_=PS[bi][:].rearrange("p (a b) -> p a b", a=4), func=AF.Copy),
                             reads=[PS[bi]], pwrites=[dst_ft])
                    else:
                        P.op("dve", lambda e, bi=bi, half=half: e.tensor_copy(out=dst_ft[:, half * 4:half * 4 + 4, s * 128:(s + 1) * 128],
                                                                             in_=PS[bi][:].rearrange("p (a b) -> p a b", a=4)),
                             reads=[PS[bi]], pwrites=[dst_ft])

            def layer_norm_all(xl, idx):
                bss = [bnst.next() for _ in xl]
                sms = [small.next() for _ in xl]
                for xt, bs in zip(xl, bss):
                    for hf in range(2):
                        P.op("dve", lambda e, hf=hf, xt=xt, bs=bs: e.bn_stats(out=bs[:, hf, :], in_=xt[:, hf * 512:(hf + 1) * 512]), reads=[xt],
                             writes=[bs] if hf == 0 else (), pwrites=() if hf == 0 else [bs])
                for bs, sm in zip(bss, sms):
                    P.op("dve", lambda e, bs=bs, sm=sm: e.bn_aggr(out=sm[:, 0:2], in_=bs[:]), reads=[bs], writes=[sm])
                for sm in sms:
                    P.op("act", lambda e, sm=sm: e.activation(out=sm[:, 2:3], in_=sm[:, 1:2], func=AF.Sqrt, bias=EPS[:, 1:2]), reads=[sm, EPS], writes=[sm])
                for sm in sms:
                    P.op("dve", lambda e, sm=sm: e.reciprocal(out=sm[:, 3:4], in_=sm[:, 2:3]), reads=[sm], writes=[sm])
                for xt, sm in zip(xl, sms):
                    P.op("dve", lambda e, xt=xt, sm=sm: e.tensor_scalar(out=xt[:], in0=xt[:], scalar1=sm[:, 0:1], scalar2=sm[:, 3:4], op0=ALU.subtract, op1=ALU.mult),
                         reads=[xt, sm], writes=[xt])
                for xt in xl:
                    P.op("dve", lambda e, xt=xt: e.tensor_tensor(out=xt[:], in0=xt[:], in1=LNG[:, idx, :], op=ALU.mult), reads=[xt, LNG], writes=[xt])
                for xt in xl:
                    P.op("dve", lambda e, xt=xt: e.tensor_tensor(out=xt[:], in0=xt[:], in1=LNB[:, idx, :], op=ALU.add), reads=[xt, LNB], writes=[xt])

            def proj_res(src_ft, wtiles, xts):
                for s in range(NSUB):
                    for half in range(2):
                        ps = pr.next()
                        wp = wtiles[half]
                        for kc in range(KC):
                            mm(ps[:], src_ft[:, kc, s * 128:(s + 1) * 128], wp[:, kc, :], [src_ft, wp], ps, kc == 0, start=(kc == 0), stop=(kc == KC - 1))
                        xt = xts[s]
                        P.op("dve", lambda e, ps=ps, xt=xt, half=half: e.scalar_tensor_tensor(
                            out=xt[:, half * 512:(half + 1) * 512], in0=xt[:, half * 512:(half + 1) * 512], scalar=ALPHA, in1=ps[:],
                            op0=ALU.mult, op1=ALU.add), reads=[xt, ps], writes=[xt])

            def mm(out_ap, lhsT, rhs, rd, bank, first, start=True, stop=True):
                P.op("pe", lambda e: e.matmul(out_ap, lhsT=lhsT, rhs=rhs, start=start, stop=stop),
                     reads=rd, writes=[bank] if first else (), pwrites=() if first else [bank])

            memT = FT[0]
            for mt in range(2):
                xt = xin.next()
                ld(xt, xt[:], seq.mem[mt * 128:(mt + 1) * 128, :])
                tr32(xt, memT, mt)
            for pc in range(4):
                wp = wload(wb["wkv"][l], pc * 512, 512)
                if pc < 2:
                    for c4 in range(4):
                        c = pc * 4 + c4
                        ps = pr.next()
                        for kc in range(KC):
                            mm(ps[:, 0:MEM], wp[:, kc, c4 * 128:(c4 + 1) * 128], memT[:, kc, 0:MEM], [wp, memT], ps, kc == 0,
                               start=(kc == 0), stop=(kc == KC - 1))
                        P.op("act", lambda e, ps=ps, c=c: e.activation(out=memKT[:, c, :], in_=ps[:, 0:MEM], func=AF.Copy), reads=[ps], pwrites=[memKT])
                else:
                    half = pc - 2
                    for mt in range(2):
                        ps = pr.next()
                        for kc in range(KC):
                            mm(ps[:], memT[:, kc, mt * 128:(mt + 1) * 128], wp[:, kc, :], [wp, memT], ps, kc == 0, start=(kc == 0), stop=(kc == KC - 1))
                        P.op("dve", lambda e, ps=ps, mt=mt, half=half: e.tensor_copy(out=memV[:, mt, half * 512:(half + 1) * 512], in_=ps[:]),
                             reads=[ps], pwrites=[memV])

            ix3 = RowIdx(T, "ix3", [s_ * 128 for s_ in range(NSUB)], 0, BLK)

            def p3_body(b=None):
                t0 = DynRow(ix3) if b is None else b * BLK
                xts = []
                for s in range(NSUB):
                    xt = xin.next()
                    ld(xt, xt[:], rows(xcur, t0 + s * 128, 128))
                    xts.append(xt)
                mgs = []
                for s in range(NSUB):
                    r0 = t0 + s * 128
                    outs_n = []
                    for mi, (od, gd, NG) in enumerate(((oa_d, ga_d, NGA), (oc_d, gcg_d, NGC))):
                        of_ = o32.next(); ob_ = o32.next(); gt = g16.next()
                        ld(of_, of_[:], rows(od[0], r0, 128)); ld(ob_, ob_[:], rows(od[1], r0, 128)); ld(gt, gt[:], rows(gd, r0, 128))
                        sq = tmp.next(); sm = small.next(); on = on16.next()
                        P.op("pool", lambda e, of_=of_, ob_=ob_: e.tensor_tensor(out=of_[:], in0=of_[:], in1=ob_[:], op=ALU.add), reads=[of_, ob_], writes=[of_])
                        P.op("pool", lambda e, of_=of_, sq=sq: e.tensor_tensor(out=sq[:], in0=of_[:], in1=of_[:], op=ALU.mult), reads=[of_], writes=[sq])
                        P.op("dve", lambda e, sq=sq, sm=sm: e.tensor_reduce(out=sm[:, 0:4], in_=sq[:].rearrange("p (h d) -> p h d", h=4), axis=AX.X, op=ALU.add),
                             reads=[sq], writes=[sm])
                        P.op("act", lambda e, sm=sm: e.activation(out=sm[:, 4:8], in_=sm[:, 0:4], func=AF.Sqrt, bias=EPS[:, 0:1], scale=1.0 / 128.0),
                             reads=[sm, EPS], writes=[sm])
                        P.op("dve", lambda e, sm=sm: e.reciprocal(out=sm[:, 8:12], in_=sm[:, 4:8]), reads=[sm], writes=[sm])
                        P.op("dve", lambda e, of_=of_, sm=sm: e.tensor_tensor(out=of_[:].rearrange("p (h d) -> p h d", h=4), in0=of_[:].rearrange("p (h d) -> p h d", h=4),
                                                                            in1=sm[:, 8:12].unsqueeze(2).to_broadcast([128, 4, 128]), op=ALU.mult),
                             reads=[of_, sm], writes=[of_])
                        P.op("pool", lambda e, of_=of_, NG=NG: e.tensor_tensor(out=of_[:].rearrange("p (h d) -> p h d", h=4), in0=of_[:].rearrange("p (h d) -> p h d", h=4),
                                                                             in1=NG[:].unsqueeze(1).to_broadcast([128, 4, 128]), op=ALU.mult),
                             reads=[of_, NG], writes=[of_])
                        P.op("dve", lambda e, of_=of_, gt=gt, on=on: e.tensor_tensor(out=on[:], in0=of_[:], in1=gt[:], op=ALU.mult), reads=[of_, gt], writes=[on])
                        outs_n.append(on)
                    obt = g16.next()
                    ld(obt, obt[:], rows(ob_d, r0, 128))
                    mg = mgt.next()
                    for j6 in range(6):
                        ld(mg, mg[:, j6 * 512:(j6 + 1) * 512], rows(mg_d[j6], r0, 128), partial=(j6 > 0))
                    mgs.append(mg)
                    tr16([(outs_n[0], outs_n[0][:, k * 128:(k + 1) * 128]) for k in range(4)] +
                         [(outs_n[1], outs_n[1][:, k * 128:(k + 1) * 128]) for k in range(4)], FT[1], 0, s)
                    tr16([(obt, obt[:, k * 128:(k + 1) * 128]) for k in range(4)], FT[2], 0, s)
                    if s == 0:
                        wbr = []
                        for nm in ("wba", "wbb", "wbc"):
                            wp = wpan.next()
                            ld(wp, wp[:].rearrange("p a b -> p (a b)").rearrange("p (k c) -> p k c", k=4),
                               wb[nm][l].rearrange("(kc p) c -> p kc c", p=128))
                            wbr.append(wp)
                    mx = mixt.next()
                    for half in range(2):
                        pss = []
                        for bi_, (ft, k0) in enumerate(((FT[1], 0), (FT[2], 0), (FT[1], 4))):
                            ps = pr.next()
                            wv = wbr[bi_][:].rearrange("p a b -> p (a b)").rearrange("p (k c) -> p k c", k=4)
                            for kc in range(4):
                                mm(ps[:], ft[:, k0 + kc, s * 128:(s + 1) * 128], wv[:, kc, half * 512:(half + 1) * 512], [ft, wbr[bi_]], ps, kc == 0,
                                   start=(kc == 0), stop=(kc == 3))
                            pss.append(ps)
                        ta = tmp.next(); tb = tmp.next()
                        hs = slice(half * 512, (half + 1) * 512)
                        P.op("dve", lambda e, ps=pss[0], ta=ta, mg=mg, half=half: e.tensor_tensor(out=ta[:], in0=ps[:], in1=mg[:, half * 512:(half + 1) * 512], op=ALU.mult),
                             reads=[pss[0], mg], writes=[ta])
                        P.op("dve", lambda e, ps=pss[1], tb=tb, mg=mg, half=half: e.tensor_tensor(out=tb[:], in0=ps[:], in1=mg[:, 1024 + half * 512:1024 + (half + 1) * 512], op=ALU.mult),
                             reads=[pss[1], mg], writes=[tb])
                        P.op("pool", lambda e, ta=ta, tb=tb: e.tensor_tensor(out=ta[:], in0=ta[:], in1=tb[:], op=ALU.add), reads=[ta, tb], writes=[ta])
                        tc_ = tmp.next()
                        P.op("dve", lambda e, ps=pss[2], tc_=tc_, mg=mg, half=half: e.tensor_tensor(out=tc_[:], in0=ps[:], in1=mg[:, 2048 + half * 512:2048 + (half + 1) * 512], op=ALU.mult),
                             reads=[pss[2], mg], writes=[tc_])
                        P.op("pool", lambda e, ta=ta, tc_=tc_, mx=mx, hs=hs: e.tensor_tensor(out=mx[:, hs], in0=ta[:], in1=tc_[:], op=ALU.add),
                             reads=[ta, tc_], writes=[mx] if half == 0 else (), pwrites=() if half == 0 else [mx])
                    tr16([(mx, mx[:, k * 128:(k + 1) * 128]) for k in range(8)], FT[0], 0, s)
                wm = [wload(wb["wmix"][l], h_ * 512, 512) for h_ in range(2)]
                proj_res(FT[0], wm, xts)
                layer_norm_all(xts, 0)
                for s in range(NSUB):
                    tr32(xts[s], FT[1], s)
                for pc in range(2):
                    wp = wload(wb["wmq"][l], pc * 512, 512)
                    for c4 in range(4):
                        c = pc * 4 + c4
                        ps = pr.next()
                        for kc in range(KC):
                            mm(ps[:, 0:BLK], wp[:, kc, c4 * 128:(c4 + 1) * 128], FT[1][:, kc, 0:BLK], [wp, FT[1]], ps, kc == 0,
                               start=(kc == 0), stop=(kc == KC - 1))
                        P.op("act", lambda e, ps=ps, c=c: e.activation(out=FT[2][:, c, 0:BLK], in_=ps[:, 0:BLK], func=AF.Copy, scale=1.0 / 16.0),
                             reads=[ps], pwrites=[FT[2]])
                for s in range(NSUB):
                    ss_ = slice(s * 128, (s + 1) * 128)
                    sb2 = [pr.next(), pr.next()]
                    for h in range(4):
                        ps = sb2[h // 2]
                        for dc in range(2):
                            mm(ps[:, (h % 2) * 256:(h % 2) * 256 + 256], FT[2][:, 2 * h + dc, ss_], memKT[:, 2 * h + dc, :], [FT[2], memKT], ps,
                               h % 2 == 0 and dc == 0, start=(dc == 0), stop=(dc == 1))
                    sm = small.next()
                    for bnk in range(2):
                        P.op("dve", lambda e, bnk=bnk, sm=sm, ps=sb2[bnk]: e.tensor_reduce(out=sm[:, 2 * bnk:2 * bnk + 2], in_=ps[:].rearrange("p (h m) -> p h m", h=2),
                                                                                     axis=AX.X, op=ALU.max),
                             reads=[sb2[bnk]], writes=[sm] if bnk == 0 else (), pwrites=() if bnk == 0 else [sm])
                    P.op("dve", lambda e, sm=sm: e.tensor_scalar(out=sm[:, 4:8], in0=sm[:, 0:4], scalar1=-1.0, scalar2=None, op0=ALU.mult), reads=[sm], writes=[sm])
                    pb = pbt.next()
                    for h in range(4):
                        ps = sb2[h // 2]
                        P.op("act", lambda e, h=h, ps=ps, pb=pb, sm=sm: e.activation(out=pb[:, h, :], in_=ps[:, (h % 2) * 256:(h % 2) * 256 + 256], func=AF.Exp,
                                                                                     bias=sm[:, 4 + h:5 + h], accum_out=sm[:, 8 + h:9 + h]),
                             reads=[ps, sm], pwrites=[pb, sm])
                    P.op("dve", lambda e, sm=sm: e.reciprocal(out=sm[:, 12:16], in_=sm[:, 8:12]), reads=[sm], writes=[sm])
                    P.op("dve", lambda e, pb=pb, sm=sm: e.tensor_tensor(out=pb[:], in0=pb[:], in1=sm[:, 12:16].unsqueeze(2).to_broadcast([128, 4, MEM]), op=ALU.mult),
                         reads=[pb, sm], writes=[pb])
                    pT = pTt.next()
                    bi = ptr.next()
                    for i in range(8):
                        h, mt = i // 2, i % 2
                        P.op("pe", lambda e, i=i, h=h, mt=mt, bi=bi, pb=pb: e.transpose(out=psb(bi, i * 128, (i + 1) * 128), in_=pb[:, h, mt * 128:(mt + 1) * 128], identity=IDB[:]),
                             reads=[pb, IDB], writes=[PS[bi]] if i == 0 else (), pwrites=() if i == 0 else [PS[bi]])
                    P.op("dve", lambda e, bi=bi, pT=pT: e.tensor_copy(out=pT[:].rearrange("p a b -> p (a b)"), in_=psb(bi, 0, 1024)), reads=[PS[bi]], writes=[pT])
                    ob2 = [pr.next(), pr.next()]
                    for c in range(8):
                        ps = ob2[c // 4]
                        hd = c // 2
                        for mt in range(2):
                            mm(ps[:, (c % 4) * 128:(c % 4) * 128 + 128], memV[:, mt, c * 128:(c + 1) * 128], pT[:, hd * 2 + mt, :], [memV, pT], ps,
                               c % 4 == 0 and mt == 0, start=(mt == 0), stop=(mt == 1))
                    for bnk in range(2):
                        P.op("act", lambda e, bnk=bnk, ps=ob2[bnk], ss_=ss_: e.activation(out=FT[0][:, bnk * 4:bnk * 4 + 4, ss_], in_=ps[:].rearrange("p (a b) -> p a b", a=4), func=AF.Copy),
                             reads=[ob2[bnk]], pwrites=[FT[0]])
                wo = [wload(wb["wmo"][l], h_ * 512, 512) for h_ in range(2)]
                proj_res(FT[0], wo, xts)
                layer_norm_all(xts, 1)
                for s in range(NSUB):
                    tr32(xts[s], FT[1], s)
                for jp in range(11):
                    wp = wpan.next()
                    ld(wp, wp[:, :, 0:256], wb["wfi"][l][:, jp * 256:(jp + 1) * 256].rearrange("(kc p) c -> p kc c", p=128))
                    ld(wp, wp[:, :, 256:512], wb["wfi"][l][:, FFN_H + jp * 256:FFN_H + (jp + 1) * 256].rearrange("(kc p) c -> p kc c", p=128), partial=True)
                    for jj in range(2):
                        j = 2 * jp + jj
                        pg = pr.next(); pu = pr.next()
                        for kc in range(KC):
                            mm(pg[:, 0:BLK], wp[:, kc, jj * 128:(jj + 1) * 128], FT[1][:, kc, 0:BLK], [wp, FT[1]], pg, kc == 0, start=(kc == 0), stop=(kc == KC - 1))
                        for kc in range(KC):
                            mm(pu[:, 0:BLK], wp[:, kc, 256 + jj * 128:256 + (jj + 1) * 128], FT[1][:, kc, 0:BLK], [wp, FT[1]], pu, kc == 0, start=(kc == 0), stop=(kc == KC - 1))
                        sg = tmp.next()
                        P.op("act", lambda e, pg=pg, sg=sg: e.activation(out=sg[:, 0:BLK], in_=pg[:, 0:BLK], func=AF.Silu), reads=[pg], writes=[sg])
                        P.op("dve", lambda e, pu=pu, sg=sg, j=j: e.tensor_tensor(out=HID[:, j, 0:BLK], in0=sg[:, 0:BLK], in1=pu[:, 0:BLK], op=ALU.mult),
                             reads=[pu, sg], pwrites=[HID])
                for half in range(2):
                    accs = [PS[half * 4 + s] for s in range(NSUB)]
                    for jg in range(2):
                        wf = wfo.next()
                        ld(wf, wf[:], wb["wfo"][l][jg * 1408:(jg + 1) * 1408, half * 512:(half + 1) * 512].rearrange("(j p) c -> p j c", p=128))
                        for s in range(NSUB):
                            for j11 in range(11):
                                j = jg * 11 + j11
                                mm(accs[s][:], HID[:, j, s * 128:(s + 1) * 128], wf[:, j11, :], [HID, wf], accs[s], j == 0, start=(j == 0), stop=(j == FKC - 1))
                    for s in range(NSUB):
                        xt = xts[s]
                        P.op("dve", lambda e, ps=accs[s], xt=xt, half=half: e.scalar_tensor_tensor(
                            out=xt[:, half * 512:(half + 1) * 512], in0=xt[:, half * 512:(half + 1) * 512], scalar=ALPHA, in1=ps[:],
                            op0=ALU.mult, op1=ALU.add), reads=[xt, accs[s]], writes=[xt])
                layer_norm_all(xts, 2)
                for s in range(NSUB):
                    st(rows(xnext, t0 + s * 128, 128), xts[s], xts[s][:])

            if NB > 1:
                loop(0, NB, p3_body, [ix3])
            else:
                p3_body(0)
            end_phase(T)

        seqs = []
        if "p" in seqs_enabled:
            seqs.append(Seq("p", SP, x_prompt, mem_prompt, y_prompt))
        if "s" in seqs_enabled:
            seqs.append(Seq("s", SS, x_sample, mem_sample, y_sample))
        for seq in seqs:
            xcur = seq.x_in
            lls = list(range(depth)) if layer_list is None else layer_list
            for l in lls:
                xnext = seq.y_out if l == lls[-1] else xs_d[l % 2]
                phase_p1(seq, l, xcur)
                if stop_after == "p1":
                    break
                phase_p2a(seq, l)
                if stop_after == "p2a":
                    break
                phase_p2(seq, l)
                if stop_after == "p2":
                    break
                phase_p3(seq, l, xcur, xnext)
                if stop_after == ("after", seq.name, l):
                    break
                xcur = xnext
        GT.close()
    return nc, dbg_outputs


def host_constants(SMAX):
    c = np.zeros((128, NCONST, 128), np.float32)
    p = np.arange(128)[:, None]
    f = np.arange(128)[None, :]
    same = (p // 64) == (f // 64)
    c[:, C_ID] = (p == f)
    c[:, C_UI] = same & (f >= p)
    c[:, C_LI] = same & (f <= p)
    c[:, C_SL] = same & (f < p)
    c[:, C_SU] = same & (f > p)
    c[:, C_ONE] = 1.0
    c[:, C_SEL0] = (p < 64)
    c[:, C_SEL1] = (p >= 64)
    inv = (10000.0 ** (-np.arange(0, 64, 2, dtype=np.float32) / 64)).astype(np.float32)
    ang = (np.arange(SMAX, dtype=np.float32)[:, None] * inv[None, :]).astype(np.float32)
    cs, sn = np.cos(ang).astype(np.float32), np.sin(ang).astype(np.float32)
    rope = np.concatenate([cs, cs, -sn, sn], axis=1).astype(np.float32)
    q = np.arange(128)[:, None] + 128
    k = np.arange(384)[None, :]
    band = np.abs(q - k) <= 128
    am = np.zeros((128, 4, 384), np.float32)
    for v in range(4):
        ok = band.copy()
        if v in (1, 3):
            ok &= (k >= 128)
        if v in (2, 3):
            ok &= (k < 256)
        am[:, v] = np.where(ok, 0.0, -30000.0)
    sh = np.zeros((128, 9, 128), np.float32)
    for k in range(5):
        sh[:, k] = (p == f + (k - 2))
    for ii, k in enumerate((0, 1, 3, 4)):
        srow = (-2, -1) if k < 2 else (128, 129)
        for hp in range(2):
            sh[hp, 5 + ii] = (srow[hp] == np.arange(128) + (k - 2))
    return c.reshape(128, NCONST * 128), rope, am.reshape(128, 4 * 384), sh.reshape(128, 9 * 128)


_CACHE = {}


def kernel(**inputs):
    SP, SS = inputs["x_prompt"].shape[1], inputs["x_sample"].shape[1]
    NCORE = inputs["x_sample"].shape[0]
    key = (SP, SS)
    if key not in _CACHE:
        _CACHE[key] = build(SP, SS)[0]
    nc = _CACHE[key]
    cm, rope, am, sh = host_constants(max(SP, SS))
    f32 = lambda a: np.ascontiguousarray(np.asarray(a, dtype=np.float32))
    shared = {
        "x_prompt": f32(inputs["x_prompt"]).reshape(SP, D),
        "mem_prompt": f32(inputs["mem_prompt"]).reshape(MEM, D),
        "w_in": f32(inputs["w_in"]),
        "hgrn_lb_logits": f32(inputs["hgrn_lb_logits"]).reshape(1, -1),
        "hgrn_norm_g": f32(inputs["hgrn_norm_g"]),
        "attn_sink": f32(inputs["attn_sink"]),
        "gdn_conv_w": f32(inputs["gdn_conv_w"]).reshape(DEPTH, -1),
        "gdn_a_log": f32(inputs["gdn_a_log"]).reshape(DEPTH, 8),
        "gdn_dt_bias": f32(inputs["gdn_dt_bias"]).reshape(DEPTH, 8),
        "gdn_norm_g": f32(inputs["gdn_norm_g"]),
        "w_branch_a": f32(inputs["w_branch_a"]),
        "w_branch_b": f32(inputs["w_branch_b"]),
        "w_branch_c": f32(inputs["w_branch_c"]),
        "w_mix_out": f32(inputs["w_mix_out"]),
        "w_mem_q": f32(inputs["w_mem_q"]),
        "w_mem_kv": f32(inputs["w_mem_kv"]),
        "w_mem_o": f32(inputs["w_mem_o"]),
        "w_ffn_in": f32(inputs["w_ffn_in"]),
        "w_ffn_out": f32(inputs["w_ffn_out"]),
        "ln_g": f32(inputs["ln_g"]).reshape(DEPTH, -1),
        "ln_b": f32(inputs["ln_b"]).reshape(DEPTH, -1),
        "consts": cm, "rope": rope, "amask": am, "cshift": sh, "zrow": np.zeros((2, 1536), np.float32), "iota": np.arange(128, dtype=np.float32).reshape(128, 1),
    }
    xs = f32(inputs["x_sample"])
    ms = f32(inputs["mem_sample"])
    in_maps = []
    for c in range(NCORE):
        m = dict(shared)
        m["x_sample"] = xs[c]
        m["mem_sample"] = ms[c]
        in_maps.append(m)
    res = run_bass_kernel_spmd(nc, in_maps, core_ids=list(range(NCORE)))
    yp = np.asarray(res.results[0]["y_prompt"], dtype=np.float32).reshape(1, SP, D)
    ysm = np.stack([np.asarray(res.results[c]["y_sample"], dtype=np.float32) for c in range(NCORE)], 0)
    return (yp, ysm)
```
